# Optimizing a Trainium2 kernel written in Bass

```python
import jax, jax.numpy as jnp
from jax import lax
import numpy as np

D_MODEL = 1024
BATCH = 8
SEQ = 2048
DEPTH = 4
DEC_BATCH = 128
DEC_SEQ = 8
PAST_LEN = 16384
PAGE_SIZE = 128

N_MIXERS = 2
N_POOL_LAYERS = (DEPTH + 1) // 2
N_SSD_LAYERS = DEPTH // 2
EXPAND = 2
D_INNER = EXPAND * D_MODEL
POOL_WINDOWS = (2, 4, 8, 16)
N_POOL_GROUPS = len(POOL_WINDOWS)
POOL_GROUP = D_INNER // N_POOL_GROUPS
POOL_BUF = max(POOL_WINDOWS) - 1
HEAD_DIM = 64
N_HEADS = D_INNER // HEAD_DIM
D_STATE = 128
N_GROUPS = 4
HEADS_PER_GROUP = N_HEADS // N_GROUPS
CONV_K = 4
CONV_DIM = D_INNER + 2 * N_GROUPS * D_STATE
SSD_IN = D_INNER + CONV_DIM + N_HEADS
CHUNK = 128
EPS = 1e-6

kernel_name = 'pool_ssd_hybrid_step'


def rmsnorm(x, w):
    xf = x.astype(jnp.float32)
    y = xf * lax.rsqrt(jnp.mean(xf * xf, axis=-1, keepdims=True) + EPS)
    return (y * w.astype(jnp.float32)).astype(x.dtype)


def pool_mixer(h, buf, start, w_in, w_mix, scale, w_out):
    b, l, _ = h.shape
    uz = h @ w_in
    u, z = uz[..., :D_INNER], uz[..., D_INNER:]
    u_ext = jnp.concatenate([buf.astype(u.dtype), u], axis=1)
    csum = jnp.cumsum(u_ext.astype(jnp.float32), axis=1)
    csum = jnp.concatenate([jnp.zeros((b, 1, D_INNER), jnp.float32), csum], axis=1)
    hi = csum[:, POOL_BUF + 1:]
    pos = start + jnp.arange(l)
    uf = u.astype(jnp.float32)
    outs = []
    for g, w in enumerate(POOL_WINDOWS):
        sl = slice(g * POOL_GROUP, (g + 1) * POOL_GROUP)
        lo = csum[:, POOL_BUF + 1 - w:POOL_BUF + 1 - w + l, sl]
        cnt = jnp.minimum(pos + 1, w).astype(jnp.float32)[None, :, None]
        outs.append((hi[..., sl] - lo) / cnt - uf[..., sl])
    p = jnp.stack(outs, axis=2)
    mixed = jnp.einsum('blgc,gcd->blgd', p, w_mix.astype(jnp.float32)).reshape(b, l, D_INNER)
    y = mixed * scale.astype(jnp.float32) * jax.nn.silu(z.astype(jnp.float32))
    return y.astype(h.dtype) @ w_out, u_ext[:, -POOL_BUF:]


def causal_conv(xbc, buf, w, bias):
    l = xbc.shape[1]
    ext = jnp.concatenate([buf.astype(xbc.dtype), xbc], axis=1)
    out = ext[:, 0:l] * w[0]
    for k in range(1, CONV_K):
        out = out + ext[:, k:k + l] * w[k]
    return jax.nn.silu(out + bias), ext[:, -(CONV_K - 1):]


def ssd_scan(x, dt, A, B, C, h0):
    b, l = x.shape[:2]
    q = min(CHUNK, l)
    nc = l // q
    x = x.reshape(b, nc, q, N_GROUPS, HEADS_PER_GROUP, HEAD_DIM)
    dt = dt.reshape(b, nc, q, N_GROUPS, HEADS_PER_GROUP)
    B = B.reshape(b, nc, q, N_GROUPS, D_STATE)
    C = C.reshape(b, nc, q, N_GROUPS, D_STATE)
    a = dt * A.reshape(N_GROUPS, HEADS_PER_GROUP)
    a_cum_t = jnp.moveaxis(jnp.cumsum(a, axis=2), 2, -1)
    dt_t = jnp.moveaxis(dt, 2, -1)
    causal = jnp.tril(jnp.ones((q, q), bool))
    seg = a_cum_t[..., :, None] - a_cum_t[..., None, :]
    L = jnp.exp(jnp.where(causal, seg, -jnp.inf))
    CB = jnp.einsum('bcign,bcjgn->bcgij', C, B)
    W = CB[:, :, :, None] * L * dt_t[..., None, :]
    y_diag = jnp.einsum('bcgeij,bcjgep->bcigep', W, x)
    decay_to_end = jnp.exp(a_cum_t[..., -1:] - a_cum_t) * dt_t
    chunk_states = jnp.einsum('bcjgn,bcgej,bcjgep->bcgepn', B, decay_to_end, x)
    chunk_decay = jnp.exp(a_cum_t[..., -1])

    def step(hs, inp):
        s, d = inp
        return d[..., None, None] * hs + s, hs

    h0g = h0.reshape(b, N_GROUPS, HEADS_PER_GROUP, HEAD_DIM, D_STATE)
    h_final, h_prev = lax.scan(step, h0g, (jnp.moveaxis(chunk_states, 1, 0), jnp.moveaxis(chunk_decay, 1, 0)))
    h_prev = jnp.moveaxis(h_prev, 0, 1)
    y_off = jnp.einsum('bcign,bcgepn,bcgei->bcigep', C, h_prev, jnp.exp(a_cum_t))
    y = (y_diag + y_off).reshape(b, l, N_HEADS, HEAD_DIM)
    return y, h_final.reshape(b, N_HEADS, HEAD_DIM, D_STATE)


def ssd_mixer(h, conv_buf, ssm_state, w_in, conv_w, conv_b, dt_bias, A_log, D_skip, norm_w, w_out):
    b, l, _ = h.shape
    proj = h @ w_in
    z = proj[..., :D_INNER]
    xbc = proj[..., D_INNER:D_INNER + CONV_DIM]
    dt_raw = proj[..., D_INNER + CONV_DIM:]
    xbc, new_conv = causal_conv(xbc, conv_buf, conv_w, conv_b)
    xbc = xbc.astype(jnp.float32)
    xs = xbc[..., :D_INNER].reshape(b, l, N_HEADS, HEAD_DIM)
    Bm = xbc[..., D_INNER:D_INNER + N_GROUPS * D_STATE].reshape(b, l, N_GROUPS, D_STATE)
    Cm = xbc[..., D_INNER + N_GROUPS * D_STATE:].reshape(b, l, N_GROUPS, D_STATE)
    dt = jax.nn.softplus(dt_raw.astype(jnp.float32) + dt_bias.astype(jnp.float32))
    A = -jnp.exp(A_log.astype(jnp.float32))
    y, new_ssm = ssd_scan(xs, dt, A, Bm, Cm, ssm_state.astype(jnp.float32))
    y = y + D_skip.astype(jnp.float32)[:, None] * xs
    y = y.reshape(b, l, D_INNER) * jax.nn.silu(z.astype(jnp.float32))
    y = rmsnorm(y, norm_w)
    return y.astype(h.dtype) @ w_out, new_conv, new_ssm.astype(ssm_state.dtype)


def trunk(x, pool_buf, conv_buf, ssm_buf, start, norm_w, pool_in_w, pool_mix_w, pool_scale, pool_out_w,
          ssd_in_w, ssd_conv_w, ssd_conv_b, ssd_dt_bias, ssd_A_log, ssd_D, ssd_norm_w, ssd_out_w, final_norm_w):
    new_pool, new_conv, new_ssm = [], [], []
    for i in range(DEPTH):
        h = rmsnorm(x, norm_w[i])
        j = i // N_MIXERS
        if i % N_MIXERS == 0:
            out, nb = pool_mixer(h, pool_buf[j], start, pool_in_w[j], pool_mix_w[j], pool_scale[j], pool_out_w[j])
            new_pool.append(nb)
        else:
            out, nc, ns = ssd_mixer(h, conv_buf[j], ssm_buf[j], ssd_in_w[j], ssd_conv_w[j], ssd_conv_b[j],
                                    ssd_dt_bias[j], ssd_A_log[j], ssd_D[j], ssd_norm_w[j], ssd_out_w[j])
            new_conv.append(nc)
            new_ssm.append(ns)
        x = x + out
    return rmsnorm(x, final_norm_w), jnp.stack(new_pool), jnp.stack(new_conv), jnp.stack(new_ssm)


def setup_inputs(seed: int = 0) -> dict:
    key = jax.random.key(seed)
    ks = jax.random.split(key, 20)
    f32 = jnp.float32
    nrm = lambda k, s, sc: jax.random.normal(k, s, f32) * sc
    dt0 = jnp.exp(jax.random.uniform(ks[14], (N_SSD_LAYERS, N_HEADS), f32, np.log(1e-3), np.log(1e-1)))
    return {
        'x_prompt': nrm(ks[0], (BATCH, SEQ, D_MODEL), 1.0),
        'x_sample': nrm(ks[1], (DEC_BATCH, DEC_SEQ, D_MODEL), 1.0),
        'state_pool': nrm(ks[2], (N_POOL_LAYERS, DEC_BATCH, POOL_BUF, D_INNER), 1.0),
        'state_conv': nrm(ks[3], (N_SSD_LAYERS, DEC_BATCH, CONV_K - 1, CONV_DIM), 1.0),
        'state_ssm': nrm(ks[4], (N_SSD_LAYERS, DEC_BATCH, N_HEADS, HEAD_DIM, D_STATE), 0.1),
        'norm_w': 1.0 + nrm(ks[5], (DEPTH, D_MODEL), 0.02),
        'pool_in_w': nrm(ks[6], (N_POOL_LAYERS, D_MODEL, 2 * D_INNER), D_MODEL ** -0.5),
        'pool_mix_w': nrm(ks[7], (N_POOL_LAYERS, N_POOL_GROUPS, POOL_GROUP, POOL_GROUP), POOL_GROUP ** -0.5),
        'pool_scale': 1.0 + nrm(ks[8], (N_POOL_LAYERS, D_INNER), 0.1),
        'pool_out_w': nrm(ks[9], (N_POOL_LAYERS, D_INNER, D_MODEL), D_INNER ** -0.5),
        'ssd_in_w': nrm(ks[10], (N_SSD_LAYERS, D_MODEL, SSD_IN), D_MODEL ** -0.5),
        'ssd_conv_w': nrm(ks[11], (N_SSD_LAYERS, CONV_K, CONV_DIM), CONV_K ** -0.5),
        'ssd_conv_b': nrm(ks[12], (N_SSD_LAYERS, CONV_DIM), 0.01),
        'ssd_dt_bias': dt0 + jnp.log(-jnp.expm1(-dt0)),
        'ssd_A_log': jnp.log(jax.random.uniform(ks[13], (N_SSD_LAYERS, N_HEADS), f32, 1.0, 16.0)),
        'ssd_D': 1.0 + nrm(ks[15], (N_SSD_LAYERS, N_HEADS), 0.1),
        'ssd_norm_w': 1.0 + nrm(ks[16], (N_SSD_LAYERS, D_INNER), 0.02),
        'ssd_out_w': nrm(ks[17], (N_SSD_LAYERS, D_INNER, D_MODEL), D_INNER ** -0.5),
        'final_norm_w': 1.0 + nrm(ks[18], (D_MODEL,), 0.02),
    }


def reference(x_prompt, x_sample, state_pool, state_conv, state_ssm, norm_w, pool_in_w, pool_mix_w, pool_scale,
              pool_out_w, ssd_in_w, ssd_conv_w, ssd_conv_b, ssd_dt_bias, ssd_A_log, ssd_D, ssd_norm_w, ssd_out_w,
              final_norm_w):
    b = x_prompt.shape[0]
    dt = x_prompt.dtype
    zero_pool = jnp.zeros((N_POOL_LAYERS, b, POOL_BUF, D_INNER), dt)
    zero_conv = jnp.zeros((N_SSD_LAYERS, b, CONV_K - 1, CONV_DIM), dt)
    zero_ssm = jnp.zeros((N_SSD_LAYERS, b, N_HEADS, HEAD_DIM, D_STATE), state_ssm.dtype)
    y_prompt, pool_p, conv_p, ssm_p = trunk(
        x_prompt, zero_pool, zero_conv, zero_ssm, 0, norm_w, pool_in_w, pool_mix_w, pool_scale, pool_out_w,
        ssd_in_w, ssd_conv_w, ssd_conv_b, ssd_dt_bias, ssd_A_log, ssd_D, ssd_norm_w, ssd_out_w, final_norm_w)
    y_sample, pool_s, conv_s, ssm_s = trunk(
        x_sample, state_pool, state_conv, state_ssm, PAST_LEN, norm_w, pool_in_w, pool_mix_w, pool_scale, pool_out_w,
        ssd_in_w, ssd_conv_w, ssd_conv_b, ssd_dt_bias, ssd_A_log, ssd_D, ssd_norm_w, ssd_out_w, final_norm_w)
    return (y_prompt, y_sample, pool_p, pool_s, conv_p, conv_s, ssm_p, ssm_s)
```

```python
import numpy as np
import concourse.bass as bass
import concourse.mybir as mybir
from concourse.bass_utils import run_bass_kernel_spmd
from contextlib import ExitStack

F32 = mybir.dt.float32
BF16 = mybir.dt.bfloat16
AF = mybir.ActivationFunctionType
ALU = mybir.AluOpType

NCORES = 8
PLAN = {}


class Stop(Exception):
    pass


def stage(n, cond=True):
    if cond and PLAN.get('stop') == n:
        raise Stop()
DM = 1024
DI = 2048
SEQ = 2048
NS = 16
DS = 8
CONV_DIM = 3072
SSD_IN = 5152
EPS = 1e-6
POOL_W = (2, 4, 8, 16)


class Trk:
    def __init__(self, nc, es):
        self.nc = nc
        self.E = {'pe': nc.tensor, 'act': nc.scalar, 'dve': nc.vector, 'pool': nc.gpsimd, 'sp': nc.sync}
        self.sem = {e: es.enter_context(nc.semaphore('c_' + e)) for e in ('pe', 'act', 'dve', 'pool')}
        self.cnt = {e: 0 for e in self.sem}
        self.dq = {'sp': [es.enter_context(nc.semaphore('dsp%d' % i)) for i in range(14)],
                   'pool': [es.enter_context(nc.semaphore('dpl%d' % i)) for i in range(6)]}
        self.dcnt = {q: [0] * len(v) for q, v in self.dq.items()}
        self.drr = {q: 0 for q in self.dq}
        self.seen = {e: {} for e in self.E}
        self.bufs = {}

    def _semobj(self, k):
        return self.sem[k] if isinstance(k, str) else self.dq[k[0]][k[1]]

    def _collect(self, r, w):
        need = {}

        def add(ev):
            for k, v in ev.items():
                if need.get(k, 0) < v:
                    need[k] = v
        for key in r:
            b = self.bufs.get(key)
            if b:
                add(b['w'])
        for key in w:
            b = self.bufs.get(key)
            if b:
                add(b['w'])
                add(b['r'])
        return need

    def _emit_waits(self, e, need):
        for k, v in need.items():
            if k == 'pe' and e == 'pe':
                continue
            if k == e and PLAN.get('noself'):
                continue
            if self.seen[e].get(k, 0) >= v:
                continue
            self.E[e].wait_ge(self._semobj(k), v)
            self.seen[e][k] = v

    def _update(self, r, w, k, v):
        for key in r:
            b = self.bufs.setdefault(key, {'w': {}, 'r': {}})
            if b['r'].get(k, 0) < v:
                b['r'][k] = v
        for key in w:
            self.bufs[key] = {'w': {k: v}, 'r': {}}

    def op(self, e, fn, r=(), w=()):
        self._emit_waits(e, self._collect(r, w))
        ins = fn(self.E[e])
        self.cnt[e] += 1
        ins.then_inc(self.sem[e], 1)
        self._update(r, w, e, self.cnt[e])

    def dma(self, q, out, in_, r=(), w=(), **kw):
        i = self.drr[q]
        self.drr[q] = (i + 1) % len(self.dq[q])
        need = self._collect(r, w)
        k = (q, i)
        prev = 16 * self.dcnt[q][i]
        if prev and need.get(k, 0) < prev:
            need[k] = prev
        self._emit_waits(q, need)
        ins = self.E[q].dma_start(out=out, in_=in_, **kw)
        self.dcnt[q][i] += 1
        ins.then_inc(self.dq[q][i], 16)
        self._update(r, w, k, 16 * self.dcnt[q][i])

    def _all(self):
        need = {e: c for e, c in self.cnt.items() if c}
        for q, lst in self.dcnt.items():
            for i, c in enumerate(lst):
                if c:
                    need[(q, i)] = 16 * c
        return need

    def barrier(self):
        need = self._all()
        for e in ('pe', 'act', 'dve', 'pool', 'sp'):
            n2 = dict(need)
            self._emit_waits(e, n2)
        self.bufs = {}

    def finish(self):
        self._emit_waits('sp', self._all())


class Arena:
    def __init__(self, ap_f32, words):
        self.ap = ap_f32
        self.words = words
        self.off = 0

    def reset(self):
        self.off = 0

    def f32(self, words, parts=128):
        a = self.ap[0:parts, self.off:self.off + words]
        self.off += words
        assert self.off <= self.words, (self.off, self.words)
        return a

    def bf16(self, elems, parts=128):
        assert elems % 2 == 0
        return self.f32(elems // 2, parts).bitcast(BF16)


def build_program():
    nc = bass.Bass("TRN2", target_bir_lowering=False)
    dt_in = lambda n, s: nc.dram_tensor(n, s, F32, kind="ExternalInput").ap()
    dt_out = lambda n, s: nc.dram_tensor(n, s, F32, kind="ExternalOutput").ap()
    xp = dt_in("xp", [SEQ, DM])
    xsm = dt_in("xsm", [128, DM])
    spool = dt_in("spool", [2, NS, 15, DI])
    sconv = dt_in("sconv", [2, NS, 3, CONV_DIM])
    sssm = dt_in("sssm", [2, NS, 32, 64, 128])
    pool_in_w = dt_in("pool_in_w", [2, DM, 2 * DI])
    pool_mix_w = dt_in("pool_mix_w", [2, 4, 512, 512])
    pool_out_w = dt_in("pool_out_w", [2, DI, DM])
    ssd_in_w = dt_in("ssd_in_w", [2, DM, SSD_IN])
    ssd_out_w = dt_in("ssd_out_w", [2, DI, DM])
    nwc_d = dt_in("nwc", [128, 4 * 8])
    fnw_d = dt_in("fnw", [1, DM])
    pscale_d = dt_in("pscale", [128, 2 * 16])
    convw_d = dt_in("convw", [128, 2 * 24 * 4])
    convb_d = dt_in("convb", [128, 2 * 24])
    dtb_d = dt_in("dtb", [1, 64])
    alog_d = dt_in("alog", [1, 64])
    dsk_d = dt_in("dsk", [1, 64])
    snw_d = dt_in("snw", [128, 2 * 16])

    yp = dt_out("yp", [SEQ, DM])
    ysm = dt_out("ysm", [128, DM])
    poolp = dt_out("poolp", [2, 15, DI])
    pools = dt_out("pools", [2, NS, 15, DI])
    convp = dt_out("convp", [2, 3, CONV_DIM])
    convs = dt_out("convs", [2, NS, 3, CONV_DIM])
    ssmp = dt_out("ssmp", [2, 32, 64, 128])
    ssms = dt_out("ssms", [2, NS, 32, 64, 128])
    xscr = nc.dram_tensor("xscr", [SEQ + 128, DM], F32).ap()

    with ExitStack() as es:
        sb = lambda n, s, d: es.enter_context(nc.sbuf_tensor(n, s, d))
        T = Trk(nc, es)
        WA = sb("WA", [128, 8 * SSD_IN], BF16)
        WB = sb("WB", [128, 16 * DM], BF16)
        ident_f = sb("ident_f", [128, 128], F32)
        ident_b = sb("ident_b", [128, 128], BF16)
        Umat = sb("Umat", [128, 128], F32)
        SLmat = sb("SLmat", [128, 128], F32)
        ones_f = sb("ones_f", [128, 128], F32)
        Us = sb("Us", [128, 128], F32)
        SLs = sb("SLs", [128, 128], F32)
        seqind = sb("seqind", [128, 16], F32)
        icnt = sb("icnt", [128, 4 * 16], F32)
        nwc = sb("nwc_s", [128, 32], F32)
        fnw = sb("fnw_s", [128, DM], F32)
        pscale = sb("pscale_s", [128, 32], F32)
        convw = sb("convw_s", [128, 192], F32)
        convb = sb("convb_s", [128, 48], F32)
        dtb = sb("dtb_s", [128, 64], F32)
        negA = sb("negA_s", [128, 64], F32)
        dsk = sb("dsk_s", [128, 64], F32)
        snw = sb("snw_s", [128, 32], F32)
        AW = 21600
        arena_t = sb("arena", [128, AW], F32)
        A = Arena(arena_t, AW)
        PS = es.enter_context(nc.psum_tensor("PS", [128, 4096], F32))

        st = {'b': 0, 'g': 0}

        def bank():
            nb = st.get('nb', 4)
            b = st['b'] % nb
            st['b'] = (b + 1) % nb
            return PS[:, b * 512:(b + 1) * 512], ('ps', b)

        def bgroup(n=4):
            if n == 4:
                return PS[:, 2048:4096], [('ps', 4 + i) for i in range(4)]
            g = st['g']
            st['g'] = (g + 1) % 2
            return PS[:, 2048 + g * 1024:2048 + (g + 1) * 1024], [('ps', 4 + 2 * g + i) for i in range(2)]

        def hbank(i):
            b = 4 + (i % 2)
            return PS[:, b * 512:(b + 1) * 512], ('ps', b)

        T.op('pool', lambda e: e.memset(ident_f[:], 0.0), w=['ident_f'])
        T.op('pool', lambda e: e.affine_select(out=ident_f[:], in_=ident_f[:], pattern=[[-1, 128]],
                                               compare_op=ALU.not_equal, fill=1.0, base=0, channel_multiplier=1),
             r=['ident_f'], w=['ident_f'])
        T.op('dve', lambda e: e.tensor_copy(out=ident_b[:], in_=ident_f[:]), r=['ident_f'], w=['ident_b'])
        T.op('pool', lambda e: e.memset(ones_f[:], 1.0), w=['ones_f'])
        T.op('pool', lambda e: e.memset(Umat[:], 1.0), w=['U'])
        T.op('pool', lambda e: e.affine_select(out=Umat[:], in_=Umat[:], pattern=[[1, 128]], compare_op=ALU.is_ge,
                                               fill=0.0, base=0, channel_multiplier=-1), r=['U'], w=['U'])
        T.op('pool', lambda e: e.memset(SLmat[:], 1.0), w=['SL'])
        T.op('pool', lambda e: e.affine_select(out=SLmat[:], in_=SLmat[:], pattern=[[-1, 128]], compare_op=ALU.is_gt,
                                               fill=0.0, base=0, channel_multiplier=1), r=['SL'], w=['SL'])

        def blockdiag(dst, key, pat_tail):
            T.op('pool', lambda e: e.affine_select(out=dst, in_=dst, pattern=[[-8, 16]] + pat_tail,
                                                   compare_op=ALU.is_ge, fill=0.0, base=0, channel_multiplier=1),
                 r=[key], w=[key])
            T.op('pool', lambda e: e.affine_select(out=dst, in_=dst, pattern=[[8, 16]] + pat_tail,
                                                   compare_op=ALU.is_ge, fill=0.0, base=7, channel_multiplier=-1),
                 r=[key], w=[key])
        T.op('pool', lambda e: e.tensor_copy(out=Us[:], in_=Umat[:]), r=['U'], w=['Us'])
        blockdiag(Us[:].rearrange("p (s t) -> p s t", t=8), 'Us', [[0, 8]])
        T.op('pool', lambda e: e.tensor_copy(out=SLs[:], in_=SLmat[:]), r=['SL'], w=['SLs'])
        blockdiag(SLs[:].rearrange("p (s t) -> p s t", t=8), 'SLs', [[0, 8]])
        T.op('pool', lambda e: e.memset(seqind[:], 1.0), w=['seqind'])
        blockdiag(seqind[:], 'seqind', [])
        icnt3 = icnt[:].rearrange("p (g t) -> p g t", g=4)
        T.op('pool', lambda e: e.iota(icnt3, pattern=[[0, 4], [1, 16]], base=1, channel_multiplier=0,
                                      allow_small_or_imprecise_dtypes=True), w=['icnt'])
        for g in range(4):
            T.op('dve', lambda e, g=g: e.tensor_scalar(out=icnt3[:, g, :], in0=icnt3[:, g, :], scalar1=float(POOL_W[g]),
                                                       scalar2=None, op0=ALU.min), r=['icnt'], w=['icnt'])
        T.op('dve', lambda e: e.reciprocal(out=icnt[:], in_=icnt[:]), r=['icnt'], w=['icnt'])

        T.dma('sp', nwc[:], nwc_d[:, :], w=['nwc'])
        T.dma('sp', fnw[:], fnw_d.partition_broadcast(128), w=['fnw'])
        T.dma('sp', pscale[:], pscale_d[:, :], w=['pscale'])
        T.dma('sp', convw[:], convw_d[:, :], w=['convw'])
        T.dma('sp', convb[:], convb_d[:, :], w=['convb'])
        T.dma('sp', dtb[:], dtb_d.partition_broadcast(128), w=['dtb'])
        T.dma('sp', negA[:], alog_d.partition_broadcast(128), w=['negA'])
        T.dma('sp', dsk[:], dsk_d.partition_broadcast(128), w=['dsk'])
        T.dma('sp', snw[:], snw_d[:, :], w=['snw'])
        T.op('act', lambda e: e.activation(out=negA[:], in_=negA[:], func=AF.Exp), r=['negA'], w=['negA'])
        T.op('dve', lambda e: e.tensor_scalar(out=negA[:], in0=negA[:], scalar1=-1.0, scalar2=None, op0=ALU.mult),
             r=['negA'], w=['negA'])

        def src_rows(L, row0, n):
            if L == 0:
                return (xp[row0:row0 + n, :], []) if row0 < SEQ else (xsm[row0 - SEQ:row0 - SEQ + n, :], [])
            return xscr[row0:row0 + n, :], [('xscr', row0 // 128)]

        def rms_stat(src_ap, src_key, stat, col, n_feat, junk, junk_key):
            T.op('act', lambda e: e.activation(out=junk, in_=src_ap, func=AF.Square, accum_out=stat[:, col:col + 1]),
                 r=[src_key], w=[junk_key, 'stat'])
            T.op('act', lambda e: e.activation(out=stat[:, col + 1:col + 2], in_=stat[:, col:col + 1], func=AF.Sqrt,
                                               scale=1.0 / n_feat, bias=epsb[:, 0:1]), r=['stat', 'epsb'], w=['stat'])
            T.op('dve', lambda e: e.reciprocal(out=stat[:, col:col + 1], in_=stat[:, col + 1:col + 2]),
                 r=['stat'], w=['stat'])

        WAK = [('WA', k) for k in range(24)]
        WBK = [('WB', k) for k in range(16)]

        def load_weights_bf16(dst3, src3, nk, key, k0=0):
            for k in range(nk):
                T.dma('pool', dst3[:, k, :], src3[:, k, :], w=[(key, k0 + k)])

        epsb = sb("epsb", [128, 1], F32)
        T.op('pool', lambda e: e.memset(epsb[:], EPS), w=['epsb'])

        def norm_and_transpose(xblk, xkey, L, h, hT3, hkey, hTkey, stat, col, tok0):
            rms_stat(xblk, xkey, stat, col, DM, h, hkey)
            T.op('dve', lambda e: e.tensor_scalar(out=h, in0=xblk, scalar1=stat[:, col:col + 1], scalar2=None,
                                                  op0=ALU.mult), r=[xkey, 'stat'], w=[hkey])
            pb, pk = bank()
            pbb = pb.bitcast(BF16)
            for k in range(8):
                T.op('pe', lambda e, k=k: e.transpose(pbb[:, k * 128:(k + 1) * 128], h[:, k * 128:(k + 1) * 128],
                                                      ident_b[:]), r=[hkey, 'ident_b'], w=[pk])
            T.op('dve', lambda e: e.tensor_tensor(
                out=hT3[:, :, tok0:tok0 + 128], in0=pbb.rearrange("p (k t) -> p k t", k=8),
                in1=nwc[:, L * 8:(L + 1) * 8].unsqueeze(2).broadcast_to([128, 8, 128]), op=ALU.mult),
                r=[pk, 'nwc'], w=[hTkey])

        def out_proj_and_store(L, yT3, yTkey, tokoff, xblk, xkey, row0, w_out3, blkidx, junk, junk_key, stat):
            for hh in range(2):
                pb, pk = bank()
                for f in range(16):
                    T.op('pe', lambda e, f=f: e.matmul(pb, lhsT=yT3[:, f, tokoff:tokoff + 128],
                                                       rhs=w_out3[:, f, hh * 512:(hh + 1) * 512],
                                                       start=(f == 0), stop=(f == 15)), r=[yTkey] + WBK, w=[pk])
                T.op('dve', lambda e: e.tensor_tensor(out=xblk[:, hh * 512:(hh + 1) * 512],
                                                      in0=xblk[:, hh * 512:(hh + 1) * 512], in1=pb, op=ALU.add),
                     r=[pk, xkey], w=[xkey])
            if L < 3:
                T.dma('sp', xscr[row0:row0 + 128, :], xblk, r=[xkey], w=[('xscr', row0 // 128)])
            else:
                rms_stat(xblk, xkey, stat, 8, DM, junk, junk_key)
                T.op('dve', lambda e: e.scalar_tensor_tensor(out=xblk, in0=xblk, scalar=stat[:, 8:9], in1=fnw[:],
                                                             op0=ALU.mult, op1=ALU.mult),
                     r=[xkey, 'stat', 'fnw'], w=[xkey])
                dst = yp[row0:row0 + 128, :] if row0 < SEQ else ysm[row0 - SEQ:row0 - SEQ + 128, :]
                T.dma('sp', dst, xblk, r=[xkey])

        def pool_layer(j):
            L = 2 * j
            T.barrier()
            st['nb'] = 4
            A.reset()
            w_in3 = WA[:, 0:8 * 4096].rearrange("p (k n) -> p k n", k=8)
            w_mix3 = WA[:, 8 * 4096:8 * 4096 + 16 * 512].rearrange("p (k n) -> p k n", k=16)
            w_out3 = WB[:].rearrange("p (k n) -> p k n", k=16)
            load_weights_bf16(w_in3, pool_in_w[j].rearrange("(k p) n -> p k n", p=128), 8, 'WA')
            load_weights_bf16(w_mix3, pool_mix_w[j].rearrange("g (c p) n -> p (g c) n", p=128), 16, 'WA', k0=8)
            load_weights_bf16(w_out3, pool_out_w[j].rearrange("(k p) n -> p k n", p=128), 16, 'WB')

            xts = [A.f32(2048).rearrange("p (b n) -> p b n", b=2) for _ in range(2)]
            h = A.bf16(1024)
            hTs = [A.bf16(8 * 256).rearrange("p (k t) -> p k t", k=8) for _ in range(2)]
            ues = [A.f32(1472), A.f32(1472)]
            sa = A.f32(1472)
            sbuf2 = A.f32(1472)
            pgs = [A.bf16(4 * 256), A.bf16(4 * 256)]
            szs = [A.f32(256) for _ in range(4)]
            ybf = A.bf16(16 * 256)
            carry = A.f32(240).rearrange("p (c r) -> p c r", c=16)
            strows = A.f32(1024, parts=120).rearrange("p (a n) -> p a n", a=2)
            tokrows = A.f32(2048)
            stat = A.f32(64)

            T.op('dve', lambda e: e.memset(carry, 0.0), w=['carry'])
            szi = [0]

            class Tile:
                pass

            def mk(ti, row0, Tn, sample, first, last_prompt):
                t = Tile()
                t.ti, t.row0, t.Tn, t.sample, t.first, t.last = ti, row0, Tn, sample, first, last_prompt
                t.nb = Tn // 128
                t.xt = xts[ti % 2]
                t.xk = lambda b: ('xt', ti % 2, b)
                t.hT3 = hTs[ti % 2]
                t.hTk = 'hT%d' % (ti % 2)
                t.hTv = t.hT3[:, :, 0:Tn]
                t.y3 = ybf[:, 0:16 * Tn].rearrange("p (f t) -> p f t", f=16)
                return t

            def front(t):
                for b in range(t.nb):
                    src, rk = src_rows(L, t.row0 + 128 * b, 128)
                    T.dma('sp', t.xt[:, b, :], src, r=rk, w=[t.xk(b)])
                for b in range(t.nb):
                    norm_and_transpose(t.xt[:, b, :], t.xk(b), L, h, t.hT3, 'h', t.hTk, stat, 16 * (t.ti % 2) + 2 * b,
                                       128 * b)

            def views(t, g):
                ue = ues[g % 2]
                Tn = t.Tn
                if not t.sample:
                    EW = 15 + Tn
                    r3 = lambda buf: buf[:, 0:4 * EW].rearrange("p (c t) -> p c t", c=4)
                else:
                    r3 = lambda buf: buf[:, 0:4 * 16 * 23].rearrange("p (c t) -> p c t", c=4)
                return r3(ue), r3(sa), r3(sbuf2), 'ue%d' % (g % 2)

            def U(t, g):
                Tn = t.Tn
                ue3, sa3, sb3, uek = views(t, g)
                if not t.sample:
                    T.op('act', lambda e: e.activation(out=ue3[:, :, 0:15], in_=carry[:, 4 * g:4 * g + 4, :],
                                                       func=AF.Copy), r=['carry'], w=[uek])
                else:
                    ue4 = ue3.rearrange("p c (s t) -> p c s t", s=16)
                    for a in range(2):
                        T.dma('sp', strows[:, a, :],
                              spool[j, 8 * a:8 * a + 8, :, g * 512:(g + 1) * 512].rearrange("s r n -> (s r) n"),
                              w=['strows'])
                    for cc in range(4):
                        pb, pk = bank()
                        for a in range(2):
                            T.op('pe', lambda e, a=a, cc=cc: e.transpose(
                                pb[:, a * 120:(a + 1) * 120], strows[:, a, cc * 128:(cc + 1) * 128],
                                ident_f[0:120, 0:120]), r=['strows', 'ident_f'], w=[pk])
                        T.op('act', lambda e, cc=cc: e.activation(
                            out=ue4[:, cc, :, 0:15], in_=pb[:, 0:240].rearrange("p (s r) -> p s r", r=15),
                            func=AF.Copy), r=[pk], w=[uek])
                for cc in range(4):
                    c = 4 * g + cc
                    pb, pk = bank()
                    for k in range(8):
                        T.op('pe', lambda e, k=k: e.matmul(pb[:, 0:Tn], lhsT=w_in3[:, k, c * 128:(c + 1) * 128],
                                                           rhs=t.hTv[:, k, :], start=(k == 0), stop=(k == 7)),
                             r=WAK + [t.hTk], w=[pk])
                    if not t.sample:
                        T.op('act', lambda e, cc=cc: e.activation(out=ue3[:, cc, 15:15 + Tn], in_=pb[:, 0:Tn],
                                                                  func=AF.Copy), r=[pk], w=[uek])
                    else:
                        T.op('act', lambda e, cc=cc: e.activation(
                            out=ue4[:, cc, :, 15:23], in_=pb[:, 0:128].rearrange("p (s t) -> p s t", t=8),
                            func=AF.Copy), r=[pk], w=[uek])
                        pb2, pk2 = bank()
                        T.op('act', lambda e: e.activation(out=szs[0][:, 0:128], in_=pb[:, 0:128], func=AF.Copy),
                             r=[pk], w=['sz0'])
                        T.op('pe', lambda e: e.transpose(pb2[:, 0:128], szs[0][:, 0:128], ident_f[:]),
                             r=['sz0', 'ident_f'], w=[pk2])
                        T.op('act', lambda e, c=c: e.activation(out=tokrows[:, c * 128:(c + 1) * 128],
                                                                in_=pb2[:, 0:128], func=AF.Copy),
                             r=[pk2], w=['tokrows'])
                if not t.sample:
                    T.op('act', lambda e: e.activation(out=carry[:, 4 * g:4 * g + 4, :],
                                                       in_=ue3[:, :, Tn:Tn + 15], func=AF.Copy), r=[uek], w=['carry'])

            def Pst(t, g):
                Tn = t.Tn
                wdw = POOL_W[g]
                ue3, sa3, sb3, uek = views(t, g)
                pg = pgs[g % 2]
                pgk = 'pg%d' % (g % 2)
                if not t.sample:
                    def sl(buf3, lo, hi):
                        return buf3[:, :, lo:hi]
                    Wd = 15 + Tn
                else:
                    def sl(buf3, lo, hi):
                        return buf3.rearrange("p c (s t) -> p c s t", s=16)[:, :, :, lo:hi]
                    Wd = 23
                cur, curkey = ue3, uek
                tmp = [(sa3, 'sa'), (sb3, 'sb')]
                step = 1
                ti = 0
                eng_rr = ['dve', 'dve']
                while step < wdw:
                    dst, dkey = tmp[ti % 2]
                    lo = 2 * step - 1
                    T.op(eng_rr[ti % 2], lambda e, cur=cur, dst=dst, step=step, lo=lo: e.tensor_tensor(
                        out=sl(dst, lo, Wd), in0=sl(cur, lo, Wd), in1=sl(cur, lo - step, Wd - step), op=ALU.add),
                        r=[curkey], w=[dkey])
                    cur, curkey = dst, dkey
                    step *= 2
                    ti += 1
                if not t.sample:
                    pg3 = pg[:, 0:4 * Tn].rearrange("p (c t) -> p c t", c=4)
                    T.op('dve', lambda e, cur=cur: e.scalar_tensor_tensor(
                        out=pg3, in0=cur[:, :, 15:15 + Tn], scalar=1.0 / wdw, in1=ue3[:, :, 15:15 + Tn],
                        op0=ALU.mult, op1=ALU.subtract), r=[curkey, uek], w=[pgk])
                    if t.first:
                        dst, dkey = tmp[ti % 2]
                        T.op('dve', lambda e, cur=cur, dst=dst: e.tensor_tensor(
                            out=dst[:, :, 15:31], in0=cur[:, :, 15:31],
                            in1=icnt3[:, g, :].unsqueeze(1).broadcast_to([128, 4, 16]), op=ALU.mult),
                            r=[curkey, 'icnt'], w=[dkey])
                        T.op('dve', lambda e, dst=dst: e.tensor_tensor(
                            out=pg3[:, :, 0:16], in0=dst[:, :, 15:31], in1=ue3[:, :, 15:31], op=ALU.subtract),
                            r=[dkey, uek, pgk], w=[pgk])
                else:
                    pg2 = pg[:, 0:4 * 128].rearrange("p (cs t) -> p cs t", t=8)
                    c2 = cur.rearrange("p c (s t) -> p (c s) t", s=16)[:, :, 15:23]
                    u2 = ue3.rearrange("p c (s t) -> p (c s) t", s=16)[:, :, 15:23]
                    T.op('dve', lambda e: e.scalar_tensor_tensor(
                        out=pg2, in0=c2, scalar=1.0 / wdw, in1=u2, op0=ALU.mult, op1=ALU.subtract),
                        r=[curkey, uek], w=[pgk])

            def Zp(t, g):
                Tn = t.Tn
                for dd in range(4):
                    d = 4 * g + dd
                    pz, pzk = bank()
                    for k in range(8):
                        T.op('pe', lambda e, k=k: e.matmul(pz[:, 0:Tn], lhsT=w_in3[:, k, DI + d * 128:DI + (d + 1) * 128],
                                                           rhs=t.hTv[:, k, :], start=(k == 0), stop=(k == 7)),
                             r=WAK + [t.hTk], w=[pzk])
                    T.op('act', lambda e, dd=dd: e.activation(out=szs[dd][:, 0:Tn], in_=pz[:, 0:Tn], func=AF.Silu),
                         r=[pzk], w=['sz%d' % dd])

            def M(t, g):
                Tn = t.Tn
                pg = pgs[g % 2]
                pgk = 'pg%d' % (g % 2)
                pg3 = pg[:, 0:4 * Tn].rearrange("p (c t) -> p c t", c=4)
                for dd in range(4):
                    d = 4 * g + dd
                    pm, pmk = PS[:, (4 + dd) * 512:(5 + dd) * 512], ('ps', 4 + dd)
                    for cc in range(4):
                        T.op('pe', lambda e, cc=cc: e.matmul(pm[:, 0:Tn], lhsT=w_mix3[:, 4 * g + cc, dd * 128:(dd + 1) * 128],
                                                             rhs=pg3[:, cc, :], start=(cc == 0), stop=(cc == 3)),
                             r=WAK + [pgk], w=[pmk])
                    T.op('dve', lambda e, dd=dd, d=d: e.scalar_tensor_tensor(
                        out=t.y3[:, d, :], in0=pm[:, 0:Tn], scalar=pscale[:, j * 16 + d:j * 16 + d + 1],
                        in1=szs[dd][:, 0:Tn], op0=ALU.mult, op1=ALU.mult), r=[pmk, 'sz%d' % dd, 'pscale'], w=['y'])

            def back(t):
                for b in range(t.nb):
                    out_proj_and_store(L, t.y3, 'y', 128 * b, t.xt[:, b, :], t.xk(b), t.row0 + 128 * b, w_out3, b,
                                       h, 'h', stat)
                if t.last:
                    pgp, pgk_ = bgroup(4)
                    for c in range(16):
                        T.op('pe', lambda e, c=c: e.transpose(pgp[0:15, c * 128:(c + 1) * 128], carry[:, c, :],
                                                              ident_f[:]), r=['carry', 'ident_f'], w=pgk_)
                    ystage = ybf.bitcast(F32)
                    T.op('act', lambda e: e.activation(out=ystage[0:15, :], in_=pgp[0:15, :], func=AF.Copy),
                         r=pgk_, w=['y'])
                    T.dma('sp', poolp[j], ystage[0:15, :], r=['y'])
                if t.sample:
                    for s in range(NS):
                        T.dma('sp', pools[j, s, 7:15, :], tokrows[8 * s:8 * s + 8, :], r=['tokrows'])
                    T.dma('sp', pools[j, :, 0:7, :], spool[j, :, 8:15, :])

            tiles = [mk(ti, 256 * ti, 256, False, ti == 0, ti == 7) for ti in range(PLAN.get('ptiles', 8))]
            if PLAN.get('psample', True):
                tiles.append(mk(len(tiles), SEQ, 128, True, False, False))
            if tiles:
                front(tiles[0])
                U(tiles[0], 0)
            for i, t in enumerate(tiles):
                for g in range(4):
                    if g + 1 < 4:
                        U(t, g + 1)
                    Zp(t, g)
                    Pst(t, g)
                    M(t, g)
                if i + 1 < len(tiles):
                    front(tiles[i + 1])
                    U(tiles[i + 1], 0)
                back(t)

        def ssd_layer(j):
            L = 2 * j + 1
            T.barrier()
            st['nb'] = 4
            A.reset()
            w_in3 = WA[:].rearrange("p (k n) -> p k n", k=8)
            w_out3 = WB[:].rearrange("p (k n) -> p k n", k=16)
            load_weights_bf16(w_in3, ssd_in_w[j].rearrange("(k p) n -> p k n", p=128), 8, 'WA')
            load_weights_bf16(w_out3, ssd_out_w[j].rearrange("(k p) n -> p k n", p=128), 16, 'WB')
            cw = convw[:, j * 96:(j + 1) * 96].rearrange("p (c k) -> p c k", k=4)
            cb = convb[:, j * 24:(j + 1) * 24]
            dtb_j = dtb[:, j * 32:(j + 1) * 32]
            negA_j = negA[:, j * 32:(j + 1) * 32]
            dsk_j = dsk[:, j * 32:(j + 1) * 32]
            snw_j = snw[:, j * 16:(j + 1) * 16]

            xts = [A.f32(1024), A.f32(1024)]
            h = A.bf16(1024)
            hTs = [A.bf16(8 * 128).rearrange("p (k t) -> p k t", k=8) for _ in range(2)]
            xe4s = [A.f32(704), A.f32(704)]
            acc4s = [A.f32(512), A.f32(512)]
            BC = A.bf16(8 * 128).rearrange("p (c t) -> p c t", c=8)
            h0a_off = A.off
            tmpA = [A.f32(512), A.f32(512)]
            lq_off = A.off
            Lq = [A.f32(512), A.f32(512)]
            stage1k = arena_t[:, lq_off:lq_off + 1024]
            rawc4 = Lq[0]
            scr4 = Lq[1][0:48, :]
            xs_tok = A.f32(2048)
            bfA = A.bf16(2048)
            bfB = A.bf16(2048)
            Btok = A.bf16(4 * 128).rearrange("p (g n) -> p g n", g=4)
            CBTm = A.f32(512).rearrange("p (g i) -> p g i", g=4)
            WTq = [A.bf16(512), A.bf16(512)]
            ybuf = A.f32(2048)
            ST_all = A.f32(3072)
            ST = ST_all[:, 0:2048]
            STb = ST_all[:, 2048:3072].bitcast(BF16)
            rawT = ST_all
            smalls = [A.f32(256), A.f32(256)]
            ccarry = A.f32(72).rearrange("p (c r) -> p c r", c=24)
            CTm = WTq
            Bm = [A.bf16(512), A.bf16(512)]
            dcol = A.f32(256).rearrange("p (a s) -> p a s", a=16)
            stat = A.f32(64)
            if PLAN.get('verbose'):
                print('ssd arena words used', A.off, 'of', A.words)
            h0s = [arena_t[:, h0a_off:h0a_off + 2048], xs_tok]
            h0keys = ['h0a', 'xs_tok']

            T.op('dve', lambda e: e.memset(ccarry, 0.0), w=['ccarry'])

            def bc64(v32):
                return v32.unsqueeze(2).broadcast_to([128, 32, 64])

            def v3(ap2048):
                return ap2048.rearrange("p (h d) -> p h d", h=32)

            def chunk(ci, row0, sample):
                first = (ci == 0)
                par = ci % 2
                K = lambda n: n + str(par)
                xt = xts[par]
                hT3 = hTs[par]
                small = smalls[par]
                dtv = small[:, 0:32]
                av = small[:, 32:64]
                ex = small[:, 64:160]
                dte = small[:, 160:192]
                tdt = small[:, 192:224]
                ecum = ex[:, 0:32]
                eaft = ex[:, 32:64]
                dec_bc = ex[:, 64:96]
                src, rk = src_rows(L, row0, 128)
                T.dma('sp', xt, src, r=rk, w=[K('xt')])
                norm_and_transpose(xt, K('xt'), L, h, hT3, 'h', K('hT'), stat, 16 * par, 0)
                pb, pk = bank()
                for k in range(8):
                    T.op('pe', lambda e, k=k: e.matmul(pb[:, 0:32], lhsT=hT3[:, k, :], rhs=w_in3[:, k, 5120:5152],
                                                       start=(k == 0), stop=(k == 7)), r=[K('hT')] + WAK, w=[pk])
                T.op('dve', lambda e: e.tensor_tensor(out=tdt, in0=pb[:, 0:32], in1=dtb_j, op=ALU.add),
                     r=[pk, 'dtb'], w=[K('tdt')])
                T.op('act', lambda e: e.activation(out=tdt, in_=tdt, func=AF.Exp), r=[K('tdt')], w=[K('tdt')])
                T.op('act', lambda e: e.activation(out=dtv, in_=tdt, func=AF.Ln, bias=1.0), r=[K('tdt')], w=[K('dt')])
                T.op('dve', lambda e: e.tensor_tensor(out=av, in0=dtv, in1=negA_j, op=ALU.mult),
                     r=[K('dt'), 'negA'], w=[K('a')])
                pb, pk = bank()
                T.op('pe', lambda e: e.matmul(pb[:, 0:32], lhsT=(Us if sample else Umat)[:], rhs=av, start=True, stop=True),
                     r=[K('a'), 'U', 'Us'], w=[pk])
                T.op('pe', lambda e: e.matmul(pb[:, 32:64], lhsT=(SLs if sample else SLmat)[:], rhs=av, start=True,
                                              stop=True), r=[K('a'), 'SL', 'SLs'], w=[pk])
                T.op('pe', lambda e: e.matmul(pb[:, 64:96], lhsT=ones_f[:], rhs=av, start=True, stop=True),
                     r=[K('a'), 'ones_f'], w=[pk])
                T.op('act', lambda e: e.activation(out=ex, in_=pb[:, 0:96], func=AF.Exp), r=[pk], w=[K('ex')])
                T.op('dve', lambda e: e.tensor_tensor(out=dte, in0=eaft, in1=dtv, op=ALU.mult), r=[K('ex'), K('dt')], w=[K('dte')])
                yield 'front'
                def xbc_front(cg):
                    xe4 = xe4s[cg % 2]
                    xek = 'xe%d' % (cg % 2)
                    pb, pk = bank()
                    for cc in range(4):
                        c = 4 * cg + cc
                        for k in range(8):
                            T.op('pe', lambda e, k=k, c=c, cc=cc: e.matmul(
                                pb[:, cc * 128:(cc + 1) * 128], lhsT=w_in3[:, k, DI + c * 128:DI + (c + 1) * 128],
                                rhs=hT3[:, k, :], start=(k == 0), stop=(k == 7)), r=WAK + [K('hT')], w=[pk])
                    if not sample:
                        xe3 = xe4[:, 0:4 * 131].rearrange("p (c t) -> p c t", c=4)
                        T.op('act', lambda e: e.activation(out=xe3[:, :, 0:3], in_=ccarry[:, 4 * cg:4 * cg + 4, :],
                                                           func=AF.Copy), r=['ccarry'], w=[xek])
                        T.op('act', lambda e: e.activation(out=xe3[:, :, 3:131],
                                                           in_=pb.rearrange("p (c t) -> p c t", c=4), func=AF.Copy),
                             r=[pk], w=[xek])
                        T.op('act', lambda e: e.activation(out=ccarry[:, 4 * cg:4 * cg + 4, :], in_=xe3[:, :, 128:131],
                                                           func=AF.Copy), r=[xek], w=['ccarry'])
                    else:
                        xe4v = xe4.rearrange("p (c s t) -> p c s t", c=4, s=16)
                        T.dma('sp', scr4, sconv[j, :, :, cg * 512:(cg + 1) * 512].rearrange("s r n -> (s r) n"),
                              w=['Lq1'])
                        pb2, pk2 = bank()
                        for cc in range(4):
                            T.op('pe', lambda e, cc=cc: e.transpose(pb2[:, cc * 48:(cc + 1) * 48],
                                                                    scr4[:, cc * 128:(cc + 1) * 128], ident_f[0:48, 0:48]),
                                 r=['Lq1', 'ident_f'], w=[pk2])
                        T.op('act', lambda e: e.activation(
                            out=xe4v[:, :, :, 0:3], in_=pb2[:, 0:192].rearrange("p (c s r) -> p c s r", c=4, s=16),
                            func=AF.Copy), r=[pk2], w=[xek])
                        T.op('act', lambda e: e.activation(out=rawc4, in_=pb, func=AF.Copy), r=[pk], w=['Lq0'])
                        T.op('pool', lambda e: e.tensor_copy(
                            out=xe4v[:, :, :, 3:11], in_=rawc4.rearrange("p (c s t) -> p c s t", c=4, s=16)),
                            r=['Lq0'], w=[xek])
                        pb3, pk3 = bank()
                        for cc in range(4):
                            T.op('pe', lambda e, cc=cc: e.transpose(pb3[:, cc * 128:(cc + 1) * 128],
                                                                    rawc4[:, cc * 128:(cc + 1) * 128], ident_f[:]),
                                 r=['Lq0', 'ident_f'], w=[pk3])
                        T.op('act', lambda e: e.activation(out=rawT[:, cg * 512:(cg + 1) * 512], in_=pb3, func=AF.Copy),
                             r=[pk3], w=['rawT'])

                def xbc_back(cg):
                    xe4 = xe4s[cg % 2]
                    xek = 'xe%d' % (cg % 2)
                    acc4 = acc4s[cg % 2]
                    acck = 'acc%d' % (cg % 2)
                    if not sample:
                        xe3 = xe4[:, 0:4 * 131].rearrange("p (c t) -> p c t", c=4)
                        tap = lambda cc, kk: xe3[:, cc, kk:kk + 128]
                        accv = lambda cc: acc4[:, cc * 128:(cc + 1) * 128]
                    else:
                        xe4v = xe4.rearrange("p (c s t) -> p c s t", c=4, s=16)
                        tap = lambda cc, kk: xe4v[:, cc, :, kk:kk + 8]
                        accv = lambda cc: acc4[:, cc * 128:(cc + 1) * 128].rearrange("p (s t) -> p s t", t=8)
                    for kk in range(4):
                        for cc in range(4):
                            c = 4 * cg + cc
                            if kk == 0:
                                T.op('dve', lambda e, cc=cc, c=c: e.tensor_scalar(
                                    out=accv(cc), in0=tap(cc, 0), scalar1=cw[:, c, 0:1], scalar2=None, op0=ALU.mult),
                                    r=[xek, 'convw'], w=[(acck, cc)])
                            else:
                                T.op('dve', lambda e, cc=cc, c=c, kk=kk: e.scalar_tensor_tensor(
                                    out=accv(cc), in0=tap(cc, kk), scalar=cw[:, c, kk:kk + 1], in1=accv(cc),
                                    op0=ALU.mult, op1=ALU.add), r=[xek, 'convw', (acck, cc)], w=[(acck, cc)])
                    for cc in range(4):
                        c = 4 * cg + cc
                        a1 = acc4[:, cc * 128:(cc + 1) * 128]
                        if c < 16:
                            T.op('act', lambda e, c=c, a1=a1: e.activation(out=a1, in_=a1, func=AF.Silu, bias=cb[:, c:c + 1]),
                                 r=[(acck, cc), 'convb'], w=[(acck, cc)])
                        else:
                            T.op('act', lambda e, c=c, a1=a1: e.activation(out=BC[:, c - 16, :], in_=a1, func=AF.Silu,
                                                                           bias=cb[:, c:c + 1]),
                                 r=[(acck, cc), 'convb'], w=['BC'])
                    if cg < 4:
                        xg, xgk = hbank(cg)
                        for cc in range(4):
                            T.op('pe', lambda e, cc=cc: e.transpose(xg[:, cc * 128:(cc + 1) * 128],
                                                                    acc4[:, cc * 128:(cc + 1) * 128], ident_f[:]),
                                 r=[(acck, cc), 'ident_f'], w=[xgk])
                        T.op('act', lambda e: e.activation(out=xs_tok[:, cg * 512:(cg + 1) * 512], in_=xg, func=AF.Copy),
                             r=[xgk], w=['xs_tok'])

                xbc_front(0)
                for cg in range(6):
                    if cg + 1 < 6:
                        xbc_front(cg + 1)
                    xbc_back(cg)
                    yield 'xbc'
                st['nb'] = 4
                pb, pk = bank()
                pbb = pb.bitcast(BF16)
                for g in range(4):
                    T.op('pe', lambda e, g=g: e.transpose(pbb[:, g * 128:(g + 1) * 128], BC[:, g, :], ident_b[:]),
                         r=['BC', 'ident_b'], w=[pk])
                T.op('act', lambda e: e.activation(out=Btok, in_=pbb[:, 0:512].rearrange("p (g n) -> p g n", g=4),
                                                   func=AF.Copy), r=[pk], w=['Btok'])
                pb, pk = bank()
                for g in range(4):
                    T.op('pe', lambda e, g=g: e.matmul(pb[:, g * 128:(g + 1) * 128], lhsT=BC[:, g, :], rhs=BC[:, 4 + g, :],
                                                       start=True, stop=True), r=['BC'], w=[pk])
                msk = Us if sample else Umat
                T.op('dve', lambda e: e.tensor_tensor(out=CBTm, in0=pb.rearrange("p (g i) -> p g i", g=4),
                                                      in1=msk[:].unsqueeze(1).broadcast_to([128, 4, 128]), op=ALU.mult),
                     r=[pk, 'U', 'Us'], w=['CBTm'])
                if sample:
                    arep = ybuf
                    T.op('dve', lambda e: e.tensor_copy(out=v3(arep), in_=bc64(av)), r=[K('a')], w=['y'])
                    pb, pk = bank()
                    for hp in range(16):
                        T.op('pe', lambda e, hp=hp: e.matmul(pb[:, hp * 16:(hp + 1) * 16],
                                                             lhsT=arep[:, hp * 128:(hp + 1) * 128], rhs=seqind[:],
                                                             start=True, stop=True), r=['y', 'seqind'], w=[pk])
                    T.op('act', lambda e: e.activation(out=dcol, in_=pb[:, 0:256].rearrange("p (a s) -> p a s", a=16),
                                                       func=AF.Exp), r=[pk], w=['dcol'])
                T.op('dve', lambda e: e.tensor_tensor(out=v3(bfA), in0=v3(xs_tok), in1=bc64(dtv), op=ALU.mult),
                     r=['xs_tok', K('dt')], w=['bfA'])
                T.op('dve', lambda e: e.tensor_tensor(out=v3(ybuf), in0=v3(xs_tok), in1=bc64(dsk_j), op=ALU.mult),
                     r=['xs_tok', 'dsk'], w=['y'])
                T.op('dve', lambda e: e.tensor_tensor(out=v3(bfB), in0=v3(xs_tok), in1=bc64(dte), op=ALU.mult),
                     r=['xs_tok', K('dte')], w=['bfB'])
                deferred = []
                if not sample:
                    if not first:
                        for g in range(4):
                            def yoff_unit(g=g):
                                pb, pk = bank()
                                T.op('pe', lambda e: e.matmul(pb, lhsT=BC[:, 4 + g, :], rhs=STb[:, g * 512:(g + 1) * 512],
                                                              start=True, stop=True), r=['BC', 'STb'], w=[pk])
                                tq = xs_tok[:, g * 512:(g + 1) * 512]
                                T.op('dve', lambda e: e.tensor_tensor(
                                    out=tq.rearrange("p (h d) -> p h d", h=8), in0=pb.rearrange("p (h d) -> p h d", h=8),
                                    in1=ecum[:, 8 * g:8 * g + 8].unsqueeze(2).broadcast_to([128, 8, 64]), op=ALU.mult),
                                    r=[pk, K('ex')], w=['xs_tok'])
                                T.op('dve', lambda e: e.tensor_tensor(
                                    out=ybuf[:, g * 512:(g + 1) * 512], in0=ybuf[:, g * 512:(g + 1) * 512], in1=tq,
                                    op=ALU.add), r=['xs_tok', 'y'], w=['y'])
                            deferred.append(yoff_unit)
                    for g in range(4):
                        def cs_unit(g=g):
                            pb, pk = bank()
                            T.op('pe', lambda e: e.matmul(pb, lhsT=Btok[:, g, :], rhs=bfB[:, g * 512:(g + 1) * 512],
                                                          start=True, stop=True), r=['Btok', 'bfB'] + (['STb'] if not first else []),
                                 w=[pk])
                            sg = ST[:, g * 512:(g + 1) * 512]
                            if first:
                                T.op('act', lambda e: e.activation(out=sg, in_=pb, func=AF.Copy), r=[pk], w=[('ST', g)])
                            else:
                                T.op('dve', lambda e: e.tensor_tensor(
                                    out=sg.rearrange("p (h d) -> p h d", h=8), in0=sg.rearrange("p (h d) -> p h d", h=8),
                                    in1=dec_bc[:, 8 * g:8 * g + 8].unsqueeze(2).broadcast_to([128, 8, 64]), op=ALU.mult),
                                    r=[('ST', g), K('ex')], w=[('ST', g)])
                                T.op('dve', lambda e: e.tensor_tensor(out=sg, in0=sg, in1=pb, op=ALU.add),
                                     r=[pk, ('ST', g)], w=[('ST', g)])
                        deferred.append(cs_unit)
                pgd, pgdk = bgroup(4)
                segp = {}

                def seg_front(q):
                    ax = tmpA[q % 2]
                    axk = 'tmpA%d' % (q % 2)
                    ax3 = ax.rearrange("p (h i) -> p h i", h=4)
                    T.op('pool', lambda e: e.tensor_tensor(
                        out=ax3, in0=av[:, 4 * q:4 * q + 4].unsqueeze(2).broadcast_to([128, 4, 128]),
                        in1=Umat[:].unsqueeze(1).broadcast_to([128, 4, 128]), op=ALU.mult), r=[K('a'), 'U'], w=[axk])
                    pb, pk = bank()
                    T.op('pe', lambda e: e.matmul(pb, lhsT=SLmat[:], rhs=ax, start=True, stop=True),
                         r=[axk, 'SL'], w=[pk])
                    segp[q] = (pb, pk)

                def seg_back(q):
                    pb, pk = segp[q]
                    lq = Lq[q % 2]
                    lqk = 'Lq%d' % (q % 2)
                    wt = WTq[q % 2]
                    wtk = 'WT%d' % (q % 2)
                    T.op('act', lambda e: e.activation(out=lq, in_=pb, func=AF.Exp), r=[pk], w=[lqk])
                    T.op('dve', lambda e: e.tensor_tensor(
                        out=wt.rearrange("p (h i) -> p h i", h=4), in0=lq.rearrange("p (h i) -> p h i", h=4),
                        in1=CBTm[:, q // 2, :].unsqueeze(1).broadcast_to([128, 4, 128]), op=ALU.mult),
                        r=[lqk, 'CBTm'], w=[wtk])
                    for hh in range(4):
                        hd = 4 * q + hh
                        T.op('pe', lambda e, hh=hh, hd=hd: e.matmul(pgd[:, hd * 64:(hd + 1) * 64],
                                                                    lhsT=wt[:, hh * 128:(hh + 1) * 128],
                                                                    rhs=bfA[:, hd * 64:(hd + 1) * 64], start=True, stop=True),
                             r=[wtk, 'bfA'], w=[pgdk[hd // 8]])

                seg_front(0)
                for q in range(8):
                    if q + 1 < 8:
                        seg_front(q + 1)
                    seg_back(q)
                    if q < len(deferred):
                        deferred[q]()
                for fn in deferred[8:]:
                    fn()
                T.op('dve', lambda e: e.tensor_tensor(out=ybuf, in0=ybuf, in1=pgd, op=ALU.add),
                     r=pgdk + ['y'], w=['y'])
                STK = [('ST', g) for g in range(4)]
                if not sample:
                    if ci < 15:
                        T.op('act', lambda e: e.activation(out=STb, in_=ST, func=AF.Copy), r=STK, w=['STb'])
                    else:
                        pgt, pgtk = bgroup(4)
                        for hp in range(16):
                            T.op('pe', lambda e, hp=hp: e.transpose(pgt[:, hp * 128:(hp + 1) * 128],
                                                                    ST[:, hp * 128:(hp + 1) * 128], ident_f[:]),
                                 r=STK + ['ident_f'], w=[pgtk[hp // 4]])
                        T.op('act', lambda e: e.activation(out=xs_tok, in_=pgt, func=AF.Copy), r=pgtk,
                             w=['xs_tok'])
                        T.dma('sp', ssmp[j].rearrange("(hp two) p n -> (two p) hp n", two=2),
                              xs_tok.rearrange("p (a n) -> p a n", a=16), r=['xs_tok'])
                        for third in range(3):
                            pgv, pgvk = bgroup(2)
                            for cc in range(8):
                                c = third * 8 + cc
                                T.op('pe', lambda e, c=c, cc=cc: e.transpose(pgv[0:3, cc * 128:(cc + 1) * 128],
                                                                             ccarry[:, c, :], ident_f[:]),
                                     r=['ccarry', 'ident_f'], w=[pgvk[cc // 4]])
                            T.op('act', lambda e: e.activation(out=stage1k[0:3, :], in_=pgv[0:3, :],
                                                               func=AF.Copy), r=pgvk, w=['Lq0', 'Lq1'])
                            T.dma('sp', convp[j, :, third * 1024:(third + 1) * 1024], stage1k[0:3, :], r=['Lq0', 'Lq1'])
                else:
                    T.barrier()
                    if PLAN.get('dump'):
                        T.dma('pool', yp[0:128, :], bfB[:, 0:1024])
                        T.dma('pool', yp[128:256, :], bfB[:, 1024:2048])
                        T.dma('sp', yp[256:384, 0:224], small[:, 0:224])
                        T.dma('sp', yp[384:512, :], xs_tok[:, 0:1024])
                        T.dma('sp', yp[512:640, :], xs_tok[:, 1024:2048])
                        T.barrier()
                    pgo, pgok = PS[:, 0:2048], [('ps', i) for i in range(4)]
                    for s in range(NS):
                        h0 = h0s[s % 2]
                        h0k = 'h0_%d' % (s % 2)
                        h03 = h0.rearrange("p (a n) -> p a n", a=16)
                        if s == 0:
                            T.dma('sp', h03, sssm[j, 0].rearrange("(hp two) p n -> (two p) hp n", two=2), w=[h0k])
                        if s + 1 < NS:
                            T.dma('sp', h0s[(s + 1) % 2].rearrange("p (a n) -> p a n", a=16),
                                  sssm[j, s + 1].rearrange("(hp two) p n -> (two p) hp n", two=2),
                                  w=['h0_%d' % ((s + 1) % 2)])
                        ctm = CTm[s % 2]
                        ctmk = 'WT%d' % (s % 2)
                        bm = Bm[s % 2]
                        bmk = 'Bm%d' % (s % 2)
                        ctm3 = ctm.rearrange("p (g t) -> p g t", g=4)
                        T.op('pool', lambda e, s=s: e.affine_select(
                            out=ctm3, in_=BC[:, 4:8, :], pattern=[[0, 4], [1, 128]], compare_op=ALU.is_ge, fill=0.0,
                            base=-8 * s, channel_multiplier=0), r=['BC'], w=[ctmk])
                        T.op('pool', lambda e, s=s: e.affine_select(
                            out=ctm3, in_=ctm3, pattern=[[0, 4], [-1, 128]], compare_op=ALU.is_ge, fill=0.0,
                            base=8 * s + 7, channel_multiplier=0), r=[ctmk], w=[ctmk])
                        T.op('dve', lambda e, s=s: e.tensor_scalar(
                            out=bm, in0=Btok.rearrange("p g n -> p (g n)"), scalar1=seqind[:, s:s + 1], scalar2=None,
                            op0=ALU.mult), r=['Btok', 'seqind'], w=[bmk])
                        for half in range(2):
                            pt = PS[:, 2048 + 0:2048 + 1024]
                            ptk = [('ps', 4), ('ps', 5)]
                            for a in range(8):
                                hp = half * 8 + a
                                T.op('pe', lambda e, a=a, hp=hp: e.transpose(pt[:, a * 128:(a + 1) * 128], h03[:, hp, :],
                                                                             ident_f[:]),
                                     r=[h0k, 'ident_f'], w=[ptk[a // 4]])
                            T.op('act', lambda e, half=half: e.activation(out=bfA[:, half * 1024:(half + 1) * 1024], in_=pt,
                                                                          func=AF.Copy), r=ptk, w=['bfA'])
                        for g in range(4):
                            T.op('pe', lambda e, g=g, s=s: e.matmul(pgo[:, g * 512:(g + 1) * 512],
                                                                    lhsT=ctm[:, g * 128:(g + 1) * 128],
                                                                    rhs=bfA[:, g * 512:(g + 1) * 512],
                                                                    start=(s == 0), stop=(s == NS - 1)),
                                 r=[ctmk, 'bfA'], w=[pgok[g]])
                        for half in range(2):
                            pc = PS[:, 3072:4096]
                            pck = [('ps', 6), ('ps', 7)]
                            for a in range(8):
                                hp = half * 8 + a
                                T.op('pe', lambda e, a=a, hp=hp: e.matmul(pc[:, a * 128:(a + 1) * 128],
                                                                          lhsT=bfB[:, hp * 128:(hp + 1) * 128],
                                                                          rhs=bm[:, (hp // 4) * 128:(hp // 4 + 1) * 128],
                                                                          start=True, stop=True),
                                     r=['bfB', bmk], w=[pck[a // 4]])
                            hv = h03[:, half * 8:(half + 1) * 8, :]
                            T.op('dve', lambda e, hv=hv, half=half, s=s: e.tensor_tensor(
                                out=hv, in0=hv, in1=dcol[:, half * 8:(half + 1) * 8, s:s + 1].broadcast_to([128, 8, 128]),
                                op=ALU.mult), r=[h0k, 'dcol'], w=[h0k])
                            T.op('dve', lambda e, hv=hv: e.tensor_tensor(
                                out=hv, in0=hv, in1=pc.rearrange("p (a n) -> p a n", a=8), op=ALU.add),
                                r=pck + [h0k], w=[h0k])
                        T.dma('sp', ssms[j, s].rearrange("(hp two) p n -> (two p) hp n", two=2), h03, r=[h0k])
                    T.barrier()
                    T.op('dve', lambda e: e.tensor_tensor(out=v3(xs_tok), in0=v3(pgo), in1=bc64(ecum), op=ALU.mult),
                         r=pgok + [K('ex')], w=['xs_tok'])
                    T.op('dve', lambda e: e.tensor_tensor(out=ybuf, in0=ybuf, in1=xs_tok, op=ALU.add),
                         r=['xs_tok', 'y'], w=['y'])
                    for s in range(NS):
                        T.dma('sp', convs[j, s], rawT[8 * s + 5:8 * s + 8, :], r=['rawT'])
                st['nb'] = 6
                yield 'mid'
                for zc in range(4):
                    pb, pk = bank()
                    for k in range(8):
                        T.op('pe', lambda e, k=k: e.matmul(pb, lhsT=hT3[:, k, :], rhs=w_in3[:, k, zc * 512:(zc + 1) * 512],
                                                           start=(k == 0), stop=(k == 7)), r=[K('hT')] + WAK, w=[pk])
                    zs = tmpA[zc % 2]
                    zsk = 'tmpA%d' % (zc % 2)
                    T.op('act', lambda e: e.activation(out=zs, in_=pb, func=AF.Silu), r=[pk], w=[zsk])
                    yield 'backP'
                    T.op('dve', lambda e: e.tensor_tensor(out=ybuf[:, zc * 512:(zc + 1) * 512],
                                                          in0=ybuf[:, zc * 512:(zc + 1) * 512], in1=zs, op=ALU.mult),
                         r=[zsk, 'y'], w=['y'])
                    yield 'back'
                c4 = 16 * par + 4
                T.op('act', lambda e: e.activation(out=bfA, in_=ybuf, func=AF.Square, accum_out=stat[:, c4:c4 + 1]),
                     r=['y'], w=['bfA', 'stat'])
                T.op('act', lambda e: e.activation(out=stat[:, c4 + 1:c4 + 2], in_=stat[:, c4:c4 + 1], func=AF.Sqrt,
                                                   scale=1.0 / DI, bias=epsb[:, 0:1]), r=['stat', 'epsb'], w=['stat'])
                yield 'backP'
                T.op('dve', lambda e: e.reciprocal(out=stat[:, c4:c4 + 1], in_=stat[:, c4 + 1:c4 + 2]),
                     r=['stat'], w=['stat'])
                T.op('dve', lambda e: e.tensor_scalar(out=bfA, in0=ybuf, scalar1=stat[:, c4:c4 + 1], scalar2=None, op0=ALU.mult),
                     r=['y', 'stat'], w=['bfA'])
                yield 'back'
                pg2, pg2k = PS[:, 3072:4096], [('ps', 6), ('ps', 7)]
                pg2b = pg2.bitcast(BF16)
                for f in range(16):
                    T.op('pe', lambda e, f=f: e.transpose(pg2b[:, f * 128:(f + 1) * 128], bfA[:, f * 128:(f + 1) * 128],
                                                          ident_b[:]), r=['bfA', 'ident_b'], w=[pg2k[f // 8]])
                yield 'backP'
                yT3 = bfB.rearrange("p (f t) -> p f t", f=16)
                T.op('dve', lambda e: e.tensor_tensor(out=yT3, in0=pg2b.rearrange("p (f t) -> p f t", f=16),
                                                      in1=snw_j.unsqueeze(2).broadcast_to([128, 16, 128]), op=ALU.mult),
                     r=pg2k + ['snw'], w=['bfB'])
                yield 'back'
                xkey = K('xt')
                for hh in range(2):
                    pb, pk = bank()
                    for f in range(16):
                        T.op('pe', lambda e, f=f: e.matmul(pb, lhsT=yT3[:, f, :], rhs=w_out3[:, f, hh * 512:(hh + 1) * 512],
                                                           start=(f == 0), stop=(f == 15)), r=['bfB'] + WBK, w=[pk])
                    yield 'backP'
                    T.op('dve', lambda e: e.tensor_tensor(out=xt[:, hh * 512:(hh + 1) * 512],
                                                          in0=xt[:, hh * 512:(hh + 1) * 512], in1=pb, op=ALU.add),
                         r=[pk, xkey], w=[xkey])
                    if hh == 0:
                        yield 'back'
                if L < 3:
                    T.dma('sp', xscr[row0:row0 + 128, :], xt, r=[xkey], w=[('xscr', row0 // 128)])
                else:
                    rms_stat(xt, xkey, stat, 8, DM, h, 'h')
                    T.op('dve', lambda e: e.scalar_tensor_tensor(out=xt, in0=xt, scalar=stat[:, 8:9], in1=fnw[:],
                                                                 op0=ALU.mult, op1=ALU.mult),
                         r=[xkey, 'stat', 'fnw'], w=[xkey])
                    dst = yp[row0:row0 + 128, :] if row0 < SEQ else ysm[row0 - SEQ:row0 - SEQ + 128, :]
                    T.dma('sp', dst, xt, r=[xkey])

            return chunk, (ST, STb)

        def run_ssd(j):
            chunk, (ST, STb) = ssd_layer(j)
            n = PLAN.get('schunks', 16)
            gens = [chunk(ci, 128 * ci, False) for ci in range(n)]
            adv = lambda g: next(g, None)
            if n:
                assert adv(gens[0]) == 'front'
                for _ in range(6):
                    assert adv(gens[0]) == 'xbc'
            for ci in range(n):
                assert adv(gens[ci]) == 'mid'
                nxt = gens[ci + 1] if ci + 1 < n else None
                xleft = 0
                if nxt is not None:
                    assert adv(nxt) == 'front'
                    xleft = 6
                done_back = False
                while (not done_back) or xleft:
                    r = None
                    if not done_back:
                        r = adv(gens[ci])
                        if r is None:
                            done_back = True
                    if xleft:
                        assert adv(nxt) == 'xbc'
                        xleft -= 1
                    if r == 'backP':
                        r = adv(gens[ci])
                        if r is None:
                            done_back = True
            T.barrier()
            if PLAN.get('ssample', True):
                for _ in chunk(0, SEQ, True):
                    pass

        if PLAN.get('pool0', True):
            pool_layer(0)
        if PLAN.get('ssd0', True):
            run_ssd(0)
        if PLAN.get('pool1', True):
            pool_layer(1)
        if PLAN.get('ssd1', True):
            run_ssd(1)
        T.finish()
    return nc


_NC_CACHE = {}


def _col(v, nchunk):
    return np.ascontiguousarray(v.reshape(nchunk, 128).T)


def kernel(x_prompt, x_sample, state_pool, state_conv, state_ssm, norm_w, pool_in_w, pool_mix_w, pool_scale,
           pool_out_w, ssd_in_w, ssd_conv_w, ssd_conv_b, ssd_dt_bias, ssd_A_log, ssd_D, ssd_norm_w, ssd_out_w,
           final_norm_w):
    f = lambda a: np.ascontiguousarray(np.asarray(a, dtype=np.float32))
    x_prompt, x_sample, state_pool, state_conv, state_ssm = map(f, (x_prompt, x_sample, state_pool, state_conv, state_ssm))
    norm_w, pool_scale, ssd_conv_w, ssd_conv_b = map(f, (norm_w, pool_scale, ssd_conv_w, ssd_conv_b))
    ssd_dt_bias, ssd_A_log, ssd_D, ssd_norm_w, final_norm_w = map(f, (ssd_dt_bias, ssd_A_log, ssd_D, ssd_norm_w, final_norm_w))
    nwc = np.concatenate([_col(norm_w[l], 8) for l in range(4)], axis=1)
    pscale = np.concatenate([_col(pool_scale[j], 16) for j in range(2)], axis=1)
    convw = np.concatenate([np.ascontiguousarray(ssd_conv_w[j].reshape(4, 24, 128).transpose(2, 1, 0)).reshape(128, 96)
                            for j in range(2)], axis=1)
    convb = np.concatenate([_col(ssd_conv_b[j], 24) for j in range(2)], axis=1)
    snw = np.concatenate([_col(ssd_norm_w[j], 16) for j in range(2)], axis=1)
    shared = {
        "pool_in_w": f(pool_in_w), "pool_mix_w": f(pool_mix_w), "pool_out_w": f(pool_out_w),
        "ssd_in_w": f(ssd_in_w), "ssd_out_w": f(ssd_out_w),
        "nwc": f(nwc), "fnw": f(final_norm_w.reshape(1, DM)), "pscale": f(pscale), "convw": f(convw), "convb": f(convb),
        "dtb": f(ssd_dt_bias.reshape(1, 64)), "alog": f(ssd_A_log.reshape(1, 64)), "dsk": f(ssd_D.reshape(1, 64)),
        "snw": f(snw),
    }
    in_maps = []
    for c in range(NCORES):
        m = dict(shared)
        m["xp"] = x_prompt[c]
        m["xsm"] = f(x_sample[NS * c:NS * (c + 1)].reshape(128, DM))
        m["spool"] = f(state_pool[:, NS * c:NS * (c + 1)])
        m["sconv"] = f(state_conv[:, NS * c:NS * (c + 1)])
        m["sssm"] = f(state_ssm[:, NS * c:NS * (c + 1)])
        in_maps.append(m)
    if "nc" not in _NC_CACHE:
        _NC_CACHE["nc"] = build_program()
    res = run_bass_kernel_spmd(_NC_CACHE["nc"], in_maps, core_ids=list(range(NCORES)))
    R = res.results
    y_prompt = np.stack([R[c]["yp"] for c in range(NCORES)], axis=0)
    y_sample = np.concatenate([R[c]["ysm"].reshape(NS, DS, DM) for c in range(NCORES)], axis=0)
    pool_p = np.stack([R[c]["poolp"] for c in range(NCORES)], axis=1)
    pool_s = np.concatenate([R[c]["pools"] for c in range(NCORES)], axis=1)
    conv_p = np.stack([R[c]["convp"] for c in range(NCORES)], axis=1)
    conv_s = np.concatenate([R[c]["convs"] for c in range(NCORES)], axis=1)
    ssm_p = np.stack([R[c]["ssmp"] for c in range(NCORES)], axis=1)
    ssm_s = np.concatenate([R[c]["ssms"] for c in range(NCORES)], axis=1)
    return tuple(np.ascontiguousarray(a, dtype=np.float32) for a in
                 (y_prompt, y_sample, pool_p, pool_s, conv_p, conv_s, ssm_p, ssm_s))
```

```python
import numpy as np
import concourse.bass as bass
import concourse.mybir as mybir
from concourse.bass_utils import run_bass_kernel_spmd
from contextlib import ExitStack

F32 = mybir.dt.float32
BF16 = mybir.dt.bfloat16
AF = mybir.ActivationFunctionType
ALU = mybir.AluOpType

NCORES = 8
PLAN = {}


class Stop(Exception):
    pass


def stage(n, cond=True):
    if cond and PLAN.get('stop') == n:
        raise Stop()
DM = 1024
DI = 2048
SEQ = 2048
NS = 16
DS = 8
CONV_DIM = 3072
SSD_IN = 5152
EPS = 1e-6
POOL_W = (2, 4, 8, 16)


class Trk:
    def __init__(self, nc, es):
        self.nc = nc
        self.E = {'pe': nc.tensor, 'act': nc.scalar, 'dve': nc.vector, 'pool': nc.gpsimd, 'sp': nc.sync}
        self.sem = {e: es.enter_context(nc.semaphore('c_' + e)) for e in ('pe', 'act', 'dve', 'pool')}
        self.cnt = {e: 0 for e in self.sem}
        self.dq = {'sp': [es.enter_context(nc.semaphore('dsp%d' % i)) for i in range(14)],
                   'pool': [es.enter_context(nc.semaphore('dpl%d' % i)) for i in range(6)]}
        self.dcnt = {q: [0] * len(v) for q, v in self.dq.items()}
        self.drr = {q: 0 for q in self.dq}
        self.seen = {e: {} for e in self.E}
        self.bufs = {}

    def _semobj(self, k):
        return self.sem[k] if isinstance(k, str) else self.dq[k[0]][k[1]]

    def _collect(self, r, w):
        need = {}

        def add(ev):
            for k, v in ev.items():
                if need.get(k, 0) < v:
                    need[k] = v
        for key in r:
            b = self.bufs.get(key)
            if b:
                add(b['w'])
        for key in w:
            b = self.bufs.get(key)
            if b:
                add(b['w'])
                add(b['r'])
        return need

    def _emit_waits(self, e, need):
        for k, v in need.items():
            if k == 'pe' and e == 'pe':
                continue
            if k == e and PLAN.get('noself'):
                continue
            if self.seen[e].get(k, 0) >= v:
                continue
            self.E[e].wait_ge(self._semobj(k), v)
            self.seen[e][k] = v

    def _update(self, r, w, k, v):
        for key in r:
            b = self.bufs.setdefault(key, {'w': {}, 'r': {}})
            if b['r'].get(k, 0) < v:
                b['r'][k] = v
        for key in w:
            self.bufs[key] = {'w': {k: v}, 'r': {}}

    def op(self, e, fn, r=(), w=()):
        self._emit_waits(e, self._collect(r, w))
        ins = fn(self.E[e])
        self.cnt[e] += 1
        ins.then_inc(self.sem[e], 1)
        self._update(r, w, e, self.cnt[e])

    def dma(self, q, out, in_, r=(), w=(), **kw):
        i = self.drr[q]
        self.drr[q] = (i + 1) % len(self.dq[q])
        need = self._collect(r, w)
        k = (q, i)
        prev = 16 * self.dcnt[q][i]
        if prev and need.get(k, 0) < prev:
            need[k] = prev
        self._emit_waits(q, need)
        ins = self.E[q].dma_start(out=out, in_=in_, **kw)
        self.dcnt[q][i] += 1
        ins.then_inc(self.dq[q][i], 16)
        self._update(r, w, k, 16 * self.dcnt[q][i])

    def _all(self):
        need = {e: c for e, c in self.cnt.items() if c}
        for q, lst in self.dcnt.items():
            for i, c in enumerate(lst):
                if c:
                    need[(q, i)] = 16 * c
        return need

    def barrier(self):
        need = self._all()
        for e in ('pe', 'act', 'dve', 'pool', 'sp'):
            n2 = dict(need)
            self._emit_waits(e, n2)
        self.bufs = {}

    def finish(self):
        self._emit_waits('sp', self._all())


class Arena:
    def __init__(self, ap_f32, words):
        self.ap = ap_f32
        self.words = words
        self.off = 0

    def reset(self):
        self.off = 0

    def f32(self, words, parts=128):
        a = self.ap[0:parts, self.off:self.off + words]
        self.off += words
        assert self.off <= self.words, (self.off, self.words)
        return a

    def bf16(self, elems, parts=128):
        assert elems % 2 == 0
        return self.f32(elems // 2, parts).bitcast(BF16)


def build_program():
    nc = bass.Bass("TRN2", target_bir_lowering=False)
    dt_in = lambda n, s: nc.dram_tensor(n, s, F32, kind="ExternalInput").ap()
    dt_out = lambda n, s: nc.dram_tensor(n, s, F32, kind="ExternalOutput").ap()
    xp = dt_in("xp", [SEQ, DM])
    xsm = dt_in("xsm", [128, DM])
    spool = dt_in("spool", [2, NS, 15, DI])
    sconv = dt_in("sconv", [2, NS, 3, CONV_DIM])
    sssm = dt_in("sssm", [2, NS, 32, 64, 128])
    pool_in_w = dt_in("pool_in_w", [2, DM, 2 * DI])
    pool_mix_w = dt_in("pool_mix_w", [2, 4, 512, 512])
    pool_out_w = dt_in("pool_out_w", [2, DI, DM])
    ssd_in_w = dt_in("ssd_in_w", [2, DM, SSD_IN])
    ssd_out_w = dt_in("ssd_out_w", [2, DI, DM])
    nwc_d = dt_in("nwc", [128, 4 * 8])
    fnw_d = dt_in("fnw", [1, DM])
    pscale_d = dt_in("pscale", [128, 2 * 16])
    convw_d = dt_in("convw", [128, 2 * 24 * 4])
    convb_d = dt_in("convb", [128, 2 * 24])
    dtb_d = dt_in("dtb", [1, 64])
    alog_d = dt_in("alog", [1, 64])
    dsk_d = dt_in("dsk", [1, 64])
    snw_d = dt_in("snw", [128, 2 * 16])

    yp = dt_out("yp", [SEQ, DM])
    ysm = dt_out("ysm", [128, DM])
    poolp = dt_out("poolp", [2, 15, DI])
    pools = dt_out("pools", [2, NS, 15, DI])
    convp = dt_out("convp", [2, 3, CONV_DIM])
    convs = dt_out("convs", [2, NS, 3, CONV_DIM])
    ssmp = dt_out("ssmp", [2, 32, 64, 128])
    ssms = dt_out("ssms", [2, NS, 32, 64, 128])
    xscr = nc.dram_tensor("xscr", [SEQ + 128, DM], F32).ap()

    with ExitStack() as es:
        sb = lambda n, s, d: es.enter_context(nc.sbuf_tensor(n, s, d))
        T = Trk(nc, es)
        WA = sb("WA", [128, 8 * SSD_IN], BF16)
        WB = sb("WB", [128, 16 * DM], BF16)
        ident_f = sb("ident_f", [128, 128], F32)
        ident_b = sb("ident_b", [128, 128], BF16)
        Umat = sb("Umat", [128, 128], F32)
        SLmat = sb("SLmat", [128, 128], F32)
        ones_f = sb("ones_f", [128, 128], F32)
        Us = sb("Us", [128, 128], F32)
        SLs = sb("SLs", [128, 128], F32)
        seqind = sb("seqind", [128, 16], F32)
        icnt = sb("icnt", [128, 4 * 16], F32)
        nwc = sb("nwc_s", [128, 32], F32)
        fnw = sb("fnw_s", [128, DM], F32)
        pscale = sb("pscale_s", [128, 32], F32)
        convw = sb("convw_s", [128, 192], F32)
        convb = sb("convb_s", [128, 48], F32)
        dtb = sb("dtb_s", [128, 64], F32)
        negA = sb("negA_s", [128, 64], F32)
        dsk = sb("dsk_s", [128, 64], F32)
        snw = sb("snw_s", [128, 32], F32)
        AW = 21600
        arena_t = sb("arena", [128, AW], F32)
        A = Arena(arena_t, AW)
        PS = es.enter_context(nc.psum_tensor("PS", [128, 4096], F32))

        st = {'b': 0, 'g': 0}

        def bank():
            b = st['b']
            st['b'] = (b + 1) % 4
            return PS[:, b * 512:(b + 1) * 512], ('ps', b)

        def bgroup(n=4):
            if n == 4:
                return PS[:, 2048:4096], [('ps', 4 + i) for i in range(4)]
            g = st['g']
            st['g'] = (g + 1) % 2
            return PS[:, 2048 + g * 1024:2048 + (g + 1) * 1024], [('ps', 4 + 2 * g + i) for i in range(2)]

        def hbank(i):
            b = 4 + (i % 2)
            return PS[:, b * 512:(b + 1) * 512], ('ps', b)

        T.op('pool', lambda e: e.memset(ident_f[:], 0.0), w=['ident_f'])
        T.op('pool', lambda e: e.affine_select(out=ident_f[:], in_=ident_f[:], pattern=[[-1, 128]],
                                               compare_op=ALU.not_equal, fill=1.0, base=0, channel_multiplier=1),
             r=['ident_f'], w=['ident_f'])
        T.op('dve', lambda e: e.tensor_copy(out=ident_b[:], in_=ident_f[:]), r=['ident_f'], w=['ident_b'])
        T.op('pool', lambda e: e.memset(ones_f[:], 1.0), w=['ones_f'])
        T.op('pool', lambda e: e.memset(Umat[:], 1.0), w=['U'])
        T.op('pool', lambda e: e.affine_select(out=Umat[:], in_=Umat[:], pattern=[[1, 128]], compare_op=ALU.is_ge,
                                               fill=0.0, base=0, channel_multiplier=-1), r=['U'], w=['U'])
        T.op('pool', lambda e: e.memset(SLmat[:], 1.0), w=['SL'])
        T.op('pool', lambda e: e.affine_select(out=SLmat[:], in_=SLmat[:], pattern=[[-1, 128]], compare_op=ALU.is_gt,
                                               fill=0.0, base=0, channel_multiplier=1), r=['SL'], w=['SL'])

        def blockdiag(dst, key, pat_tail):
            T.op('pool', lambda e: e.affine_select(out=dst, in_=dst, pattern=[[-8, 16]] + pat_tail,
                                                   compare_op=ALU.is_ge, fill=0.0, base=0, channel_multiplier=1),
                 r=[key], w=[key])
            T.op('pool', lambda e: e.affine_select(out=dst, in_=dst, pattern=[[8, 16]] + pat_tail,
                                                   compare_op=ALU.is_ge, fill=0.0, base=7, channel_multiplier=-1),
                 r=[key], w=[key])
        T.op('pool', lambda e: e.tensor_copy(out=Us[:], in_=Umat[:]), r=['U'], w=['Us'])
        blockdiag(Us[:].rearrange("p (s t) -> p s t", t=8), 'Us', [[0, 8]])
        T.op('pool', lambda e: e.tensor_copy(out=SLs[:], in_=SLmat[:]), r=['SL'], w=['SLs'])
        blockdiag(SLs[:].rearrange("p (s t) -> p s t", t=8), 'SLs', [[0, 8]])
        T.op('pool', lambda e: e.memset(seqind[:], 1.0), w=['seqind'])
        blockdiag(seqind[:], 'seqind', [])
        icnt3 = icnt[:].rearrange("p (g t) -> p g t", g=4)
        T.op('pool', lambda e: e.iota(icnt3, pattern=[[0, 4], [1, 16]], base=1, channel_multiplier=0,
                                      allow_small_or_imprecise_dtypes=True), w=['icnt'])
        for g in range(4):
            T.op('dve', lambda e, g=g: e.tensor_scalar(out=icnt3[:, g, :], in0=icnt3[:, g, :], scalar1=float(POOL_W[g]),
                                                       scalar2=None, op0=ALU.min), r=['icnt'], w=['icnt'])
        T.op('dve', lambda e: e.reciprocal(out=icnt[:], in_=icnt[:]), r=['icnt'], w=['icnt'])

        T.dma('sp', nwc[:], nwc_d[:, :], w=['nwc'])
        T.dma('sp', fnw[:], fnw_d.partition_broadcast(128), w=['fnw'])
        T.dma('sp', pscale[:], pscale_d[:, :], w=['pscale'])
        T.dma('sp', convw[:], convw_d[:, :], w=['convw'])
        T.dma('sp', convb[:], convb_d[:, :], w=['convb'])
        T.dma('sp', dtb[:], dtb_d.partition_broadcast(128), w=['dtb'])
        T.dma('sp', negA[:], alog_d.partition_broadcast(128), w=['negA'])
        T.dma('sp', dsk[:], dsk_d.partition_broadcast(128), w=['dsk'])
        T.dma('sp', snw[:], snw_d[:, :], w=['snw'])
        T.op('act', lambda e: e.activation(out=negA[:], in_=negA[:], func=AF.Exp), r=['negA'], w=['negA'])
        T.op('dve', lambda e: e.tensor_scalar(out=negA[:], in0=negA[:], scalar1=-1.0, scalar2=None, op0=ALU.mult),
             r=['negA'], w=['negA'])

        def src_rows(L, row0, n):
            if L == 0:
                return (xp[row0:row0 + n, :], []) if row0 < SEQ else (xsm[row0 - SEQ:row0 - SEQ + n, :], [])
            return xscr[row0:row0 + n, :], [('xscr', row0 // 128)]

        def rms_stat(src_ap, src_key, stat, col, n_feat, junk, junk_key):
            T.op('act', lambda e: e.activation(out=junk, in_=src_ap, func=AF.Square, accum_out=stat[:, col:col + 1]),
                 r=[src_key], w=[junk_key, 'stat'])
            T.op('act', lambda e: e.activation(out=stat[:, col + 1:col + 2], in_=stat[:, col:col + 1], func=AF.Sqrt,
                                               scale=1.0 / n_feat, bias=epsb[:, 0:1]), r=['stat', 'epsb'], w=['stat'])
            T.op('dve', lambda e: e.reciprocal(out=stat[:, col:col + 1], in_=stat[:, col + 1:col + 2]),
                 r=['stat'], w=['stat'])

        WAK = [('WA', k) for k in range(24)]
        WBK = [('WB', k) for k in range(16)]

        def load_weights_bf16(dst3, src3, nk, key, k0=0):
            for k in range(nk):
                T.dma('pool', dst3[:, k, :], src3[:, k, :], w=[(key, k0 + k)])

        epsb = sb("epsb", [128, 1], F32)
        T.op('pool', lambda e: e.memset(epsb[:], EPS), w=['epsb'])

        def norm_and_transpose(xblk, xkey, L, h, hT3, hkey, hTkey, stat, col, tok0):
            rms_stat(xblk, xkey, stat, col, DM, h, hkey)
            T.op('dve', lambda e: e.tensor_scalar(out=h, in0=xblk, scalar1=stat[:, col:col + 1], scalar2=None,
                                                  op0=ALU.mult), r=[xkey, 'stat'], w=[hkey])
            pb, pk = bank()
            pbb = pb.bitcast(BF16)
            for k in range(8):
                T.op('pe', lambda e, k=k: e.transpose(pbb[:, k * 128:(k + 1) * 128], h[:, k * 128:(k + 1) * 128],
                                                      ident_b[:]), r=[hkey, 'ident_b'], w=[pk])
            T.op('dve', lambda e: e.tensor_tensor(
                out=hT3[:, :, tok0:tok0 + 128], in0=pbb.rearrange("p (k t) -> p k t", k=8),
                in1=nwc[:, L * 8:(L + 1) * 8].unsqueeze(2).broadcast_to([128, 8, 128]), op=ALU.mult),
                r=[pk, 'nwc'], w=[hTkey])

        def out_proj_and_store(L, yT3, yTkey, tokoff, xblk, xkey, row0, w_out3, blkidx, junk, junk_key, stat):
            for hh in range(2):
                pb, pk = bank()
                for f in range(16):
                    T.op('pe', lambda e, f=f: e.matmul(pb, lhsT=yT3[:, f, tokoff:tokoff + 128],
                                                       rhs=w_out3[:, f, hh * 512:(hh + 1) * 512],
                                                       start=(f == 0), stop=(f == 15)), r=[yTkey] + WBK, w=[pk])
                T.op('dve', lambda e: e.tensor_tensor(out=xblk[:, hh * 512:(hh + 1) * 512],
                                                      in0=xblk[:, hh * 512:(hh + 1) * 512], in1=pb, op=ALU.add),
                     r=[pk, xkey], w=[xkey])
            if L < 3:
                T.dma('sp', xscr[row0:row0 + 128, :], xblk, r=[xkey], w=[('xscr', row0 // 128)])
            else:
                rms_stat(xblk, xkey, stat, 8, DM, junk, junk_key)
                T.op('dve', lambda e: e.scalar_tensor_tensor(out=xblk, in0=xblk, scalar=stat[:, 8:9], in1=fnw[:],
                                                             op0=ALU.mult, op1=ALU.mult),
                     r=[xkey, 'stat', 'fnw'], w=[xkey])
                dst = yp[row0:row0 + 128, :] if row0 < SEQ else ysm[row0 - SEQ:row0 - SEQ + 128, :]
                T.dma('sp', dst, xblk, r=[xkey])

        def pool_layer(j):
            L = 2 * j
            T.barrier()
            A.reset()
            w_in3 = WA[:, 0:8 * 4096].rearrange("p (k n) -> p k n", k=8)
            w_mix3 = WA[:, 8 * 4096:8 * 4096 + 16 * 512].rearrange("p (k n) -> p k n", k=16)
            w_out3 = WB[:].rearrange("p (k n) -> p k n", k=16)
            load_weights_bf16(w_in3, pool_in_w[j].rearrange("(k p) n -> p k n", p=128), 8, 'WA')
            load_weights_bf16(w_mix3, pool_mix_w[j].rearrange("g (c p) n -> p (g c) n", p=128), 16, 'WA', k0=8)
            load_weights_bf16(w_out3, pool_out_w[j].rearrange("(k p) n -> p k n", p=128), 16, 'WB')

            xts = [A.f32(2048).rearrange("p (b n) -> p b n", b=2) for _ in range(2)]
            h = A.bf16(1024)
            hTs = [A.bf16(8 * 256).rearrange("p (k t) -> p k t", k=8) for _ in range(2)]
            ues = [A.f32(1472), A.f32(1472)]
            sa = A.f32(1472)
            sbuf2 = A.f32(1472)
            pgs = [A.bf16(4 * 256), A.bf16(4 * 256)]
            szs = [A.f32(256) for _ in range(4)]
            ybf = A.bf16(16 * 256)
            carry = A.f32(240).rearrange("p (c r) -> p c r", c=16)
            strows = A.f32(1024, parts=120).rearrange("p (a n) -> p a n", a=2)
            tokrows = A.f32(2048)
            stat = A.f32(64)

            T.op('dve', lambda e: e.memset(carry, 0.0), w=['carry'])
            szi = [0]

            class Tile:
                pass

            def mk(ti, row0, Tn, sample, first, last_prompt):
                t = Tile()
                t.ti, t.row0, t.Tn, t.sample, t.first, t.last = ti, row0, Tn, sample, first, last_prompt
                t.nb = Tn // 128
                t.xt = xts[ti % 2]
                t.xk = lambda b: ('xt', ti % 2, b)
                t.hT3 = hTs[ti % 2]
                t.hTk = 'hT%d' % (ti % 2)
                t.hTv = t.hT3[:, :, 0:Tn]
                t.y3 = ybf[:, 0:16 * Tn].rearrange("p (f t) -> p f t", f=16)
                return t

            def front(t):
                for b in range(t.nb):
                    src, rk = src_rows(L, t.row0 + 128 * b, 128)
                    T.dma('sp', t.xt[:, b, :], src, r=rk, w=[t.xk(b)])
                for b in range(t.nb):
                    norm_and_transpose(t.xt[:, b, :], t.xk(b), L, h, t.hT3, 'h', t.hTk, stat, 16 * (t.ti % 2) + 2 * b,
                                       128 * b)

            def views(t, g):
                ue = ues[g % 2]
                Tn = t.Tn
                if not t.sample:
                    EW = 15 + Tn
                    r3 = lambda buf: buf[:, 0:4 * EW].rearrange("p (c t) -> p c t", c=4)
                else:
                    r3 = lambda buf: buf[:, 0:4 * 16 * 23].rearrange("p (c t) -> p c t", c=4)
                return r3(ue), r3(sa), r3(sbuf2), 'ue%d' % (g % 2)

            def U(t, g):
                Tn = t.Tn
                ue3, sa3, sb3, uek = views(t, g)
                if not t.sample:
                    T.op('act', lambda e: e.activation(out=ue3[:, :, 0:15], in_=carry[:, 4 * g:4 * g + 4, :],
                                                       func=AF.Copy), r=['carry'], w=[uek])
                else:
                    ue4 = ue3.rearrange("p c (s t) -> p c s t", s=16)
                    for a in range(2):
                        T.dma('sp', strows[:, a, :],
                              spool[j, 8 * a:8 * a + 8, :, g * 512:(g + 1) * 512].rearrange("s r n -> (s r) n"),
                              w=['strows'])
                    for cc in range(4):
                        pb, pk = bank()
                        for a in range(2):
                            T.op('pe', lambda e, a=a, cc=cc: e.transpose(
                                pb[:, a * 120:(a + 1) * 120], strows[:, a, cc * 128:(cc + 1) * 128],
                                ident_f[0:120, 0:120]), r=['strows', 'ident_f'], w=[pk])
                        T.op('act', lambda e, cc=cc: e.activation(
                            out=ue4[:, cc, :, 0:15], in_=pb[:, 0:240].rearrange("p (s r) -> p s r", r=15),
                            func=AF.Copy), r=[pk], w=[uek])
                for cc in range(4):
                    c = 4 * g + cc
                    pb, pk = bank()
                    for k in range(8):
                        T.op('pe', lambda e, k=k: e.matmul(pb[:, 0:Tn], lhsT=w_in3[:, k, c * 128:(c + 1) * 128],
                                                           rhs=t.hTv[:, k, :], start=(k == 0), stop=(k == 7)),
                             r=WAK + [t.hTk], w=[pk])
                    if not t.sample:
                        T.op('act', lambda e, cc=cc: e.activation(out=ue3[:, cc, 15:15 + Tn], in_=pb[:, 0:Tn],
                                                                  func=AF.Copy), r=[pk], w=[uek])
                    else:
                        T.op('act', lambda e, cc=cc: e.activation(
                            out=ue4[:, cc, :, 15:23], in_=pb[:, 0:128].rearrange("p (s t) -> p s t", t=8),
                            func=AF.Copy), r=[pk], w=[uek])
                        pb2, pk2 = bank()
                        T.op('act', lambda e: e.activation(out=szs[0][:, 0:128], in_=pb[:, 0:128], func=AF.Copy),
                             r=[pk], w=['sz0'])
                        T.op('pe', lambda e: e.transpose(pb2[:, 0:128], szs[0][:, 0:128], ident_f[:]),
                             r=['sz0', 'ident_f'], w=[pk2])
                        T.op('act', lambda e, c=c: e.activation(out=tokrows[:, c * 128:(c + 1) * 128],
                                                                in_=pb2[:, 0:128], func=AF.Copy),
                             r=[pk2], w=['tokrows'])
                if not t.sample:
                    T.op('act', lambda e: e.activation(out=carry[:, 4 * g:4 * g + 4, :],
                                                       in_=ue3[:, :, Tn:Tn + 15], func=AF.Copy), r=[uek], w=['carry'])

            def Pst(t, g):
                Tn = t.Tn
                wdw = POOL_W[g]
                ue3, sa3, sb3, uek = views(t, g)
                pg = pgs[g % 2]
                pgk = 'pg%d' % (g % 2)
                if not t.sample:
                    def sl(buf3, lo, hi):
                        return buf3[:, :, lo:hi]
                    Wd = 15 + Tn
                else:
                    def sl(buf3, lo, hi):
                        return buf3.rearrange("p c (s t) -> p c s t", s=16)[:, :, :, lo:hi]
                    Wd = 23
                cur, curkey = ue3, uek
                tmp = [(sa3, 'sa'), (sb3, 'sb')]
                step = 1
                ti = 0
                eng_rr = ['dve', 'dve']
                while step < wdw:
                    dst, dkey = tmp[ti % 2]
                    lo = 2 * step - 1
                    T.op(eng_rr[ti % 2], lambda e, cur=cur, dst=dst, step=step, lo=lo: e.tensor_tensor(
                        out=sl(dst, lo, Wd), in0=sl(cur, lo, Wd), in1=sl(cur, lo - step, Wd - step), op=ALU.add),
                        r=[curkey], w=[dkey])
                    cur, curkey = dst, dkey
                    step *= 2
                    ti += 1
                if not t.sample:
                    pg3 = pg[:, 0:4 * Tn].rearrange("p (c t) -> p c t", c=4)
                    T.op('dve', lambda e, cur=cur: e.scalar_tensor_tensor(
                        out=pg3, in0=cur[:, :, 15:15 + Tn], scalar=1.0 / wdw, in1=ue3[:, :, 15:15 + Tn],
                        op0=ALU.mult, op1=ALU.subtract), r=[curkey, uek], w=[pgk])
                    if t.first:
                        dst, dkey = tmp[ti % 2]
                        T.op('dve', lambda e, cur=cur, dst=dst: e.tensor_tensor(
                            out=dst[:, :, 15:31], in0=cur[:, :, 15:31],
                            in1=icnt3[:, g, :].unsqueeze(1).broadcast_to([128, 4, 16]), op=ALU.mult),
                            r=[curkey, 'icnt'], w=[dkey])
                        T.op('dve', lambda e, dst=dst: e.tensor_tensor(
                            out=pg3[:, :, 0:16], in0=dst[:, :, 15:31], in1=ue3[:, :, 15:31], op=ALU.subtract),
                            r=[dkey, uek, pgk], w=[pgk])
                else:
                    pg2 = pg[:, 0:4 * 128].rearrange("p (cs t) -> p cs t", t=8)
                    c2 = cur.rearrange("p c (s t) -> p (c s) t", s=16)[:, :, 15:23]
                    u2 = ue3.rearrange("p c (s t) -> p (c s) t", s=16)[:, :, 15:23]
                    T.op('dve', lambda e: e.scalar_tensor_tensor(
                        out=pg2, in0=c2, scalar=1.0 / wdw, in1=u2, op0=ALU.mult, op1=ALU.subtract),
                        r=[curkey, uek], w=[pgk])

            def Zp(t, g):
                Tn = t.Tn
                for dd in range(4):
                    d = 4 * g + dd
                    pz, pzk = bank()
                    for k in range(8):
                        T.op('pe', lambda e, k=k: e.matmul(pz[:, 0:Tn], lhsT=w_in3[:, k, DI + d * 128:DI + (d + 1) * 128],
                                                           rhs=t.hTv[:, k, :], start=(k == 0), stop=(k == 7)),
                             r=WAK + [t.hTk], w=[pzk])
                    T.op('act', lambda e, dd=dd: e.activation(out=szs[dd][:, 0:Tn], in_=pz[:, 0:Tn], func=AF.Silu),
                         r=[pzk], w=['sz%d' % dd])

            def M(t, g):
                Tn = t.Tn
                pg = pgs[g % 2]
                pgk = 'pg%d' % (g % 2)
                pg3 = pg[:, 0:4 * Tn].rearrange("p (c t) -> p c t", c=4)
                for dd in range(4):
                    d = 4 * g + dd
                    pm, pmk = PS[:, (4 + dd) * 512:(5 + dd) * 512], ('ps', 4 + dd)
                    for cc in range(4):
                        T.op('pe', lambda e, cc=cc: e.matmul(pm[:, 0:Tn], lhsT=w_mix3[:, 4 * g + cc, dd * 128:(dd + 1) * 128],
                                                             rhs=pg3[:, cc, :], start=(cc == 0), stop=(cc == 3)),
                             r=WAK + [pgk], w=[pmk])
                    T.op('dve', lambda e, dd=dd, d=d: e.scalar_tensor_tensor(
                        out=t.y3[:, d, :], in0=pm[:, 0:Tn], scalar=pscale[:, j * 16 + d:j * 16 + d + 1],
                        in1=szs[dd][:, 0:Tn], op0=ALU.mult, op1=ALU.mult), r=[pmk, 'sz%d' % dd, 'pscale'], w=['y'])

            def back(t):
                for b in range(t.nb):
                    out_proj_and_store(L, t.y3, 'y', 128 * b, t.xt[:, b, :], t.xk(b), t.row0 + 128 * b, w_out3, b,
                                       h, 'h', stat)
                if t.last:
                    pgp, pgk_ = bgroup(4)
                    for c in range(16):
                        T.op('pe', lambda e, c=c: e.transpose(pgp[0:15, c * 128:(c + 1) * 128], carry[:, c, :],
                                                              ident_f[:]), r=['carry', 'ident_f'], w=pgk_)
                    ystage = ybf.bitcast(F32)
                    T.op('act', lambda e: e.activation(out=ystage[0:15, :], in_=pgp[0:15, :], func=AF.Copy),
                         r=pgk_, w=['y'])
                    T.dma('sp', poolp[j], ystage[0:15, :], r=['y'])
                if t.sample:
                    for s in range(NS):
                        T.dma('sp', pools[j, s, 7:15, :], tokrows[8 * s:8 * s + 8, :], r=['tokrows'])
                    T.dma('sp', pools[j, :, 0:7, :], spool[j, :, 8:15, :])

            tiles = [mk(ti, 256 * ti, 256, False, ti == 0, ti == 7) for ti in range(PLAN.get('ptiles', 8))]
            if PLAN.get('psample', True):
                tiles.append(mk(len(tiles), SEQ, 128, True, False, False))
            if tiles:
                front(tiles[0])
                U(tiles[0], 0)
            for i, t in enumerate(tiles):
                for g in range(4):
                    if g + 1 < 4:
                        U(t, g + 1)
                    Zp(t, g)
                    Pst(t, g)
                    M(t, g)
                if i + 1 < len(tiles):
                    front(tiles[i + 1])
                    U(tiles[i + 1], 0)
                back(t)

        def ssd_layer(j):
            L = 2 * j + 1
            T.barrier()
            A.reset()
            w_in3 = WA[:].rearrange("p (k n) -> p k n", k=8)
            w_out3 = WB[:].rearrange("p (k n) -> p k n", k=16)
            load_weights_bf16(w_in3, ssd_in_w[j].rearrange("(k p) n -> p k n", p=128), 8, 'WA')
            load_weights_bf16(w_out3, ssd_out_w[j].rearrange("(k p) n -> p k n", p=128), 16, 'WB')
            cw = convw[:, j * 96:(j + 1) * 96].rearrange("p (c k) -> p c k", k=4)
            cb = convb[:, j * 24:(j + 1) * 24]
            dtb_j = dtb[:, j * 32:(j + 1) * 32]
            negA_j = negA[:, j * 32:(j + 1) * 32]
            dsk_j = dsk[:, j * 32:(j + 1) * 32]
            snw_j = snw[:, j * 16:(j + 1) * 16]

            xts = [A.f32(1024), A.f32(1024)]
            h = A.bf16(1024)
            hTs = [A.bf16(8 * 128).rearrange("p (k t) -> p k t", k=8) for _ in range(2)]
            xe4s = [A.f32(704), A.f32(704)]
            acc4s = [A.f32(512), A.f32(512)]
            BC = A.bf16(8 * 128).rearrange("p (c t) -> p c t", c=8)
            h0a_off = A.off
            tmpA = [A.f32(512), A.f32(512)]
            lq_off = A.off
            Lq = [A.f32(512), A.f32(512)]
            stage1k = arena_t[:, lq_off:lq_off + 1024]
            rawc4 = Lq[0]
            scr4 = Lq[1][0:48, :]
            xs_tok = A.f32(2048)
            bfA = A.bf16(2048)
            bfB = A.bf16(2048)
            Btok = A.bf16(4 * 128).rearrange("p (g n) -> p g n", g=4)
            CBTm = A.f32(512).rearrange("p (g i) -> p g i", g=4)
            WTq = [A.bf16(512), A.bf16(512)]
            ybuf = A.f32(2048)
            ST_all = A.f32(3072)
            ST = ST_all[:, 0:2048]
            STb = ST_all[:, 2048:3072].bitcast(BF16)
            rawT = ST_all
            smalls = [A.f32(256), A.f32(256)]
            ccarry = A.f32(72).rearrange("p (c r) -> p c r", c=24)
            CTm = WTq
            Bm = [A.bf16(512), A.bf16(512)]
            dcol = A.f32(256).rearrange("p (a s) -> p a s", a=16)
            stat = A.f32(64)
            if PLAN.get('verbose'):
                print('ssd arena words used', A.off, 'of', A.words)
            h0s = [arena_t[:, h0a_off:h0a_off + 2048], xs_tok]
            h0keys = ['h0a', 'xs_tok']

            T.op('dve', lambda e: e.memset(ccarry, 0.0), w=['ccarry'])

            def bc64(v32):
                return v32.unsqueeze(2).broadcast_to([128, 32, 64])

            def v3(ap2048):
                return ap2048.rearrange("p (h d) -> p h d", h=32)

            def chunk(ci, row0, sample):
                first = (ci == 0)
                par = ci % 2
                K = lambda n: n + str(par)
                xt = xts[par]
                hT3 = hTs[par]
                small = smalls[par]
                dtv = small[:, 0:32]
                av = small[:, 32:64]
                ex = small[:, 64:160]
                dte = small[:, 160:192]
                tdt = small[:, 192:224]
                ecum = ex[:, 0:32]
                eaft = ex[:, 32:64]
                dec_bc = ex[:, 64:96]
                src, rk = src_rows(L, row0, 128)
                T.dma('sp', xt, src, r=rk, w=[K('xt')])
                norm_and_transpose(xt, K('xt'), L, h, hT3, 'h', K('hT'), stat, 16 * par, 0)
                pb, pk = bank()
                for k in range(8):
                    T.op('pe', lambda e, k=k: e.matmul(pb[:, 0:32], lhsT=hT3[:, k, :], rhs=w_in3[:, k, 5120:5152],
                                                       start=(k == 0), stop=(k == 7)), r=[K('hT')] + WAK, w=[pk])
                T.op('dve', lambda e: e.tensor_tensor(out=tdt, in0=pb[:, 0:32], in1=dtb_j, op=ALU.add),
                     r=[pk, 'dtb'], w=[K('tdt')])
                T.op('act', lambda e: e.activation(out=tdt, in_=tdt, func=AF.Exp), r=[K('tdt')], w=[K('tdt')])
                T.op('act', lambda e: e.activation(out=dtv, in_=tdt, func=AF.Ln, bias=1.0), r=[K('tdt')], w=[K('dt')])
                T.op('dve', lambda e: e.tensor_tensor(out=av, in0=dtv, in1=negA_j, op=ALU.mult),
                     r=[K('dt'), 'negA'], w=[K('a')])
                pb, pk = bank()
                T.op('pe', lambda e: e.matmul(pb[:, 0:32], lhsT=(Us if sample else Umat)[:], rhs=av, start=True, stop=True),
                     r=[K('a'), 'U', 'Us'], w=[pk])
                T.op('pe', lambda e: e.matmul(pb[:, 32:64], lhsT=(SLs if sample else SLmat)[:], rhs=av, start=True,
                                              stop=True), r=[K('a'), 'SL', 'SLs'], w=[pk])
                T.op('pe', lambda e: e.matmul(pb[:, 64:96], lhsT=ones_f[:], rhs=av, start=True, stop=True),
                     r=[K('a'), 'ones_f'], w=[pk])
                T.op('act', lambda e: e.activation(out=ex, in_=pb[:, 0:96], func=AF.Exp), r=[pk], w=[K('ex')])
                T.op('dve', lambda e: e.tensor_tensor(out=dte, in0=eaft, in1=dtv, op=ALU.mult), r=[K('ex'), K('dt')], w=[K('dte')])
                yield 'front'
                def xbc_front(cg):
                    xe4 = xe4s[cg % 2]
                    xek = 'xe%d' % (cg % 2)
                    pb, pk = bank()
                    for cc in range(4):
                        c = 4 * cg + cc
                        for k in range(8):
                            T.op('pe', lambda e, k=k, c=c, cc=cc: e.matmul(
                                pb[:, cc * 128:(cc + 1) * 128], lhsT=w_in3[:, k, DI + c * 128:DI + (c + 1) * 128],
                                rhs=hT3[:, k, :], start=(k == 0), stop=(k == 7)), r=WAK + [K('hT')], w=[pk])
                    if not sample:
                        xe3 = xe4[:, 0:4 * 131].rearrange("p (c t) -> p c t", c=4)
                        T.op('act', lambda e: e.activation(out=xe3[:, :, 0:3], in_=ccarry[:, 4 * cg:4 * cg + 4, :],
                                                           func=AF.Copy), r=['ccarry'], w=[xek])
                        T.op('act', lambda e: e.activation(out=xe3[:, :, 3:131],
                                                           in_=pb.rearrange("p (c t) -> p c t", c=4), func=AF.Copy),
                             r=[pk], w=[xek])
                        T.op('act', lambda e: e.activation(out=ccarry[:, 4 * cg:4 * cg + 4, :], in_=xe3[:, :, 128:131],
                                                           func=AF.Copy), r=[xek], w=['ccarry'])
                    else:
                        xe4v = xe4.rearrange("p (c s t) -> p c s t", c=4, s=16)
                        T.dma('sp', scr4, sconv[j, :, :, cg * 512:(cg + 1) * 512].rearrange("s r n -> (s r) n"),
                              w=['Lq1'])
                        pb2, pk2 = bank()
                        for cc in range(4):
                            T.op('pe', lambda e, cc=cc: e.transpose(pb2[:, cc * 48:(cc + 1) * 48],
                                                                    scr4[:, cc * 128:(cc + 1) * 128], ident_f[0:48, 0:48]),
                                 r=['Lq1', 'ident_f'], w=[pk2])
                        T.op('act', lambda e: e.activation(
                            out=xe4v[:, :, :, 0:3], in_=pb2[:, 0:192].rearrange("p (c s r) -> p c s r", c=4, s=16),
                            func=AF.Copy), r=[pk2], w=[xek])
                        T.op('act', lambda e: e.activation(out=rawc4, in_=pb, func=AF.Copy), r=[pk], w=['Lq0'])
                        T.op('pool', lambda e: e.tensor_copy(
                            out=xe4v[:, :, :, 3:11], in_=rawc4.rearrange("p (c s t) -> p c s t", c=4, s=16)),
                            r=['Lq0'], w=[xek])
                        pb3, pk3 = bank()
                        for cc in range(4):
                            T.op('pe', lambda e, cc=cc: e.transpose(pb3[:, cc * 128:(cc + 1) * 128],
                                                                    rawc4[:, cc * 128:(cc + 1) * 128], ident_f[:]),
                                 r=['Lq0', 'ident_f'], w=[pk3])
                        T.op('act', lambda e: e.activation(out=rawT[:, cg * 512:(cg + 1) * 512], in_=pb3, func=AF.Copy),
                             r=[pk3], w=['rawT'])

                def xbc_back(cg):
                    xe4 = xe4s[cg % 2]
                    xek = 'xe%d' % (cg % 2)
                    acc4 = acc4s[cg % 2]
                    acck = 'acc%d' % (cg % 2)
                    if not sample:
                        xe3 = xe4[:, 0:4 * 131].rearrange("p (c t) -> p c t", c=4)
                        tap = lambda cc, kk: xe3[:, cc, kk:kk + 128]
                        accv = lambda cc: acc4[:, cc * 128:(cc + 1) * 128]
                    else:
                        xe4v = xe4.rearrange("p (c s t) -> p c s t", c=4, s=16)
                        tap = lambda cc, kk: xe4v[:, cc, :, kk:kk + 8]
                        accv = lambda cc: acc4[:, cc * 128:(cc + 1) * 128].rearrange("p (s t) -> p s t", t=8)
                    for kk in range(4):
                        for cc in range(4):
                            c = 4 * cg + cc
                            if kk == 0:
                                T.op('dve', lambda e, cc=cc, c=c: e.tensor_scalar(
                                    out=accv(cc), in0=tap(cc, 0), scalar1=cw[:, c, 0:1], scalar2=None, op0=ALU.mult),
                                    r=[xek, 'convw'], w=[(acck, cc)])
                            else:
                                T.op('dve', lambda e, cc=cc, c=c, kk=kk: e.scalar_tensor_tensor(
                                    out=accv(cc), in0=tap(cc, kk), scalar=cw[:, c, kk:kk + 1], in1=accv(cc),
                                    op0=ALU.mult, op1=ALU.add), r=[xek, 'convw', (acck, cc)], w=[(acck, cc)])
                    for cc in range(4):
                        c = 4 * cg + cc
                        a1 = acc4[:, cc * 128:(cc + 1) * 128]
                        if c < 16:
                            T.op('act', lambda e, c=c, a1=a1: e.activation(out=a1, in_=a1, func=AF.Silu, bias=cb[:, c:c + 1]),
                                 r=[(acck, cc), 'convb'], w=[(acck, cc)])
                        else:
                            T.op('act', lambda e, c=c, a1=a1: e.activation(out=BC[:, c - 16, :], in_=a1, func=AF.Silu,
                                                                           bias=cb[:, c:c + 1]),
                                 r=[(acck, cc), 'convb'], w=['BC'])
                    if cg < 4:
                        xg, xgk = hbank(cg)
                        for cc in range(4):
                            T.op('pe', lambda e, cc=cc: e.transpose(xg[:, cc * 128:(cc + 1) * 128],
                                                                    acc4[:, cc * 128:(cc + 1) * 128], ident_f[:]),
                                 r=[(acck, cc), 'ident_f'], w=[xgk])
                        T.op('act', lambda e: e.activation(out=xs_tok[:, cg * 512:(cg + 1) * 512], in_=xg, func=AF.Copy),
                             r=[xgk], w=['xs_tok'])

                xbc_front(0)
                for cg in range(6):
                    if cg + 1 < 6:
                        xbc_front(cg + 1)
                    xbc_back(cg)
                    yield 'xbc'
                pb, pk = bank()
                pbb = pb.bitcast(BF16)
                for g in range(4):
                    T.op('pe', lambda e, g=g: e.transpose(pbb[:, g * 128:(g + 1) * 128], BC[:, g, :], ident_b[:]),
                         r=['BC', 'ident_b'], w=[pk])
                T.op('act', lambda e: e.activation(out=Btok, in_=pbb[:, 0:512].rearrange("p (g n) -> p g n", g=4),
                                                   func=AF.Copy), r=[pk], w=['Btok'])
                pb, pk = bank()
                for g in range(4):
                    T.op('pe', lambda e, g=g: e.matmul(pb[:, g * 128:(g + 1) * 128], lhsT=BC[:, g, :], rhs=BC[:, 4 + g, :],
                                                       start=True, stop=True), r=['BC'], w=[pk])
                msk = Us if sample else Umat
                T.op('dve', lambda e: e.tensor_tensor(out=CBTm, in0=pb.rearrange("p (g i) -> p g i", g=4),
                                                      in1=msk[:].unsqueeze(1).broadcast_to([128, 4, 128]), op=ALU.mult),
                     r=[pk, 'U', 'Us'], w=['CBTm'])
                if sample:
                    arep = ybuf
                    T.op('dve', lambda e: e.tensor_copy(out=v3(arep), in_=bc64(av)), r=[K('a')], w=['y'])
                    pb, pk = bank()
                    for hp in range(16):
                        T.op('pe', lambda e, hp=hp: e.matmul(pb[:, hp * 16:(hp + 1) * 16],
                                                             lhsT=arep[:, hp * 128:(hp + 1) * 128], rhs=seqind[:],
                                                             start=True, stop=True), r=['y', 'seqind'], w=[pk])
                    T.op('act', lambda e: e.activation(out=dcol, in_=pb[:, 0:256].rearrange("p (a s) -> p a s", a=16),
                                                       func=AF.Exp), r=[pk], w=['dcol'])
                T.op('dve', lambda e: e.tensor_tensor(out=v3(bfA), in0=v3(xs_tok), in1=bc64(dtv), op=ALU.mult),
                     r=['xs_tok', K('dt')], w=['bfA'])
                T.op('dve', lambda e: e.tensor_tensor(out=v3(ybuf), in0=v3(xs_tok), in1=bc64(dsk_j), op=ALU.mult),
                     r=['xs_tok', 'dsk'], w=['y'])
                T.op('dve', lambda e: e.tensor_tensor(out=v3(bfB), in0=v3(xs_tok), in1=bc64(dte), op=ALU.mult),
                     r=['xs_tok', K('dte')], w=['bfB'])
                deferred = []
                if not sample:
                    if not first:
                        for g in range(4):
                            def yoff_unit(g=g):
                                pb, pk = bank()
                                T.op('pe', lambda e: e.matmul(pb, lhsT=BC[:, 4 + g, :], rhs=STb[:, g * 512:(g + 1) * 512],
                                                              start=True, stop=True), r=['BC', 'STb'], w=[pk])
                                tq = xs_tok[:, g * 512:(g + 1) * 512]
                                T.op('dve', lambda e: e.tensor_tensor(
                                    out=tq.rearrange("p (h d) -> p h d", h=8), in0=pb.rearrange("p (h d) -> p h d", h=8),
                                    in1=ecum[:, 8 * g:8 * g + 8].unsqueeze(2).broadcast_to([128, 8, 64]), op=ALU.mult),
                                    r=[pk, K('ex')], w=['xs_tok'])
                                T.op('dve', lambda e: e.tensor_tensor(
                                    out=ybuf[:, g * 512:(g + 1) * 512], in0=ybuf[:, g * 512:(g + 1) * 512], in1=tq,
                                    op=ALU.add), r=['xs_tok', 'y'], w=['y'])
                            deferred.append(yoff_unit)
                    for g in range(4):
                        def cs_unit(g=g):
                            pb, pk = bank()
                            T.op('pe', lambda e: e.matmul(pb, lhsT=Btok[:, g, :], rhs=bfB[:, g * 512:(g + 1) * 512],
                                                          start=True, stop=True), r=['Btok', 'bfB'] + (['STb'] if not first else []),
                                 w=[pk])
                            sg = ST[:, g * 512:(g + 1) * 512]
                            if first:
                                T.op('act', lambda e: e.activation(out=sg, in_=pb, func=AF.Copy), r=[pk], w=[('ST', g)])
                            else:
                                T.op('dve', lambda e: e.tensor_tensor(
                                    out=sg.rearrange("p (h d) -> p h d", h=8), in0=sg.rearrange("p (h d) -> p h d", h=8),
                                    in1=dec_bc[:, 8 * g:8 * g + 8].unsqueeze(2).broadcast_to([128, 8, 64]), op=ALU.mult),
                                    r=[('ST', g), K('ex')], w=[('ST', g)])
                                T.op('dve', lambda e: e.tensor_tensor(out=sg, in0=sg, in1=pb, op=ALU.add),
                                     r=[pk, ('ST', g)], w=[('ST', g)])
                        deferred.append(cs_unit)
                pgd, pgdk = bgroup(4)
                segp = {}

                def seg_front(q):
                    ax = tmpA[q % 2]
                    axk = 'tmpA%d' % (q % 2)
                    ax3 = ax.rearrange("p (h i) -> p h i", h=4)
                    T.op('pool', lambda e: e.tensor_tensor(
                        out=ax3, in0=av[:, 4 * q:4 * q + 4].unsqueeze(2).broadcast_to([128, 4, 128]),
                        in1=Umat[:].unsqueeze(1).broadcast_to([128, 4, 128]), op=ALU.mult), r=[K('a'), 'U'], w=[axk])
                    pb, pk = bank()
                    T.op('pe', lambda e: e.matmul(pb, lhsT=SLmat[:], rhs=ax, start=True, stop=True),
                         r=[axk, 'SL'], w=[pk])
                    segp[q] = (pb, pk)

                def seg_back(q):
                    pb, pk = segp[q]
                    lq = Lq[q % 2]
                    lqk = 'Lq%d' % (q % 2)
                    wt = WTq[q % 2]
                    wtk = 'WT%d' % (q % 2)
                    T.op('act', lambda e: e.activation(out=lq, in_=pb, func=AF.Exp), r=[pk], w=[lqk])
                    T.op('dve', lambda e: e.tensor_tensor(
                        out=wt.rearrange("p (h i) -> p h i", h=4), in0=lq.rearrange("p (h i) -> p h i", h=4),
                        in1=CBTm[:, q // 2, :].unsqueeze(1).broadcast_to([128, 4, 128]), op=ALU.mult),
                        r=[lqk, 'CBTm'], w=[wtk])
                    for hh in range(4):
                        hd = 4 * q + hh
                        T.op('pe', lambda e, hh=hh, hd=hd: e.matmul(pgd[:, hd * 64:(hd + 1) * 64],
                                                                    lhsT=wt[:, hh * 128:(hh + 1) * 128],
                                                                    rhs=bfA[:, hd * 64:(hd + 1) * 64], start=True, stop=True),
                             r=[wtk, 'bfA'], w=[pgdk[hd // 8]])

                seg_front(0)
                for q in range(8):
                    if q + 1 < 8 and q != 3:
                        seg_front(q + 1)
                    seg_back(q)
                    if q < len(deferred):
                        deferred[q]()
                    if q == 3:
                        yield 'midA'
                        seg_front(4)
                for fn in deferred[8:]:
                    fn()
                T.op('dve', lambda e: e.tensor_tensor(out=ybuf, in0=ybuf, in1=pgd, op=ALU.add),
                     r=pgdk + ['y'], w=['y'])
                STK = [('ST', g) for g in range(4)]
                if not sample:
                    if ci < 15:
                        T.op('act', lambda e: e.activation(out=STb, in_=ST, func=AF.Copy), r=STK, w=['STb'])
                    else:
                        pgt, pgtk = bgroup(4)
                        for hp in range(16):
                            T.op('pe', lambda e, hp=hp: e.transpose(pgt[:, hp * 128:(hp + 1) * 128],
                                                                    ST[:, hp * 128:(hp + 1) * 128], ident_f[:]),
                                 r=STK + ['ident_f'], w=[pgtk[hp // 4]])
                        T.op('act', lambda e: e.activation(out=xs_tok, in_=pgt, func=AF.Copy), r=pgtk,
                             w=['xs_tok'])
                        T.dma('sp', ssmp[j].rearrange("(hp two) p n -> (two p) hp n", two=2),
                              xs_tok.rearrange("p (a n) -> p a n", a=16), r=['xs_tok'])
                        for third in range(3):
                            pgv, pgvk = bgroup(2)
                            for cc in range(8):
                                c = third * 8 + cc
                                T.op('pe', lambda e, c=c, cc=cc: e.transpose(pgv[0:3, cc * 128:(cc + 1) * 128],
                                                                             ccarry[:, c, :], ident_f[:]),
                                     r=['ccarry', 'ident_f'], w=[pgvk[cc // 4]])
                            T.op('act', lambda e: e.activation(out=stage1k[0:3, :], in_=pgv[0:3, :],
                                                               func=AF.Copy), r=pgvk, w=['Lq0', 'Lq1'])
                            T.dma('sp', convp[j, :, third * 1024:(third + 1) * 1024], stage1k[0:3, :], r=['Lq0', 'Lq1'])
                else:
                    T.barrier()
                    if PLAN.get('dump'):
                        T.dma('pool', yp[0:128, :], bfB[:, 0:1024])
                        T.dma('pool', yp[128:256, :], bfB[:, 1024:2048])
                        T.dma('sp', yp[256:384, 0:224], small[:, 0:224])
                        T.dma('sp', yp[384:512, :], xs_tok[:, 0:1024])
                        T.dma('sp', yp[512:640, :], xs_tok[:, 1024:2048])
                        T.barrier()
                    pgo, pgok = PS[:, 0:2048], [('ps', i) for i in range(4)]
                    for s in range(NS):
                        h0 = h0s[s % 2]
                        h0k = 'h0_%d' % (s % 2)
                        h03 = h0.rearrange("p (a n) -> p a n", a=16)
                        if s == 0:
                            T.dma('sp', h03, sssm[j, 0].rearrange("(hp two) p n -> (two p) hp n", two=2), w=[h0k])
                        if s + 1 < NS:
                            T.dma('sp', h0s[(s + 1) % 2].rearrange("p (a n) -> p a n", a=16),
                                  sssm[j, s + 1].rearrange("(hp two) p n -> (two p) hp n", two=2),
                                  w=['h0_%d' % ((s + 1) % 2)])
                        ctm = CTm[s % 2]
                        ctmk = 'WT%d' % (s % 2)
                        bm = Bm[s % 2]
                        bmk = 'Bm%d' % (s % 2)
                        ctm3 = ctm.rearrange("p (g t) -> p g t", g=4)
                        T.op('pool', lambda e, s=s: e.affine_select(
                            out=ctm3, in_=BC[:, 4:8, :], pattern=[[0, 4], [1, 128]], compare_op=ALU.is_ge, fill=0.0,
                            base=-8 * s, channel_multiplier=0), r=['BC'], w=[ctmk])
                        T.op('pool', lambda e, s=s: e.affine_select(
                            out=ctm3, in_=ctm3, pattern=[[0, 4], [-1, 128]], compare_op=ALU.is_ge, fill=0.0,
                            base=8 * s + 7, channel_multiplier=0), r=[ctmk], w=[ctmk])
                        T.op('dve', lambda e, s=s: e.tensor_scalar(
                            out=bm, in0=Btok.rearrange("p g n -> p (g n)"), scalar1=seqind[:, s:s + 1], scalar2=None,
                            op0=ALU.mult), r=['Btok', 'seqind'], w=[bmk])
                        for half in range(2):
                            pt = PS[:, 2048 + 0:2048 + 1024]
                            ptk = [('ps', 4), ('ps', 5)]
                            for a in range(8):
                                hp = half * 8 + a
                                T.op('pe', lambda e, a=a, hp=hp: e.transpose(pt[:, a * 128:(a + 1) * 128], h03[:, hp, :],
                                                                             ident_f[:]),
                                     r=[h0k, 'ident_f'], w=[ptk[a // 4]])
                            T.op('act', lambda e, half=half: e.activation(out=bfA[:, half * 1024:(half + 1) * 1024], in_=pt,
                                                                          func=AF.Copy), r=ptk, w=['bfA'])
                        for g in range(4):
                            T.op('pe', lambda e, g=g, s=s: e.matmul(pgo[:, g * 512:(g + 1) * 512],
                                                                    lhsT=ctm[:, g * 128:(g + 1) * 128],
                                                                    rhs=bfA[:, g * 512:(g + 1) * 512],
                                                                    start=(s == 0), stop=(s == NS - 1)),
                                 r=[ctmk, 'bfA'], w=[pgok[g]])
                        for half in range(2):
                            pc = PS[:, 3072:4096]
                            pck = [('ps', 6), ('ps', 7)]
                            for a in range(8):
                                hp = half * 8 + a
                                T.op('pe', lambda e, a=a, hp=hp: e.matmul(pc[:, a * 128:(a + 1) * 128],
                                                                          lhsT=bfB[:, hp * 128:(hp + 1) * 128],
                                                                          rhs=bm[:, (hp // 4) * 128:(hp // 4 + 1) * 128],
                                                                          start=True, stop=True),
                                     r=['bfB', bmk], w=[pck[a // 4]])
                            hv = h03[:, half * 8:(half + 1) * 8, :]
                            T.op('dve', lambda e, hv=hv, half=half, s=s: e.tensor_tensor(
                                out=hv, in0=hv, in1=dcol[:, half * 8:(half + 1) * 8, s:s + 1].broadcast_to([128, 8, 128]),
                                op=ALU.mult), r=[h0k, 'dcol'], w=[h0k])
                            T.op('dve', lambda e, hv=hv: e.tensor_tensor(
                                out=hv, in0=hv, in1=pc.rearrange("p (a n) -> p a n", a=8), op=ALU.add),
                                r=pck + [h0k], w=[h0k])
                        T.dma('sp', ssms[j, s].rearrange("(hp two) p n -> (two p) hp n", two=2), h03, r=[h0k])
                    T.barrier()
                    T.op('dve', lambda e: e.tensor_tensor(out=v3(xs_tok), in0=v3(pgo), in1=bc64(ecum), op=ALU.mult),
                         r=pgok + [K('ex')], w=['xs_tok'])
                    T.op('dve', lambda e: e.tensor_tensor(out=ybuf, in0=ybuf, in1=xs_tok, op=ALU.add),
                         r=['xs_tok', 'y'], w=['y'])
                    for s in range(NS):
                        T.dma('sp', convs[j, s], rawT[8 * s + 5:8 * s + 8, :], r=['rawT'])
                yield 'mid'
                for zc in range(4):
                    pb, pk = bank()
                    for k in range(8):
                        T.op('pe', lambda e, k=k: e.matmul(pb, lhsT=hT3[:, k, :], rhs=w_in3[:, k, zc * 512:(zc + 1) * 512],
                                                           start=(k == 0), stop=(k == 7)), r=[K('hT')] + WAK, w=[pk])
                    zs = tmpA[zc % 2]
                    zsk = 'tmpA%d' % (zc % 2)
                    T.op('act', lambda e: e.activation(out=zs, in_=pb, func=AF.Silu), r=[pk], w=[zsk])
                    yield 'backP'
                    T.op('dve', lambda e: e.tensor_tensor(out=ybuf[:, zc * 512:(zc + 1) * 512],
                                                          in0=ybuf[:, zc * 512:(zc + 1) * 512], in1=zs, op=ALU.mult),
                         r=[zsk, 'y'], w=['y'])
                    yield 'back'
                c4 = 16 * par + 4
                T.op('act', lambda e: e.activation(out=bfA, in_=ybuf, func=AF.Square, accum_out=stat[:, c4:c4 + 1]),
                     r=['y'], w=['bfA', 'stat'])
                T.op('act', lambda e: e.activation(out=stat[:, c4 + 1:c4 + 2], in_=stat[:, c4:c4 + 1], func=AF.Sqrt,
                                                   scale=1.0 / DI, bias=epsb[:, 0:1]), r=['stat', 'epsb'], w=['stat'])
                yield 'backP'
                T.op('dve', lambda e: e.reciprocal(out=stat[:, c4:c4 + 1], in_=stat[:, c4 + 1:c4 + 2]),
                     r=['stat'], w=['stat'])
                T.op('dve', lambda e: e.tensor_scalar(out=bfA, in0=ybuf, scalar1=stat[:, c4:c4 + 1], scalar2=None, op0=ALU.mult),
                     r=['y', 'stat'], w=['bfA'])
                yield 'back'
                pg2, pg2k = PS[:, 3072:4096], [('ps', 6), ('ps', 7)]
                pg2b = pg2.bitcast(BF16)
                for f in range(16):
                    T.op('pe', lambda e, f=f: e.transpose(pg2b[:, f * 128:(f + 1) * 128], bfA[:, f * 128:(f + 1) * 128],
                                                          ident_b[:]), r=['bfA', 'ident_b'], w=[pg2k[f // 8]])
                yield 'backP'
                yT3 = bfB.rearrange("p (f t) -> p f t", f=16)
                T.op('dve', lambda e: e.tensor_tensor(out=yT3, in0=pg2b.rearrange("p (f t) -> p f t", f=16),
                                                      in1=snw_j.unsqueeze(2).broadcast_to([128, 16, 128]), op=ALU.mult),
                     r=pg2k + ['snw'], w=['bfB'])
                yield 'back'
                xkey = K('xt')
                for hh in range(2):
                    pb, pk = bank()
                    for f in range(16):
                        T.op('pe', lambda e, f=f: e.matmul(pb, lhsT=yT3[:, f, :], rhs=w_out3[:, f, hh * 512:(hh + 1) * 512],
                                                           start=(f == 0), stop=(f == 15)), r=['bfB'] + WBK, w=[pk])
                    yield 'backP'
                    T.op('dve', lambda e: e.tensor_tensor(out=xt[:, hh * 512:(hh + 1) * 512],
                                                          in0=xt[:, hh * 512:(hh + 1) * 512], in1=pb, op=ALU.add),
                         r=[pk, xkey], w=[xkey])
                    if hh == 0:
                        yield 'back'
                if L < 3:
                    T.dma('sp', xscr[row0:row0 + 128, :], xt, r=[xkey], w=[('xscr', row0 // 128)])
                else:
                    rms_stat(xt, xkey, stat, 8, DM, h, 'h')
                    T.op('dve', lambda e: e.scalar_tensor_tensor(out=xt, in0=xt, scalar=stat[:, 8:9], in1=fnw[:],
                                                                 op0=ALU.mult, op1=ALU.mult),
                         r=[xkey, 'stat', 'fnw'], w=[xkey])
                    dst = yp[row0:row0 + 128, :] if row0 < SEQ else ysm[row0 - SEQ:row0 - SEQ + 128, :]
                    T.dma('sp', dst, xt, r=[xkey])

            return chunk, (ST, STb)

        def run_ssd(j):
            chunk, (ST, STb) = ssd_layer(j)
            n = PLAN.get('schunks', 16)
            gens = [chunk(ci, 128 * ci, False) for ci in range(n)]
            adv = lambda g: next(g, None)
            if n:
                assert adv(gens[0]) == 'front'
                for _ in range(6):
                    assert adv(gens[0]) == 'xbc'
            for ci in range(n):
                nxt = gens[ci + 1] if ci + 1 < n else None
                assert adv(gens[ci]) == 'midA'
                xleft = 0
                if nxt is not None:
                    assert adv(nxt) == 'front'
                    xleft = 6
                assert adv(gens[ci]) == 'mid'
                done_back = False
                while (not done_back) or xleft:
                    r = None
                    if not done_back:
                        r = adv(gens[ci])
                        if r is None:
                            done_back = True
                    if xleft:
                        assert adv(nxt) == 'xbc'
                        xleft -= 1
                    if r == 'backP':
                        r = adv(gens[ci])
                        if r is None:
                            done_back = True
            T.barrier()
            if PLAN.get('ssample', True):
                for _ in chunk(0, SEQ, True):
                    pass

        if PLAN.get('pool0', True):
            pool_layer(0)
        if PLAN.get('ssd0', True):
            run_ssd(0)
        if PLAN.get('pool1', True):
            pool_layer(1)
        if PLAN.get('ssd1', True):
            run_ssd(1)
        T.finish()
    return nc


_NC_CACHE = {}


def _col(v, nchunk):
    return np.ascontiguousarray(v.reshape(nchunk, 128).T)


def kernel(x_prompt, x_sample, state_pool, state_conv, state_ssm, norm_w, pool_in_w, pool_mix_w, pool_scale,
           pool_out_w, ssd_in_w, ssd_conv_w, ssd_conv_b, ssd_dt_bias, ssd_A_log, ssd_D, ssd_norm_w, ssd_out_w,
           final_norm_w):
    f = lambda a: np.ascontiguousarray(np.asarray(a, dtype=np.float32))
    x_prompt, x_sample, state_pool, state_conv, state_ssm = map(f, (x_prompt, x_sample, state_pool, state_conv, state_ssm))
    norm_w, pool_scale, ssd_conv_w, ssd_conv_b = map(f, (norm_w, pool_scale, ssd_conv_w, ssd_conv_b))
    ssd_dt_bias, ssd_A_log, ssd_D, ssd_norm_w, final_norm_w = map(f, (ssd_dt_bias, ssd_A_log, ssd_D, ssd_norm_w, final_norm_w))
    nwc = np.concatenate([_col(norm_w[l], 8) for l in range(4)], axis=1)
    pscale = np.concatenate([_col(pool_scale[j], 16) for j in range(2)], axis=1)
    convw = np.concatenate([np.ascontiguousarray(ssd_conv_w[j].reshape(4, 24, 128).transpose(2, 1, 0)).reshape(128, 96)
                            for j in range(2)], axis=1)
    convb = np.concatenate([_col(ssd_conv_b[j], 24) for j in range(2)], axis=1)
    snw = np.concatenate([_col(ssd_norm_w[j], 16) for j in range(2)], axis=1)
    shared = {
        "pool_in_w": f(pool_in_w), "pool_mix_w": f(pool_mix_w), "pool_out_w": f(pool_out_w),
        "ssd_in_w": f(ssd_in_w), "ssd_out_w": f(ssd_out_w),
        "nwc": f(nwc), "fnw": f(final_norm_w.reshape(1, DM)), "pscale": f(pscale), "convw": f(convw), "convb": f(convb),
        "dtb": f(ssd_dt_bias.reshape(1, 64)), "alog": f(ssd_A_log.reshape(1, 64)), "dsk": f(ssd_D.reshape(1, 64)),
        "snw": f(snw),
    }
    in_maps = []
    for c in range(NCORES):
        m = dict(shared)
        m["xp"] = x_prompt[c]
        m["xsm"] = f(x_sample[NS * c:NS * (c + 1)].reshape(128, DM))
        m["spool"] = f(state_pool[:, NS * c:NS * (c + 1)])
        m["sconv"] = f(state_conv[:, NS * c:NS * (c + 1)])
        m["sssm"] = f(state_ssm[:, NS * c:NS * (c + 1)])
        in_maps.append(m)
    if "nc" not in _NC_CACHE:
        _NC_CACHE["nc"] = build_program()
    res = run_bass_kernel_spmd(_NC_CACHE["nc"], in_maps, core_ids=list(range(NCORES)))
    R = res.results
    y_prompt = np.stack([R[c]["yp"] for c in range(NCORES)], axis=0)
    y_sample = np.concatenate([R[c]["ysm"].reshape(NS, DS, DM) for c in range(NCORES)], axis=0)
    pool_p = np.stack([R[c]["poolp"] for c in range(NCORES)], axis=1)
    pool_s = np.concatenate([R[c]["pools"] for c in range(NCORES)], axis=1)
    conv_p = np.stack([R[c]["convp"] for c in range(NCORES)], axis=1)
    conv_s = np.concatenate([R[c]["convs"] for c in range(NCORES)], axis=1)
    ssm_p = np.stack([R[c]["ssmp"] for c in range(NCORES)], axis=1)
    ssm_s = np.concatenate([R[c]["ssms"] for c in range(NCORES)], axis=1)
    return tuple(np.ascontiguousarray(a, dtype=np.float32) for a in
                 (y_prompt, y_sample, pool_p, pool_s, conv_p, conv_s, ssm_p, ssm_s))
```

```python
import numpy as np
import concourse.bass as bass
import concourse.mybir as mybir
from concourse.bass_utils import run_bass_kernel_spmd
from contextlib import ExitStack

F32 = mybir.dt.float32
BF16 = mybir.dt.bfloat16
AF = mybir.ActivationFunctionType
ALU = mybir.AluOpType

NCORES = 8
PLAN = {}


class Stop(Exception):
    pass


def stage(n, cond=True):
    if cond and PLAN.get('stop') == n:
        raise Stop()
DM = 1024
DI = 2048
SEQ = 2048
NS = 16
DS = 8
CONV_DIM = 3072
SSD_IN = 5152
EPS = 1e-6
POOL_W = (2, 4, 8, 16)


class Trk:
    def __init__(self, nc, es):
        self.nc = nc
        self.E = {'pe': nc.tensor, 'act': nc.scalar, 'dve': nc.vector, 'pool': nc.gpsimd, 'sp': nc.sync}
        self.sem = {e: es.enter_context(nc.semaphore('c_' + e)) for e in ('pe', 'act', 'dve', 'pool')}
        self.cnt = {e: 0 for e in self.sem}
        self.dq = {'sp': [es.enter_context(nc.semaphore('dsp%d' % i)) for i in range(14)],
                   'pool': [es.enter_context(nc.semaphore('dpl%d' % i)) for i in range(6)]}
        self.dcnt = {q: [0] * len(v) for q, v in self.dq.items()}
        self.drr = {q: 0 for q in self.dq}
        self.seen = {e: {} for e in self.E}
        self.bufs = {}

    def _semobj(self, k):
        return self.sem[k] if isinstance(k, str) else self.dq[k[0]][k[1]]

    def _collect(self, r, w):
        need = {}

        def add(ev):
            for k, v in ev.items():
                if need.get(k, 0) < v:
                    need[k] = v
        for key in r:
            b = self.bufs.get(key)
            if b:
                add(b['w'])
        for key in w:
            b = self.bufs.get(key)
            if b:
                add(b['w'])
                add(b['r'])
        return need

    def _emit_waits(self, e, need):
        for k, v in need.items():
            if k == 'pe' and e == 'pe':
                continue
            if k == e and PLAN.get('noself'):
                continue
            if self.seen[e].get(k, 0) >= v:
                continue
            self.E[e].wait_ge(self._semobj(k), v)
            self.seen[e][k] = v

    def _update(self, r, w, k, v):
        for key in r:
            b = self.bufs.setdefault(key, {'w': {}, 'r': {}})
            if b['r'].get(k, 0) < v:
                b['r'][k] = v
        for key in w:
            self.bufs[key] = {'w': {k: v}, 'r': {}}

    def op(self, e, fn, r=(), w=()):
        self._emit_waits(e, self._collect(r, w))
        ins = fn(self.E[e])
        self.cnt[e] += 1
        ins.then_inc(self.sem[e], 1)
        self._update(r, w, e, self.cnt[e])

    def dma(self, q, out, in_, r=(), w=(), **kw):
        i = self.drr[q]
        self.drr[q] = (i + 1) % len(self.dq[q])
        need = self._collect(r, w)
        k = (q, i)
        prev = 16 * self.dcnt[q][i]
        if prev and need.get(k, 0) < prev:
            need[k] = prev
        self._emit_waits(q, need)
        ins = self.E[q].dma_start(out=out, in_=in_, **kw)
        self.dcnt[q][i] += 1
        ins.then_inc(self.dq[q][i], 16)
        self._update(r, w, k, 16 * self.dcnt[q][i])

    def _all(self):
        need = {e: c for e, c in self.cnt.items() if c}
        for q, lst in self.dcnt.items():
            for i, c in enumerate(lst):
                if c:
                    need[(q, i)] = 16 * c
        return need

    def barrier(self):
        need = self._all()
        for e in ('pe', 'act', 'dve', 'pool', 'sp'):
            n2 = dict(need)
            self._emit_waits(e, n2)
        self.bufs = {}

    def finish(self):
        self._emit_waits('sp', self._all())


class Arena:
    def __init__(self, ap_f32, words):
        self.ap = ap_f32
        self.words = words
        self.off = 0

    def reset(self):
        self.off = 0

    def f32(self, words, parts=128):
        a = self.ap[0:parts, self.off:self.off + words]
        self.off += words
        assert self.off <= self.words, (self.off, self.words)
        return a

    def bf16(self, elems, parts=128):
        assert elems % 2 == 0
        return self.f32(elems // 2, parts).bitcast(BF16)


def build_program():
    nc = bass.Bass("TRN2", target_bir_lowering=False)
    dt_in = lambda n, s: nc.dram_tensor(n, s, F32, kind="ExternalInput").ap()
    dt_out = lambda n, s: nc.dram_tensor(n, s, F32, kind="ExternalOutput").ap()
    xp = dt_in("xp", [SEQ, DM])
    xsm = dt_in("xsm", [128, DM])
    spool = dt_in("spool", [2, NS, 15, DI])
    sconv = dt_in("sconv", [2, NS, 3, CONV_DIM])
    sssm = dt_in("sssm", [2, NS, 32, 64, 128])
    pool_in_w = dt_in("pool_in_w", [2, DM, 2 * DI])
    pool_mix_w = dt_in("pool_mix_w", [2, 4, 512, 512])
    pool_out_w = dt_in("pool_out_w", [2, DI, DM])
    ssd_in_w = dt_in("ssd_in_w", [2, DM, SSD_IN])
    ssd_out_w = dt_in("ssd_out_w", [2, DI, DM])
    nwc_d = dt_in("nwc", [128, 4 * 8])
    fnw_d = dt_in("fnw", [1, DM])
    pscale_d = dt_in("pscale", [128, 2 * 16])
    convw_d = dt_in("convw", [128, 2 * 24 * 4])
    convb_d = dt_in("convb", [128, 2 * 24])
    dtb_d = dt_in("dtb", [1, 64])
    alog_d = dt_in("alog", [1, 64])
    dsk_d = dt_in("dsk", [1, 64])
    snw_d = dt_in("snw", [128, 2 * 16])

    yp = dt_out("yp", [SEQ, DM])
    ysm = dt_out("ysm", [128, DM])
    poolp = dt_out("poolp", [2, 15, DI])
    pools = dt_out("pools", [2, NS, 15, DI])
    convp = dt_out("convp", [2, 3, CONV_DIM])
    convs = dt_out("convs", [2, NS, 3, CONV_DIM])
    ssmp = dt_out("ssmp", [2, 32, 64, 128])
    ssms = dt_out("ssms", [2, NS, 32, 64, 128])
    xscr = nc.dram_tensor("xscr", [SEQ + 128, DM], F32).ap()

    with ExitStack() as es:
        sb = lambda n, s, d: es.enter_context(nc.sbuf_tensor(n, s, d))
        T = Trk(nc, es)
        WA = sb("WA", [128, 8 * SSD_IN], BF16)
        WB = sb("WB", [128, 16 * DM], BF16)
        ident_f = sb("ident_f", [128, 128], F32)
        ident_b = sb("ident_b", [128, 128], BF16)
        Umat = sb("Umat", [128, 128], F32)
        SLmat = sb("SLmat", [128, 128], F32)
        ones_f = sb("ones_f", [128, 128], F32)
        Us = sb("Us", [128, 128], F32)
        SLs = sb("SLs", [128, 128], F32)
        seqind = sb("seqind", [128, 16], F32)
        icnt = sb("icnt", [128, 4 * 16], F32)
        nwc = sb("nwc_s", [128, 32], F32)
        fnw = sb("fnw_s", [128, DM], F32)
        pscale = sb("pscale_s", [128, 32], F32)
        convw = sb("convw_s", [128, 192], F32)
        convb = sb("convb_s", [128, 48], F32)
        dtb = sb("dtb_s", [128, 64], F32)
        negA = sb("negA_s", [128, 64], F32)
        dsk = sb("dsk_s", [128, 64], F32)
        snw = sb("snw_s", [128, 32], F32)
        AW = 21600
        arena_t = sb("arena", [128, AW], F32)
        A = Arena(arena_t, AW)
        PS = es.enter_context(nc.psum_tensor("PS", [128, 4096], F32))

        st = {'b': 0, 'g': 0}

        def bank():
            b = st['b']
            st['b'] = (b + 1) % 4
            return PS[:, b * 512:(b + 1) * 512], ('ps', b)

        def bgroup(n=4):
            if n == 4:
                return PS[:, 2048:4096], [('ps', 4 + i) for i in range(4)]
            g = st['g']
            st['g'] = (g + 1) % 2
            return PS[:, 2048 + g * 1024:2048 + (g + 1) * 1024], [('ps', 4 + 2 * g + i) for i in range(2)]

        def hbank(i):
            b = 4 + (i % 2)
            return PS[:, b * 512:(b + 1) * 512], ('ps', b)

        T.op('pool', lambda e: e.memset(ident_f[:], 0.0), w=['ident_f'])
        T.op('pool', lambda e: e.affine_select(out=ident_f[:], in_=ident_f[:], pattern=[[-1, 128]],
                                               compare_op=ALU.not_equal, fill=1.0, base=0, channel_multiplier=1),
             r=['ident_f'], w=['ident_f'])
        T.op('dve', lambda e: e.tensor_copy(out=ident_b[:], in_=ident_f[:]), r=['ident_f'], w=['ident_b'])
        T.op('pool', lambda e: e.memset(ones_f[:], 1.0), w=['ones_f'])
        T.op('pool', lambda e: e.memset(Umat[:], 1.0), w=['U'])
        T.op('pool', lambda e: e.affine_select(out=Umat[:], in_=Umat[:], pattern=[[1, 128]], compare_op=ALU.is_ge,
                                               fill=0.0, base=0, channel_multiplier=-1), r=['U'], w=['U'])
        T.op('pool', lambda e: e.memset(SLmat[:], 1.0), w=['SL'])
        T.op('pool', lambda e: e.affine_select(out=SLmat[:], in_=SLmat[:], pattern=[[-1, 128]], compare_op=ALU.is_gt,
                                               fill=0.0, base=0, channel_multiplier=1), r=['SL'], w=['SL'])

        def blockdiag(dst, key, pat_tail):
            T.op('pool', lambda e: e.affine_select(out=dst, in_=dst, pattern=[[-8, 16]] + pat_tail,
                                                   compare_op=ALU.is_ge, fill=0.0, base=0, channel_multiplier=1),
                 r=[key], w=[key])
            T.op('pool', lambda e: e.affine_select(out=dst, in_=dst, pattern=[[8, 16]] + pat_tail,
                                                   compare_op=ALU.is_ge, fill=0.0, base=7, channel_multiplier=-1),
                 r=[key], w=[key])
        T.op('pool', lambda e: e.tensor_copy(out=Us[:], in_=Umat[:]), r=['U'], w=['Us'])
        blockdiag(Us[:].rearrange("p (s t) -> p s t", t=8), 'Us', [[0, 8]])
        T.op('pool', lambda e: e.tensor_copy(out=SLs[:], in_=SLmat[:]), r=['SL'], w=['SLs'])
        blockdiag(SLs[:].rearrange("p (s t) -> p s t", t=8), 'SLs', [[0, 8]])
        T.op('pool', lambda e: e.memset(seqind[:], 1.0), w=['seqind'])
        blockdiag(seqind[:], 'seqind', [])
        icnt3 = icnt[:].rearrange("p (g t) -> p g t", g=4)
        T.op('pool', lambda e: e.iota(icnt3, pattern=[[0, 4], [1, 16]], base=1, channel_multiplier=0,
                                      allow_small_or_imprecise_dtypes=True), w=['icnt'])
        for g in range(4):
            T.op('dve', lambda e, g=g: e.tensor_scalar(out=icnt3[:, g, :], in0=icnt3[:, g, :], scalar1=float(POOL_W[g]),
                                                       scalar2=None, op0=ALU.min), r=['icnt'], w=['icnt'])
        T.op('dve', lambda e: e.reciprocal(out=icnt[:], in_=icnt[:]), r=['icnt'], w=['icnt'])

        T.dma('sp', nwc[:], nwc_d[:, :], w=['nwc'])
        T.dma('sp', fnw[:], fnw_d.partition_broadcast(128), w=['fnw'])
        T.dma('sp', pscale[:], pscale_d[:, :], w=['pscale'])
        T.dma('sp', convw[:], convw_d[:, :], w=['convw'])
        T.dma('sp', convb[:], convb_d[:, :], w=['convb'])
        T.dma('sp', dtb[:], dtb_d.partition_broadcast(128), w=['dtb'])
        T.dma('sp', negA[:], alog_d.partition_broadcast(128), w=['negA'])
        T.dma('sp', dsk[:], dsk_d.partition_broadcast(128), w=['dsk'])
        T.dma('sp', snw[:], snw_d[:, :], w=['snw'])
        T.op('act', lambda e: e.activation(out=negA[:], in_=negA[:], func=AF.Exp), r=['negA'], w=['negA'])
        T.op('dve', lambda e: e.tensor_scalar(out=negA[:], in0=negA[:], scalar1=-1.0, scalar2=None, op0=ALU.mult),
             r=['negA'], w=['negA'])

        def src_rows(L, row0, n):
            if L == 0:
                return (xp[row0:row0 + n, :], []) if row0 < SEQ else (xsm[row0 - SEQ:row0 - SEQ + n, :], [])
            return xscr[row0:row0 + n, :], [('xscr', row0 // 128)]

        def rms_stat(src_ap, src_key, stat, col, n_feat, junk, junk_key):
            T.op('act', lambda e: e.activation(out=junk, in_=src_ap, func=AF.Square, accum_out=stat[:, col:col + 1]),
                 r=[src_key], w=[junk_key, 'stat'])
            T.op('act', lambda e: e.activation(out=stat[:, col + 1:col + 2], in_=stat[:, col:col + 1], func=AF.Sqrt,
                                               scale=1.0 / n_feat, bias=epsb[:, 0:1]), r=['stat', 'epsb'], w=['stat'])
            T.op('dve', lambda e: e.reciprocal(out=stat[:, col:col + 1], in_=stat[:, col + 1:col + 2]),
                 r=['stat'], w=['stat'])

        WAK = [('WA', k) for k in range(24)]
        WBK = [('WB', k) for k in range(16)]

        def load_weights_bf16(dst3, src3, nk, key, k0=0):
            for k in range(nk):
                T.dma('pool', dst3[:, k, :], src3[:, k, :], w=[(key, k0 + k)])

        epsb = sb("epsb", [128, 1], F32)
        T.op('pool', lambda e: e.memset(epsb[:], EPS), w=['epsb'])

        def norm_and_transpose(xblk, xkey, L, h, hT3, hkey, hTkey, stat, col, tok0):
            rms_stat(xblk, xkey, stat, col, DM, h, hkey)
            T.op('dve', lambda e: e.tensor_scalar(out=h, in0=xblk, scalar1=stat[:, col:col + 1], scalar2=None,
                                                  op0=ALU.mult), r=[xkey, 'stat'], w=[hkey])
            pb, pk = bank()
            pbb = pb.bitcast(BF16)
            for k in range(8):
                T.op('pe', lambda e, k=k: e.transpose(pbb[:, k * 128:(k + 1) * 128], h[:, k * 128:(k + 1) * 128],
                                                      ident_b[:]), r=[hkey, 'ident_b'], w=[pk])
            T.op('dve', lambda e: e.tensor_tensor(
                out=hT3[:, :, tok0:tok0 + 128], in0=pbb.rearrange("p (k t) -> p k t", k=8),
                in1=nwc[:, L * 8:(L + 1) * 8].unsqueeze(2).broadcast_to([128, 8, 128]), op=ALU.mult),
                r=[pk, 'nwc'], w=[hTkey])

        def out_proj_and_store(L, yT3, yTkey, tokoff, xblk, xkey, row0, w_out3, blkidx, junk, junk_key, stat):
            for hh in range(2):
                pb, pk = bank()
                for f in range(16):
                    T.op('pe', lambda e, f=f: e.matmul(pb, lhsT=yT3[:, f, tokoff:tokoff + 128],
                                                       rhs=w_out3[:, f, hh * 512:(hh + 1) * 512],
                                                       start=(f == 0), stop=(f == 15)), r=[yTkey] + WBK, w=[pk])
                T.op('dve', lambda e: e.tensor_tensor(out=xblk[:, hh * 512:(hh + 1) * 512],
                                                      in0=xblk[:, hh * 512:(hh + 1) * 512], in1=pb, op=ALU.add),
                     r=[pk, xkey], w=[xkey])
            if L < 3:
                T.dma('sp', xscr[row0:row0 + 128, :], xblk, r=[xkey], w=[('xscr', row0 // 128)])
            else:
                rms_stat(xblk, xkey, stat, 8, DM, junk, junk_key)
                T.op('dve', lambda e: e.scalar_tensor_tensor(out=xblk, in0=xblk, scalar=stat[:, 8:9], in1=fnw[:],
                                                             op0=ALU.mult, op1=ALU.mult),
                     r=[xkey, 'stat', 'fnw'], w=[xkey])
                dst = yp[row0:row0 + 128, :] if row0 < SEQ else ysm[row0 - SEQ:row0 - SEQ + 128, :]
                T.dma('sp', dst, xblk, r=[xkey])

        def pool_layer(j):
            L = 2 * j
            T.barrier()
            A.reset()
            w_in3 = WA[:, 0:8 * 4096].rearrange("p (k n) -> p k n", k=8)
            w_mix3 = WA[:, 8 * 4096:8 * 4096 + 16 * 512].rearrange("p (k n) -> p k n", k=16)
            w_out3 = WB[:].rearrange("p (k n) -> p k n", k=16)
            load_weights_bf16(w_in3, pool_in_w[j].rearrange("(k p) n -> p k n", p=128), 8, 'WA')
            load_weights_bf16(w_mix3, pool_mix_w[j].rearrange("g (c p) n -> p (g c) n", p=128), 16, 'WA', k0=8)
            load_weights_bf16(w_out3, pool_out_w[j].rearrange("(k p) n -> p k n", p=128), 16, 'WB')

            xts = [A.f32(2048).rearrange("p (b n) -> p b n", b=2) for _ in range(2)]
            h = A.bf16(1024)
            hTs = [A.bf16(8 * 256).rearrange("p (k t) -> p k t", k=8) for _ in range(2)]
            ues = [A.f32(1472), A.f32(1472)]
            sa = A.f32(1472)
            sbuf2 = A.f32(1472)
            pgs = [A.bf16(4 * 256), A.bf16(4 * 256)]
            szs = [A.f32(256) for _ in range(4)]
            ybf = A.bf16(16 * 256)
            carry = A.f32(240).rearrange("p (c r) -> p c r", c=16)
            strows = A.f32(1024, parts=120).rearrange("p (a n) -> p a n", a=2)
            tokrows = A.f32(2048)
            stat = A.f32(64)

            T.op('dve', lambda e: e.memset(carry, 0.0), w=['carry'])
            szi = [0]

            class Tile:
                pass

            def mk(ti, row0, Tn, sample, first, last_prompt):
                t = Tile()
                t.ti, t.row0, t.Tn, t.sample, t.first, t.last = ti, row0, Tn, sample, first, last_prompt
                t.nb = Tn // 128
                t.xt = xts[ti % 2]
                t.xk = lambda b: ('xt', ti % 2, b)
                t.hT3 = hTs[ti % 2]
                t.hTk = 'hT%d' % (ti % 2)
                t.hTv = t.hT3[:, :, 0:Tn]
                t.y3 = ybf[:, 0:16 * Tn].rearrange("p (f t) -> p f t", f=16)
                return t

            def front(t):
                for b in range(t.nb):
                    src, rk = src_rows(L, t.row0 + 128 * b, 128)
                    T.dma('sp', t.xt[:, b, :], src, r=rk, w=[t.xk(b)])
                for b in range(t.nb):
                    norm_and_transpose(t.xt[:, b, :], t.xk(b), L, h, t.hT3, 'h', t.hTk, stat, 16 * (t.ti % 2) + 2 * b,
                                       128 * b)

            def views(t, g):
                ue = ues[g % 2]
                Tn = t.Tn
                if not t.sample:
                    EW = 15 + Tn
                    r3 = lambda buf: buf[:, 0:4 * EW].rearrange("p (c t) -> p c t", c=4)
                else:
                    r3 = lambda buf: buf[:, 0:4 * 16 * 23].rearrange("p (c t) -> p c t", c=4)
                return r3(ue), r3(sa), r3(sbuf2), 'ue%d' % (g % 2)

            def U(t, g):
                Tn = t.Tn
                ue3, sa3, sb3, uek = views(t, g)
                if not t.sample:
                    T.op('act', lambda e: e.activation(out=ue3[:, :, 0:15], in_=carry[:, 4 * g:4 * g + 4, :],
                                                       func=AF.Copy), r=['carry'], w=[uek])
                else:
                    ue4 = ue3.rearrange("p c (s t) -> p c s t", s=16)
                    for a in range(2):
                        T.dma('sp', strows[:, a, :],
                              spool[j, 8 * a:8 * a + 8, :, g * 512:(g + 1) * 512].rearrange("s r n -> (s r) n"),
                              w=['strows'])
                    for cc in range(4):
                        pb, pk = bank()
                        for a in range(2):
                            T.op('pe', lambda e, a=a, cc=cc: e.transpose(
                                pb[:, a * 120:(a + 1) * 120], strows[:, a, cc * 128:(cc + 1) * 128],
                                ident_f[0:120, 0:120]), r=['strows', 'ident_f'], w=[pk])
                        T.op('act', lambda e, cc=cc: e.activation(
                            out=ue4[:, cc, :, 0:15], in_=pb[:, 0:240].rearrange("p (s r) -> p s r", r=15),
                            func=AF.Copy), r=[pk], w=[uek])
                for cc in range(4):
                    c = 4 * g + cc
                    pb, pk = bank()
                    for k in range(8):
                        T.op('pe', lambda e, k=k: e.matmul(pb[:, 0:Tn], lhsT=w_in3[:, k, c * 128:(c + 1) * 128],
                                                           rhs=t.hTv[:, k, :], start=(k == 0), stop=(k == 7)),
                             r=WAK + [t.hTk], w=[pk])
                    if not t.sample:
                        T.op('act', lambda e, cc=cc: e.activation(out=ue3[:, cc, 15:15 + Tn], in_=pb[:, 0:Tn],
                                                                  func=AF.Copy), r=[pk], w=[uek])
                    else:
                        T.op('act', lambda e, cc=cc: e.activation(
                            out=ue4[:, cc, :, 15:23], in_=pb[:, 0:128].rearrange("p (s t) -> p s t", t=8),
                            func=AF.Copy), r=[pk], w=[uek])
                        pb2, pk2 = bank()
                        T.op('act', lambda e: e.activation(out=szs[0][:, 0:128], in_=pb[:, 0:128], func=AF.Copy),
                             r=[pk], w=['sz0'])
                        T.op('pe', lambda e: e.transpose(pb2[:, 0:128], szs[0][:, 0:128], ident_f[:]),
                             r=['sz0', 'ident_f'], w=[pk2])
                        T.op('act', lambda e, c=c: e.activation(out=tokrows[:, c * 128:(c + 1) * 128],
                                                                in_=pb2[:, 0:128], func=AF.Copy),
                             r=[pk2], w=['tokrows'])
                if not t.sample:
                    T.op('act', lambda e: e.activation(out=carry[:, 4 * g:4 * g + 4, :],
                                                       in_=ue3[:, :, Tn:Tn + 15], func=AF.Copy), r=[uek], w=['carry'])

            def Pst(t, g):
                Tn = t.Tn
                wdw = POOL_W[g]
                ue3, sa3, sb3, uek = views(t, g)
                pg = pgs[g % 2]
                pgk = 'pg%d' % (g % 2)
                if not t.sample:
                    def sl(buf3, lo, hi):
                        return buf3[:, :, lo:hi]
                    Wd = 15 + Tn
                else:
                    def sl(buf3, lo, hi):
                        return buf3.rearrange("p c (s t) -> p c s t", s=16)[:, :, :, lo:hi]
                    Wd = 23
                cur, curkey = ue3, uek
                tmp = [(sa3, 'sa'), (sb3, 'sb')]
                step = 1
                ti = 0
                eng_rr = ['dve', 'dve']
                while step < wdw:
                    dst, dkey = tmp[ti % 2]
                    lo = 2 * step - 1
                    T.op(eng_rr[ti % 2], lambda e, cur=cur, dst=dst, step=step, lo=lo: e.tensor_tensor(
                        out=sl(dst, lo, Wd), in0=sl(cur, lo, Wd), in1=sl(cur, lo - step, Wd - step), op=ALU.add),
                        r=[curkey], w=[dkey])
                    cur, curkey = dst, dkey
                    step *= 2
                    ti += 1
                if not t.sample:
                    pg3 = pg[:, 0:4 * Tn].rearrange("p (c t) -> p c t", c=4)
                    T.op('dve', lambda e, cur=cur: e.scalar_tensor_tensor(
                        out=pg3, in0=cur[:, :, 15:15 + Tn], scalar=1.0 / wdw, in1=ue3[:, :, 15:15 + Tn],
                        op0=ALU.mult, op1=ALU.subtract), r=[curkey, uek], w=[pgk])
                    if t.first:
                        dst, dkey = tmp[ti % 2]
                        T.op('dve', lambda e, cur=cur, dst=dst: e.tensor_tensor(
                            out=dst[:, :, 15:31], in0=cur[:, :, 15:31],
                            in1=icnt3[:, g, :].unsqueeze(1).broadcast_to([128, 4, 16]), op=ALU.mult),
                            r=[curkey, 'icnt'], w=[dkey])
                        T.op('dve', lambda e, dst=dst: e.tensor_tensor(
                            out=pg3[:, :, 0:16], in0=dst[:, :, 15:31], in1=ue3[:, :, 15:31], op=ALU.subtract),
                            r=[dkey, uek, pgk], w=[pgk])
                else:
                    pg2 = pg[:, 0:4 * 128].rearrange("p (cs t) -> p cs t", t=8)
                    c2 = cur.rearrange("p c (s t) -> p (c s) t", s=16)[:, :, 15:23]
                    u2 = ue3.rearrange("p c (s t) -> p (c s) t", s=16)[:, :, 15:23]
                    T.op('dve', lambda e: e.scalar_tensor_tensor(
                        out=pg2, in0=c2, scalar=1.0 / wdw, in1=u2, op0=ALU.mult, op1=ALU.subtract),
                        r=[curkey, uek], w=[pgk])

            def Zp(t, g):
                Tn = t.Tn
                for dd in range(4):
                    d = 4 * g + dd
                    pz, pzk = bank()
                    for k in range(8):
                        T.op('pe', lambda e, k=k: e.matmul(pz[:, 0:Tn], lhsT=w_in3[:, k, DI + d * 128:DI + (d + 1) * 128],
                                                           rhs=t.hTv[:, k, :], start=(k == 0), stop=(k == 7)),
                             r=WAK + [t.hTk], w=[pzk])
                    T.op('act', lambda e, dd=dd: e.activation(out=szs[dd][:, 0:Tn], in_=pz[:, 0:Tn], func=AF.Silu),
                         r=[pzk], w=['sz%d' % dd])

            def M(t, g):
                Tn = t.Tn
                pg = pgs[g % 2]
                pgk = 'pg%d' % (g % 2)
                pg3 = pg[:, 0:4 * Tn].rearrange("p (c t) -> p c t", c=4)
                for dd in range(4):
                    d = 4 * g + dd
                    pm, pmk = PS[:, (4 + dd) * 512:(5 + dd) * 512], ('ps', 4 + dd)
                    for cc in range(4):
                        T.op('pe', lambda e, cc=cc: e.matmul(pm[:, 0:Tn], lhsT=w_mix3[:, 4 * g + cc, dd * 128:(dd + 1) * 128],
                                                             rhs=pg3[:, cc, :], start=(cc == 0), stop=(cc == 3)),
                             r=WAK + [pgk], w=[pmk])
                    T.op('dve', lambda e, dd=dd, d=d: e.scalar_tensor_tensor(
                        out=t.y3[:, d, :], in0=pm[:, 0:Tn], scalar=pscale[:, j * 16 + d:j * 16 + d + 1],
                        in1=szs[dd][:, 0:Tn], op0=ALU.mult, op1=ALU.mult), r=[pmk, 'sz%d' % dd, 'pscale'], w=['y'])

            def back(t):
                for b in range(t.nb):
                    out_proj_and_store(L, t.y3, 'y', 128 * b, t.xt[:, b, :], t.xk(b), t.row0 + 128 * b, w_out3, b,
                                       h, 'h', stat)
                if t.last:
                    pgp, pgk_ = bgroup(4)
                    for c in range(16):
                        T.op('pe', lambda e, c=c: e.transpose(pgp[0:15, c * 128:(c + 1) * 128], carry[:, c, :],
                                                              ident_f[:]), r=['carry', 'ident_f'], w=pgk_)
                    ystage = ybf.bitcast(F32)
                    T.op('act', lambda e: e.activation(out=ystage[0:15, :], in_=pgp[0:15, :], func=AF.Copy),
                         r=pgk_, w=['y'])
                    T.dma('sp', poolp[j], ystage[0:15, :], r=['y'])
                if t.sample:
                    for s in range(NS):
                        T.dma('sp', pools[j, s, 7:15, :], tokrows[8 * s:8 * s + 8, :], r=['tokrows'])
                    T.dma('sp', pools[j, :, 0:7, :], spool[j, :, 8:15, :])

            tiles = [mk(ti, 256 * ti, 256, False, ti == 0, ti == 7) for ti in range(PLAN.get('ptiles', 8))]
            if PLAN.get('psample', True):
                tiles.append(mk(len(tiles), SEQ, 128, True, False, False))
            if tiles:
                front(tiles[0])
                U(tiles[0], 0)
            for i, t in enumerate(tiles):
                for g in range(4):
                    if g + 1 < 4:
                        U(t, g + 1)
                    Zp(t, g)
                    Pst(t, g)
                    M(t, g)
                    if g == 1 and i + 1 < len(tiles):
                        front(tiles[i + 1])
                if i + 1 < len(tiles):
                    U(tiles[i + 1], 0)
                back(t)

        def ssd_layer(j):
            L = 2 * j + 1
            T.barrier()
            A.reset()
            w_in3 = WA[:].rearrange("p (k n) -> p k n", k=8)
            w_out3 = WB[:].rearrange("p (k n) -> p k n", k=16)
            load_weights_bf16(w_in3, ssd_in_w[j].rearrange("(k p) n -> p k n", p=128), 8, 'WA')
            load_weights_bf16(w_out3, ssd_out_w[j].rearrange("(k p) n -> p k n", p=128), 16, 'WB')
            cw = convw[:, j * 96:(j + 1) * 96].rearrange("p (c k) -> p c k", k=4)
            cb = convb[:, j * 24:(j + 1) * 24]
            dtb_j = dtb[:, j * 32:(j + 1) * 32]
            negA_j = negA[:, j * 32:(j + 1) * 32]
            dsk_j = dsk[:, j * 32:(j + 1) * 32]
            snw_j = snw[:, j * 16:(j + 1) * 16]

            xts = [A.f32(1024), A.f32(1024)]
            h = A.bf16(1024)
            hTs = [A.bf16(8 * 128).rearrange("p (k t) -> p k t", k=8) for _ in range(2)]
            xe4s = [A.f32(704), A.f32(704)]
            acc4s = [A.f32(512), A.f32(512)]
            BC = A.bf16(8 * 128).rearrange("p (c t) -> p c t", c=8)
            h0a_off = A.off
            tmpA = [A.f32(512), A.f32(512)]
            lq_off = A.off
            Lq = [A.f32(512), A.f32(512)]
            stage1k = arena_t[:, lq_off:lq_off + 1024]
            rawc4 = Lq[0]
            scr4 = Lq[1][0:48, :]
            xs_tok = A.f32(2048)
            bfA = A.bf16(2048)
            bfB = A.bf16(2048)
            Btok = A.bf16(4 * 128).rearrange("p (g n) -> p g n", g=4)
            CBTm = A.f32(512).rearrange("p (g i) -> p g i", g=4)
            WTq = [A.bf16(512), A.bf16(512)]
            ybuf = A.f32(2048)
            ST_all = A.f32(3072)
            ST = ST_all[:, 0:2048]
            STb = ST_all[:, 2048:3072].bitcast(BF16)
            rawT = ST_all
            smalls = [A.f32(256), A.f32(256)]
            ccarry = A.f32(72).rearrange("p (c r) -> p c r", c=24)
            CTm = WTq
            Bm = [A.bf16(512), A.bf16(512)]
            dcol = A.f32(256).rearrange("p (a s) -> p a s", a=16)
            stat = A.f32(64)
            if PLAN.get('verbose'):
                print('ssd arena words used', A.off, 'of', A.words)
            h0s = [arena_t[:, h0a_off:h0a_off + 2048], xs_tok]
            h0keys = ['h0a', 'xs_tok']

            T.op('dve', lambda e: e.memset(ccarry, 0.0), w=['ccarry'])

            def bc64(v32):
                return v32.unsqueeze(2).broadcast_to([128, 32, 64])

            def v3(ap2048):
                return ap2048.rearrange("p (h d) -> p h d", h=32)

            def chunk(ci, row0, sample):
                first = (ci == 0)
                par = ci % 2
                K = lambda n: n + str(par)
                xt = xts[par]
                hT3 = hTs[par]
                small = smalls[par]
                dtv = small[:, 0:32]
                av = small[:, 32:64]
                ex = small[:, 64:160]
                dte = small[:, 160:192]
                tdt = small[:, 192:224]
                ecum = ex[:, 0:32]
                eaft = ex[:, 32:64]
                dec_bc = ex[:, 64:96]
                src, rk = src_rows(L, row0, 128)
                T.dma('sp', xt, src, r=rk, w=[K('xt')])
                norm_and_transpose(xt, K('xt'), L, h, hT3, 'h', K('hT'), stat, 16 * par, 0)
                pb, pk = bank()
                for k in range(8):
                    T.op('pe', lambda e, k=k: e.matmul(pb[:, 0:32], lhsT=hT3[:, k, :], rhs=w_in3[:, k, 5120:5152],
                                                       start=(k == 0), stop=(k == 7)), r=[K('hT')] + WAK, w=[pk])
                T.op('dve', lambda e: e.tensor_tensor(out=tdt, in0=pb[:, 0:32], in1=dtb_j, op=ALU.add),
                     r=[pk, 'dtb'], w=[K('tdt')])
                T.op('act', lambda e: e.activation(out=tdt, in_=tdt, func=AF.Exp), r=[K('tdt')], w=[K('tdt')])
                T.op('act', lambda e: e.activation(out=dtv, in_=tdt, func=AF.Ln, bias=1.0), r=[K('tdt')], w=[K('dt')])
                T.op('dve', lambda e: e.tensor_tensor(out=av, in0=dtv, in1=negA_j, op=ALU.mult),
                     r=[K('dt'), 'negA'], w=[K('a')])
                pb, pk = bank()
                T.op('pe', lambda e: e.matmul(pb[:, 0:32], lhsT=(Us if sample else Umat)[:], rhs=av, start=True, stop=True),
                     r=[K('a'), 'U', 'Us'], w=[pk])
                T.op('pe', lambda e: e.matmul(pb[:, 32:64], lhsT=(SLs if sample else SLmat)[:], rhs=av, start=True,
                                              stop=True), r=[K('a'), 'SL', 'SLs'], w=[pk])
                T.op('pe', lambda e: e.matmul(pb[:, 64:96], lhsT=ones_f[:], rhs=av, start=True, stop=True),
                     r=[K('a'), 'ones_f'], w=[pk])
                T.op('act', lambda e: e.activation(out=ex, in_=pb[:, 0:96], func=AF.Exp), r=[pk], w=[K('ex')])
                T.op('dve', lambda e: e.tensor_tensor(out=dte, in0=eaft, in1=dtv, op=ALU.mult), r=[K('ex'), K('dt')], w=[K('dte')])
                yield 'front'
                def xbc_front(cg):
                    xe4 = xe4s[cg % 2]
                    xek = 'xe%d' % (cg % 2)
                    pb, pk = bank()
                    for cc in range(4):
                        c = 4 * cg + cc
                        for k in range(8):
                            T.op('pe', lambda e, k=k, c=c, cc=cc: e.matmul(
                                pb[:, cc * 128:(cc + 1) * 128], lhsT=w_in3[:, k, DI + c * 128:DI + (c + 1) * 128],
                                rhs=hT3[:, k, :], start=(k == 0), stop=(k == 7)), r=WAK + [K('hT')], w=[pk])
                    if not sample:
                        xe3 = xe4[:, 0:4 * 131].rearrange("p (c t) -> p c t", c=4)
                        T.op('act', lambda e: e.activation(out=xe3[:, :, 0:3], in_=ccarry[:, 4 * cg:4 * cg + 4, :],
                                                           func=AF.Copy), r=['ccarry'], w=[xek])
                        T.op('act', lambda e: e.activation(out=xe3[:, :, 3:131],
                                                           in_=pb.rearrange("p (c t) -> p c t", c=4), func=AF.Copy),
                             r=[pk], w=[xek])
                        T.op('act', lambda e: e.activation(out=ccarry[:, 4 * cg:4 * cg + 4, :], in_=xe3[:, :, 128:131],
                                                           func=AF.Copy), r=[xek], w=['ccarry'])
                    else:
                        xe4v = xe4.rearrange("p (c s t) -> p c s t", c=4, s=16)
                        T.dma('sp', scr4, sconv[j, :, :, cg * 512:(cg + 1) * 512].rearrange("s r n -> (s r) n"),
                              w=['Lq1'])
                        pb2, pk2 = bank()
                        for cc in range(4):
                            T.op('pe', lambda e, cc=cc: e.transpose(pb2[:, cc * 48:(cc + 1) * 48],
                                                                    scr4[:, cc * 128:(cc + 1) * 128], ident_f[0:48, 0:48]),
                                 r=['Lq1', 'ident_f'], w=[pk2])
                        T.op('act', lambda e: e.activation(
                            out=xe4v[:, :, :, 0:3], in_=pb2[:, 0:192].rearrange("p (c s r) -> p c s r", c=4, s=16),
                            func=AF.Copy), r=[pk2], w=[xek])
                        T.op('act', lambda e: e.activation(out=rawc4, in_=pb, func=AF.Copy), r=[pk], w=['Lq0'])
                        T.op('pool', lambda e: e.tensor_copy(
                            out=xe4v[:, :, :, 3:11], in_=rawc4.rearrange("p (c s t) -> p c s t", c=4, s=16)),
                            r=['Lq0'], w=[xek])
                        pb3, pk3 = bank()
                        for cc in range(4):
                            T.op('pe', lambda e, cc=cc: e.transpose(pb3[:, cc * 128:(cc + 1) * 128],
                                                                    rawc4[:, cc * 128:(cc + 1) * 128], ident_f[:]),
                                 r=['Lq0', 'ident_f'], w=[pk3])
                        T.op('act', lambda e: e.activation(out=rawT[:, cg * 512:(cg + 1) * 512], in_=pb3, func=AF.Copy),
                             r=[pk3], w=['rawT'])

                def xbc_back(cg):
                    xe4 = xe4s[cg % 2]
                    xek = 'xe%d' % (cg % 2)
                    acc4 = acc4s[cg % 2]
                    acck = 'acc%d' % (cg % 2)
                    if not sample:
                        xe3 = xe4[:, 0:4 * 131].rearrange("p (c t) -> p c t", c=4)
                        tap = lambda cc, kk: xe3[:, cc, kk:kk + 128]
                        accv = lambda cc: acc4[:, cc * 128:(cc + 1) * 128]
                    else:
                        xe4v = xe4.rearrange("p (c s t) -> p c s t", c=4, s=16)
                        tap = lambda cc, kk: xe4v[:, cc, :, kk:kk + 8]
                        accv = lambda cc: acc4[:, cc * 128:(cc + 1) * 128].rearrange("p (s t) -> p s t", t=8)
                    for kk in range(4):
                        for cc in range(4):
                            c = 4 * cg + cc
                            if kk == 0:
                                T.op('dve', lambda e, cc=cc, c=c: e.tensor_scalar(
                                    out=accv(cc), in0=tap(cc, 0), scalar1=cw[:, c, 0:1], scalar2=None, op0=ALU.mult),
                                    r=[xek, 'convw'], w=[(acck, cc)])
                            else:
                                T.op('dve', lambda e, cc=cc, c=c, kk=kk: e.scalar_tensor_tensor(
                                    out=accv(cc), in0=tap(cc, kk), scalar=cw[:, c, kk:kk + 1], in1=accv(cc),
                                    op0=ALU.mult, op1=ALU.add), r=[xek, 'convw', (acck, cc)], w=[(acck, cc)])
                    for cc in range(4):
                        c = 4 * cg + cc
                        a1 = acc4[:, cc * 128:(cc + 1) * 128]
                        if c < 16:
                            T.op('act', lambda e, c=c, a1=a1: e.activation(out=a1, in_=a1, func=AF.Silu, bias=cb[:, c:c + 1]),
                                 r=[(acck, cc), 'convb'], w=[(acck, cc)])
                        else:
                            T.op('act', lambda e, c=c, a1=a1: e.activation(out=BC[:, c - 16, :], in_=a1, func=AF.Silu,
                                                                           bias=cb[:, c:c + 1]),
                                 r=[(acck, cc), 'convb'], w=['BC'])
                    if cg < 4:
                        xg, xgk = hbank(cg)
                        for cc in range(4):
                            T.op('pe', lambda e, cc=cc: e.transpose(xg[:, cc * 128:(cc + 1) * 128],
                                                                    acc4[:, cc * 128:(cc + 1) * 128], ident_f[:]),
                                 r=[(acck, cc), 'ident_f'], w=[xgk])
                        T.op('act', lambda e: e.activation(out=xs_tok[:, cg * 512:(cg + 1) * 512], in_=xg, func=AF.Copy),
                             r=[xgk], w=['xs_tok'])

                xbc_front(0)
                for cg in range(6):
                    if cg + 1 < 6:
                        xbc_front(cg + 1)
                    xbc_back(cg)
                    yield 'xbc'
                pb, pk = bank()
                pbb = pb.bitcast(BF16)
                for g in range(4):
                    T.op('pe', lambda e, g=g: e.transpose(pbb[:, g * 128:(g + 1) * 128], BC[:, g, :], ident_b[:]),
                         r=['BC', 'ident_b'], w=[pk])
                T.op('act', lambda e: e.activation(out=Btok, in_=pbb[:, 0:512].rearrange("p (g n) -> p g n", g=4),
                                                   func=AF.Copy), r=[pk], w=['Btok'])
                pb, pk = bank()
                for g in range(4):
                    T.op('pe', lambda e, g=g: e.matmul(pb[:, g * 128:(g + 1) * 128], lhsT=BC[:, g, :], rhs=BC[:, 4 + g, :],
                                                       start=True, stop=True), r=['BC'], w=[pk])
                msk = Us if sample else Umat
                T.op('dve', lambda e: e.tensor_tensor(out=CBTm, in0=pb.rearrange("p (g i) -> p g i", g=4),
                                                      in1=msk[:].unsqueeze(1).broadcast_to([128, 4, 128]), op=ALU.mult),
                     r=[pk, 'U', 'Us'], w=['CBTm'])
                if sample:
                    arep = ybuf
                    T.op('dve', lambda e: e.tensor_copy(out=v3(arep), in_=bc64(av)), r=[K('a')], w=['y'])
                    pb, pk = bank()
                    for hp in range(16):
                        T.op('pe', lambda e, hp=hp: e.matmul(pb[:, hp * 16:(hp + 1) * 16],
                                                             lhsT=arep[:, hp * 128:(hp + 1) * 128], rhs=seqind[:],
                                                             start=True, stop=True), r=['y', 'seqind'], w=[pk])
                    T.op('act', lambda e: e.activation(out=dcol, in_=pb[:, 0:256].rearrange("p (a s) -> p a s", a=16),
                                                       func=AF.Exp), r=[pk], w=['dcol'])
                T.op('dve', lambda e: e.tensor_tensor(out=v3(bfA), in0=v3(xs_tok), in1=bc64(dtv), op=ALU.mult),
                     r=['xs_tok', K('dt')], w=['bfA'])
                T.op('dve', lambda e: e.tensor_tensor(out=v3(ybuf), in0=v3(xs_tok), in1=bc64(dsk_j), op=ALU.mult),
                     r=['xs_tok', 'dsk'], w=['y'])
                T.op('dve', lambda e: e.tensor_tensor(out=v3(bfB), in0=v3(xs_tok), in1=bc64(dte), op=ALU.mult),
                     r=['xs_tok', K('dte')], w=['bfB'])
                deferred = []
                if not sample:
                    if not first:
                        for g in range(4):
                            def yoff_unit(g=g):
                                pb, pk = bank()
                                T.op('pe', lambda e: e.matmul(pb, lhsT=BC[:, 4 + g, :], rhs=STb[:, g * 512:(g + 1) * 512],
                                                              start=True, stop=True), r=['BC', 'STb'], w=[pk])
                                tq = xs_tok[:, g * 512:(g + 1) * 512]
                                T.op('dve', lambda e: e.tensor_tensor(
                                    out=tq.rearrange("p (h d) -> p h d", h=8), in0=pb.rearrange("p (h d) -> p h d", h=8),
                                    in1=ecum[:, 8 * g:8 * g + 8].unsqueeze(2).broadcast_to([128, 8, 64]), op=ALU.mult),
                                    r=[pk, K('ex')], w=['xs_tok'])
                                T.op('dve', lambda e: e.tensor_tensor(
                                    out=ybuf[:, g * 512:(g + 1) * 512], in0=ybuf[:, g * 512:(g + 1) * 512], in1=tq,
                                    op=ALU.add), r=['xs_tok', 'y'], w=['y'])
                            deferred.append(yoff_unit)
                    for g in range(4):
                        def cs_unit(g=g):
                            pb, pk = bank()
                            T.op('pe', lambda e: e.matmul(pb, lhsT=Btok[:, g, :], rhs=bfB[:, g * 512:(g + 1) * 512],
                                                          start=True, stop=True), r=['Btok', 'bfB'] + (['STb'] if not first else []),
                                 w=[pk])
                            sg = ST[:, g * 512:(g + 1) * 512]
                            if first:
                                T.op('act', lambda e: e.activation(out=sg, in_=pb, func=AF.Copy), r=[pk], w=[('ST', g)])
                            else:
                                T.op('dve', lambda e: e.tensor_tensor(
                                    out=sg.rearrange("p (h d) -> p h d", h=8), in0=sg.rearrange("p (h d) -> p h d", h=8),
                                    in1=dec_bc[:, 8 * g:8 * g + 8].unsqueeze(2).broadcast_to([128, 8, 64]), op=ALU.mult),
                                    r=[('ST', g), K('ex')], w=[('ST', g)])
                                T.op('dve', lambda e: e.tensor_tensor(out=sg, in0=sg, in1=pb, op=ALU.add),
                                     r=[pk, ('ST', g)], w=[('ST', g)])
                        deferred.append(cs_unit)
                pgd, pgdk = bgroup(4)
                segp = {}

                def seg_front(q):
                    ax = tmpA[q % 2]
                    axk = 'tmpA%d' % (q % 2)
                    ax3 = ax.rearrange("p (h i) -> p h i", h=4)
                    T.op('pool', lambda e: e.tensor_tensor(
                        out=ax3, in0=av[:, 4 * q:4 * q + 4].unsqueeze(2).broadcast_to([128, 4, 128]),
                        in1=Umat[:].unsqueeze(1).broadcast_to([128, 4, 128]), op=ALU.mult), r=[K('a'), 'U'], w=[axk])
                    pb, pk = bank()
                    T.op('pe', lambda e: e.matmul(pb, lhsT=SLmat[:], rhs=ax, start=True, stop=True),
                         r=[axk, 'SL'], w=[pk])
                    segp[q] = (pb, pk)

                def seg_back(q):
                    pb, pk = segp[q]
                    lq = Lq[q % 2]
                    lqk = 'Lq%d' % (q % 2)
                    wt = WTq[q % 2]
                    wtk = 'WT%d' % (q % 2)
                    T.op('act', lambda e: e.activation(out=lq, in_=pb, func=AF.Exp), r=[pk], w=[lqk])
                    T.op('dve', lambda e: e.tensor_tensor(
                        out=wt.rearrange("p (h i) -> p h i", h=4), in0=lq.rearrange("p (h i) -> p h i", h=4),
                        in1=CBTm[:, q // 2, :].unsqueeze(1).broadcast_to([128, 4, 128]), op=ALU.mult),
                        r=[lqk, 'CBTm'], w=[wtk])
                    for hh in range(4):
                        hd = 4 * q + hh
                        T.op('pe', lambda e, hh=hh, hd=hd: e.matmul(pgd[:, hd * 64:(hd + 1) * 64],
                                                                    lhsT=wt[:, hh * 128:(hh + 1) * 128],
                                                                    rhs=bfA[:, hd * 64:(hd + 1) * 64], start=True, stop=True),
                             r=[wtk, 'bfA'], w=[pgdk[hd // 8]])

                seg_front(0)
                for q in range(8):
                    if q + 1 < 8 and q != 3:
                        seg_front(q + 1)
                    seg_back(q)
                    if q < len(deferred):
                        deferred[q]()
                    if q == 3:
                        yield 'midA'
                        seg_front(4)
                for fn in deferred[8:]:
                    fn()
                T.op('dve', lambda e: e.tensor_tensor(out=ybuf, in0=ybuf, in1=pgd, op=ALU.add),
                     r=pgdk + ['y'], w=['y'])
                STK = [('ST', g) for g in range(4)]
                if not sample:
                    if ci < 15:
                        T.op('act', lambda e: e.activation(out=STb, in_=ST, func=AF.Copy), r=STK, w=['STb'])
                    else:
                        pgt, pgtk = bgroup(4)
                        for hp in range(16):
                            T.op('pe', lambda e, hp=hp: e.transpose(pgt[:, hp * 128:(hp + 1) * 128],
                                                                    ST[:, hp * 128:(hp + 1) * 128], ident_f[:]),
                                 r=STK + ['ident_f'], w=[pgtk[hp // 4]])
                        T.op('act', lambda e: e.activation(out=xs_tok, in_=pgt, func=AF.Copy), r=pgtk,
                             w=['xs_tok'])
                        T.dma('sp', ssmp[j].rearrange("(hp two) p n -> (two p) hp n", two=2),
                              xs_tok.rearrange("p (a n) -> p a n", a=16), r=['xs_tok'])
                        for third in range(3):
                            pgv, pgvk = bgroup(2)
                            for cc in range(8):
                                c = third * 8 + cc
                                T.op('pe', lambda e, c=c, cc=cc: e.transpose(pgv[0:3, cc * 128:(cc + 1) * 128],
                                                                             ccarry[:, c, :], ident_f[:]),
                                     r=['ccarry', 'ident_f'], w=[pgvk[cc // 4]])
                            T.op('act', lambda e: e.activation(out=stage1k[0:3, :], in_=pgv[0:3, :],
                                                               func=AF.Copy), r=pgvk, w=['Lq0', 'Lq1'])
                            T.dma('sp', convp[j, :, third * 1024:(third + 1) * 1024], stage1k[0:3, :], r=['Lq0', 'Lq1'])
                else:
                    T.barrier()
                    if PLAN.get('dump'):
                        T.dma('pool', yp[0:128, :], bfB[:, 0:1024])
                        T.dma('pool', yp[128:256, :], bfB[:, 1024:2048])
                        T.dma('sp', yp[256:384, 0:224], small[:, 0:224])
                        T.dma('sp', yp[384:512, :], xs_tok[:, 0:1024])
                        T.dma('sp', yp[512:640, :], xs_tok[:, 1024:2048])
                        T.barrier()
                    pgo, pgok = PS[:, 0:2048], [('ps', i) for i in range(4)]
                    for s in range(NS):
                        h0 = h0s[s % 2]
                        h0k = 'h0_%d' % (s % 2)
                        h03 = h0.rearrange("p (a n) -> p a n", a=16)
                        if s == 0:
                            T.dma('sp', h03, sssm[j, 0].rearrange("(hp two) p n -> (two p) hp n", two=2), w=[h0k])
                        if s + 1 < NS:
                            T.dma('sp', h0s[(s + 1) % 2].rearrange("p (a n) -> p a n", a=16),
                                  sssm[j, s + 1].rearrange("(hp two) p n -> (two p) hp n", two=2),
                                  w=['h0_%d' % ((s + 1) % 2)])
                        ctm = CTm[s % 2]
                        ctmk = 'WT%d' % (s % 2)
                        bm = Bm[s % 2]
                        bmk = 'Bm%d' % (s % 2)
                        ctm3 = ctm.rearrange("p (g t) -> p g t", g=4)
                        T.op('pool', lambda e, s=s: e.affine_select(
                            out=ctm3, in_=BC[:, 4:8, :], pattern=[[0, 4], [1, 128]], compare_op=ALU.is_ge, fill=0.0,
                            base=-8 * s, channel_multiplier=0), r=['BC'], w=[ctmk])
                        T.op('pool', lambda e, s=s: e.affine_select(
                            out=ctm3, in_=ctm3, pattern=[[0, 4], [-1, 128]], compare_op=ALU.is_ge, fill=0.0,
                            base=8 * s + 7, channel_multiplier=0), r=[ctmk], w=[ctmk])
                        T.op('dve', lambda e, s=s: e.tensor_scalar(
                            out=bm, in0=Btok.rearrange("p g n -> p (g n)"), scalar1=seqind[:, s:s + 1], scalar2=None,
                            op0=ALU.mult), r=['Btok', 'seqind'], w=[bmk])
                        for half in range(2):
                            pt = PS[:, 2048 + 0:2048 + 1024]
                            ptk = [('ps', 4), ('ps', 5)]
                            for a in range(8):
                                hp = half * 8 + a
                                T.op('pe', lambda e, a=a, hp=hp: e.transpose(pt[:, a * 128:(a + 1) * 128], h03[:, hp, :],
                                                                             ident_f[:]),
                                     r=[h0k, 'ident_f'], w=[ptk[a // 4]])
                            T.op('act', lambda e, half=half: e.activation(out=bfA[:, half * 1024:(half + 1) * 1024], in_=pt,
                                                                          func=AF.Copy), r=ptk, w=['bfA'])
                        for g in range(4):
                            T.op('pe', lambda e, g=g, s=s: e.matmul(pgo[:, g * 512:(g + 1) * 512],
                                                                    lhsT=ctm[:, g * 128:(g + 1) * 128],
                                                                    rhs=bfA[:, g * 512:(g + 1) * 512],
                                                                    start=(s == 0), stop=(s == NS - 1)),
                                 r=[ctmk, 'bfA'], w=[pgok[g]])
                        for half in range(2):
                            pc = PS[:, 3072:4096]
                            pck = [('ps', 6), ('ps', 7)]
                            for a in range(8):
                                hp = half * 8 + a
                                T.op('pe', lambda e, a=a, hp=hp: e.matmul(pc[:, a * 128:(a + 1) * 128],
                                                                          lhsT=bfB[:, hp * 128:(hp + 1) * 128],
                                                                          rhs=bm[:, (hp // 4) * 128:(hp // 4 + 1) * 128],
                                                                          start=True, stop=True),
                                     r=['bfB', bmk], w=[pck[a // 4]])
                            hv = h03[:, half * 8:(half + 1) * 8, :]
                            T.op('dve', lambda e, hv=hv, half=half, s=s: e.tensor_tensor(
                                out=hv, in0=hv, in1=dcol[:, half * 8:(half + 1) * 8, s:s + 1].broadcast_to([128, 8, 128]),
                                op=ALU.mult), r=[h0k, 'dcol'], w=[h0k])
                            T.op('dve', lambda e, hv=hv: e.tensor_tensor(
                                out=hv, in0=hv, in1=pc.rearrange("p (a n) -> p a n", a=8), op=ALU.add),
                                r=pck + [h0k], w=[h0k])
                        T.dma('sp', ssms[j, s].rearrange("(hp two) p n -> (two p) hp n", two=2), h03, r=[h0k])
                    T.barrier()
                    T.op('dve', lambda e: e.tensor_tensor(out=v3(xs_tok), in0=v3(pgo), in1=bc64(ecum), op=ALU.mult),
                         r=pgok + [K('ex')], w=['xs_tok'])
                    T.op('dve', lambda e: e.tensor_tensor(out=ybuf, in0=ybuf, in1=xs_tok, op=ALU.add),
                         r=['xs_tok', 'y'], w=['y'])
                    for s in range(NS):
                        T.dma('sp', convs[j, s], rawT[8 * s + 5:8 * s + 8, :], r=['rawT'])
                yield 'mid'
                for zc in range(4):
                    pb, pk = bank()
                    for k in range(8):
                        T.op('pe', lambda e, k=k: e.matmul(pb, lhsT=hT3[:, k, :], rhs=w_in3[:, k, zc * 512:(zc + 1) * 512],
                                                           start=(k == 0), stop=(k == 7)), r=[K('hT')] + WAK, w=[pk])
                    zs = tmpA[zc % 2]
                    zsk = 'tmpA%d' % (zc % 2)
                    T.op('act', lambda e: e.activation(out=zs, in_=pb, func=AF.Silu), r=[pk], w=[zsk])
                    yield 'backP'
                    T.op('dve', lambda e: e.tensor_tensor(out=ybuf[:, zc * 512:(zc + 1) * 512],
                                                          in0=ybuf[:, zc * 512:(zc + 1) * 512], in1=zs, op=ALU.mult),
                         r=[zsk, 'y'], w=['y'])
                    yield 'back'
                c4 = 16 * par + 4
                T.op('act', lambda e: e.activation(out=bfA, in_=ybuf, func=AF.Square, accum_out=stat[:, c4:c4 + 1]),
                     r=['y'], w=['bfA', 'stat'])
                T.op('act', lambda e: e.activation(out=stat[:, c4 + 1:c4 + 2], in_=stat[:, c4:c4 + 1], func=AF.Sqrt,
                                                   scale=1.0 / DI, bias=epsb[:, 0:1]), r=['stat', 'epsb'], w=['stat'])
                yield 'backP'
                T.op('dve', lambda e: e.reciprocal(out=stat[:, c4:c4 + 1], in_=stat[:, c4 + 1:c4 + 2]),
                     r=['stat'], w=['stat'])
                T.op('dve', lambda e: e.tensor_scalar(out=bfA, in0=ybuf, scalar1=stat[:, c4:c4 + 1], scalar2=None, op0=ALU.mult),
                     r=['y', 'stat'], w=['bfA'])
                yield 'back'
                pg2, pg2k = PS[:, 3072:4096], [('ps', 6), ('ps', 7)]
                pg2b = pg2.bitcast(BF16)
                for f in range(16):
                    T.op('pe', lambda e, f=f: e.transpose(pg2b[:, f * 128:(f + 1) * 128], bfA[:, f * 128:(f + 1) * 128],
                                                          ident_b[:]), r=['bfA', 'ident_b'], w=[pg2k[f // 8]])
                yield 'backP'
                yT3 = bfB.rearrange("p (f t) -> p f t", f=16)
                T.op('dve', lambda e: e.tensor_tensor(out=yT3, in0=pg2b.rearrange("p (f t) -> p f t", f=16),
                                                      in1=snw_j.unsqueeze(2).broadcast_to([128, 16, 128]), op=ALU.mult),
                     r=pg2k + ['snw'], w=['bfB'])
                yield 'back'
                xkey = K('xt')
                for hh in range(2):
                    pb, pk = bank()
                    for f in range(16):
                        T.op('pe', lambda e, f=f: e.matmul(pb, lhsT=yT3[:, f, :], rhs=w_out3[:, f, hh * 512:(hh + 1) * 512],
                                                           start=(f == 0), stop=(f == 15)), r=['bfB'] + WBK, w=[pk])
                    yield 'backP'
                    T.op('dve', lambda e: e.tensor_tensor(out=xt[:, hh * 512:(hh + 1) * 512],
                                                          in0=xt[:, hh * 512:(hh + 1) * 512], in1=pb, op=ALU.add),
                         r=[pk, xkey], w=[xkey])
                    if hh == 0:
                        yield 'back'
                if L < 3:
                    T.dma('sp', xscr[row0:row0 + 128, :], xt, r=[xkey], w=[('xscr', row0 // 128)])
                else:
                    rms_stat(xt, xkey, stat, 8, DM, h, 'h')
                    T.op('dve', lambda e: e.scalar_tensor_tensor(out=xt, in0=xt, scalar=stat[:, 8:9], in1=fnw[:],
                                                                 op0=ALU.mult, op1=ALU.mult),
                         r=[xkey, 'stat', 'fnw'], w=[xkey])
                    dst = yp[row0:row0 + 128, :] if row0 < SEQ else ysm[row0 - SEQ:row0 - SEQ + 128, :]
                    T.dma('sp', dst, xt, r=[xkey])

            return chunk, (ST, STb)

        def run_ssd(j):
            chunk, (ST, STb) = ssd_layer(j)
            n = PLAN.get('schunks', 16)
            gens = [chunk(ci, 128 * ci, False) for ci in range(n)]
            adv = lambda g: next(g, None)
            if n:
                assert adv(gens[0]) == 'front'
                for _ in range(6):
                    assert adv(gens[0]) == 'xbc'
            for ci in range(n):
                nxt = gens[ci + 1] if ci + 1 < n else None
                assert adv(gens[ci]) == 'midA'
                xleft = 0
                if nxt is not None:
                    assert adv(nxt) == 'front'
                    xleft = 6
                assert adv(gens[ci]) == 'mid'
                done_back = False
                while (not done_back) or xleft:
                    r = None
                    if not done_back:
                        r = adv(gens[ci])
                        if r is None:
                            done_back = True
                    if xleft:
                        assert adv(nxt) == 'xbc'
                        xleft -= 1
                    if r == 'backP':
                        r = adv(gens[ci])
                        if r is None:
                            done_back = True
            T.barrier()
            if PLAN.get('ssample', True):
                for _ in chunk(0, SEQ, True):
                    pass

        if PLAN.get('pool0', True):
            pool_layer(0)
        if PLAN.get('ssd0', True):
            run_ssd(0)
        if PLAN.get('pool1', True):
            pool_layer(1)
        if PLAN.get('ssd1', True):
            run_ssd(1)
        T.finish()
    return nc


_NC_CACHE = {}


def _col(v, nchunk):
    return np.ascontiguousarray(v.reshape(nchunk, 128).T)


def kernel(x_prompt, x_sample, state_pool, state_conv, state_ssm, norm_w, pool_in_w, pool_mix_w, pool_scale,
           pool_out_w, ssd_in_w, ssd_conv_w, ssd_conv_b, ssd_dt_bias, ssd_A_log, ssd_D, ssd_norm_w, ssd_out_w,
           final_norm_w):
    f = lambda a: np.ascontiguousarray(np.asarray(a, dtype=np.float32))
    x_prompt, x_sample, state_pool, state_conv, state_ssm = map(f, (x_prompt, x_sample, state_pool, state_conv, state_ssm))
    norm_w, pool_scale, ssd_conv_w, ssd_conv_b = map(f, (norm_w, pool_scale, ssd_conv_w, ssd_conv_b))
    ssd_dt_bias, ssd_A_log, ssd_D, ssd_norm_w, final_norm_w = map(f, (ssd_dt_bias, ssd_A_log, ssd_D, ssd_norm_w, final_norm_w))
    nwc = np.concatenate([_col(norm_w[l], 8) for l in range(4)], axis=1)
    pscale = np.concatenate([_col(pool_scale[j], 16) for j in range(2)], axis=1)
    convw = np.concatenate([np.ascontiguousarray(ssd_conv_w[j].reshape(4, 24, 128).transpose(2, 1, 0)).reshape(128, 96)
                            for j in range(2)], axis=1)
    convb = np.concatenate([_col(ssd_conv_b[j], 24) for j in range(2)], axis=1)
    snw = np.concatenate([_col(ssd_norm_w[j], 16) for j in range(2)], axis=1)
    shared = {
        "pool_in_w": f(pool_in_w), "pool_mix_w": f(pool_mix_w), "pool_out_w": f(pool_out_w),
        "ssd_in_w": f(ssd_in_w), "ssd_out_w": f(ssd_out_w),
        "nwc": f(nwc), "fnw": f(final_norm_w.reshape(1, DM)), "pscale": f(pscale), "convw": f(convw), "convb": f(convb),
        "dtb": f(ssd_dt_bias.reshape(1, 64)), "alog": f(ssd_A_log.reshape(1, 64)), "dsk": f(ssd_D.reshape(1, 64)),
        "snw": f(snw),
    }
    in_maps = []
    for c in range(NCORES):
        m = dict(shared)
        m["xp"] = x_prompt[c]
        m["xsm"] = f(x_sample[NS * c:NS * (c + 1)].reshape(128, DM))
        m["spool"] = f(state_pool[:, NS * c:NS * (c + 1)])
        m["sconv"] = f(state_conv[:, NS * c:NS * (c + 1)])
        m["sssm"] = f(state_ssm[:, NS * c:NS * (c + 1)])
        in_maps.append(m)
    if "nc" not in _NC_CACHE:
        _NC_CACHE["nc"] = build_program()
    res = run_bass_kernel_spmd(_NC_CACHE["nc"], in_maps, core_ids=list(range(NCORES)))
    R = res.results
    y_prompt = np.stack([R[c]["yp"] for c in range(NCORES)], axis=0)
    y_sample = np.concatenate([R[c]["ysm"].reshape(NS, DS, DM) for c in range(NCORES)], axis=0)
    pool_p = np.stack([R[c]["poolp"] for c in range(NCORES)], axis=1)
    pool_s = np.concatenate([R[c]["pools"] for c in range(NCORES)], axis=1)
    conv_p = np.stack([R[c]["convp"] for c in range(NCORES)], axis=1)
    conv_s = np.concatenate([R[c]["convs"] for c in range(NCORES)], axis=1)
    ssm_p = np.stack([R[c]["ssmp"] for c in range(NCORES)], axis=1)
    ssm_s = np.concatenate([R[c]["ssms"] for c in range(NCORES)], axis=1)
    return tuple(np.ascontiguousarray(a, dtype=np.float32) for a in
                 (y_prompt, y_sample, pool_p, pool_s, conv_p, conv_s, ssm_p, ssm_s))
```
